# Optimizing a Trainium2 kernel written in Bass

```python
import jax, jax.numpy as jnp
from jax import lax
import numpy as np

D_MODEL = 2048
BATCH = 16
SEQ = 256
DEPTH = 4
DEC_BATCH = 4
DEC_SEQ = 1024
PAST_LEN = 256

GRID_W = 64
HD = 128
EPS = 1e-6
NEG = -1e30
ROPE_THETA = 10000.0
CHUNK = 64
QBLK = 128
H_A = 8
DK_A = 128
DV_A = 128
W_A = H_A * DV_A
CONV_K = 5
H_B = 8
HKV_B = 2
W_B = H_B * HD
WIN = 128
H_C = 4
DK_C = 128
DV_C = 256
QK_C = H_C * DK_C
W_C = H_C * DV_C
GLA_RANK = 16
GLA_TAU = 16.0
H_D = 8
W_D = H_D * HD
NB_H = 8
NB_W = 16
NB_CBLK = 16
NB_CSPAN = 32

N_EVEN = (DEPTH + 1) // 2
N_ODD = DEPTH // 2
PA_EVEN = 4 * W_A + 4 * H_A
P_EVEN = PA_EVEN + 2 * W_B + 2 * HKV_B * HD
PC_ODD = 2 * QK_C + 2 * W_C + 2 * GLA_RANK
P_ODD = PC_ODD + 4 * W_D
MIX_EVEN = W_A + W_B
MIX_ODD = W_C + W_D
F32 = jnp.float32

kernel_name = 'hybrid_flow_backbone_step'


def rmsnorm(x, w):
    xf = x.astype(F32)
    y = xf * lax.rsqrt(jnp.mean(xf * xf, -1, keepdims=True) + EPS)
    return (y * w.astype(F32)).astype(x.dtype)


def l2norm(x):
    xf = x.astype(F32)
    return xf * lax.rsqrt(jnp.sum(xf * xf, -1, keepdims=True) + EPS)


def adaln(x, cond, norm_w, w_ada, b_ada):
    mod = jax.nn.silu(cond) @ w_ada + b_ada
    shift, scale, gate = jnp.split(mod, 3, axis=-1)
    return rmsnorm(x, norm_w) * (1 + scale) + shift, gate


def rope_axis(x, pos):
    half = x.shape[-1] // 2
    freq = ROPE_THETA ** (-jnp.arange(half, dtype=F32) / half)
    ang = pos.astype(F32)[:, None] * freq[None, :]
    cos, sin = jnp.cos(ang)[:, None, :], jnp.sin(ang)[:, None, :]
    x1, x2 = x[..., :half].astype(F32), x[..., half:].astype(F32)
    return jnp.concatenate([x1 * cos - x2 * sin, x2 * cos + x1 * sin], -1).astype(x.dtype)


def rope2d(x):
    t = jnp.arange(x.shape[1])
    h = x.shape[-1] // 2
    return jnp.concatenate([rope_axis(x[..., :h], t // GRID_W), rope_axis(x[..., h:], t % GRID_W)], -1)


def short_conv(x, w):
    pad = CONV_K // 2
    y = lax.conv_general_dilated(x, w[:, None, :].astype(x.dtype), window_strides=(1,),
                                 padding=[(pad, pad)], dimension_numbers=('NWC', 'WIO', 'NWC'),
                                 feature_group_count=x.shape[-1])
    return jax.nn.silu(y)


def attn_probs(s, sink):
    if sink is None:
        return jax.nn.softmax(s, -1)
    sk = sink.astype(F32).reshape(s.shape[1], s.shape[2])[None, :, :, None, None]
    m = jnp.maximum(jnp.max(s, -1, keepdims=True), sk)
    e = jnp.exp(s - m)
    return e / (jnp.sum(e, -1, keepdims=True) + jnp.exp(sk - m))


def dense_attn(q, k, v, sink):
    B_, L, H, D = q.shape
    Hkv = k.shape[2]
    G = H // Hkv
    qb = q.reshape(B_, L // QBLK, QBLK, Hkv, G, D).transpose(1, 0, 2, 3, 4, 5)
    scale = D ** -0.5

    def blk(qi):
        s = jnp.einsum('bqkgd,bskd->bkgqs', qi, k).astype(F32) * scale
        p = attn_probs(s, sink).astype(v.dtype)
        return jnp.einsum('bkgqs,bskd->bqkgd', p, v)

    o = lax.map(blk, qb)
    return o.transpose(1, 0, 2, 3, 4, 5).reshape(B_, L, H * D)


def windowed_attn(q, k, v, kc, vc, sink):
    B_, L, H, D = q.shape
    Hkv = k.shape[2]
    G = H // Hkv
    nq = L // QBLK
    span = QBLK + 2 * WIN
    pad = ((0, 0), (WIN, WIN), (0, 0), (0, 0))
    kp, vp = jnp.pad(k, pad), jnp.pad(v, pad)
    qb = q.reshape(B_, nq, QBLK, Hkv, G, D).transpose(1, 0, 2, 3, 4, 5)
    scale = D ** -0.5

    def blk(args):
        i, qi = args
        kw = lax.dynamic_slice_in_dim(kp, i * QBLK, span, axis=1)
        vw = lax.dynamic_slice_in_dim(vp, i * QBLK, span, axis=1)
        qpos = i * QBLK + jnp.arange(QBLK)
        kpos = i * QBLK - WIN + jnp.arange(span)
        ok = (jnp.abs(qpos[:, None] - kpos[None, :]) <= WIN) & (kpos >= 0)[None, :] & (kpos < L)[None, :]
        s_w = jnp.where(ok, jnp.einsum('bqkgd,bskd->bkgqs', qi, kw).astype(F32) * scale, NEG)
        s_c = jnp.einsum('bqkgd,bskd->bkgqs', qi, kc).astype(F32) * scale
        p = attn_probs(jnp.concatenate([s_w, s_c], -1), sink).astype(v.dtype)
        return (jnp.einsum('bkgqs,bskd->bqkgd', p[..., :span], vw)
                + jnp.einsum('bkgqs,bskd->bqkgd', p[..., span:], vc))

    o = lax.map(blk, (jnp.arange(nq), qb))
    return o.transpose(1, 0, 2, 3, 4, 5).reshape(B_, L, H * D)


def neighbourhood_attn(q, k, v, kc, vc, rpb):
    B_, L, H, D = q.shape
    rows = L // GRID_W
    kh = min(NB_H, rows)
    ncb = GRID_W // NB_CBLK
    qcol = np.arange(GRID_W).reshape(ncb, NB_CBLK)
    cstart = np.clip(qcol - NB_W // 2, 0, GRID_W - NB_W)
    cs0 = np.clip(np.arange(ncb) * NB_CBLK - NB_W // 2, 0, GRID_W - NB_CSPAN)
    kcol = cs0[:, None] + np.arange(NB_CSPAN)
    col_ok = (kcol[:, None, :] >= cstart[:, :, None]) & (kcol[:, None, :] < cstart[:, :, None] + NB_W)
    dc_idx = np.clip(kcol[:, None, :] - qcol[:, :, None], -(NB_W - 1), NB_W - 1) + NB_W - 1
    scale = D ** -0.5
    kg = k.reshape(B_, rows, GRID_W, H, D)
    vg = v.reshape(B_, rows, GRID_W, H, D)
    qg = q.reshape(B_, rows, ncb, NB_CBLK, H, D).transpose(1, 0, 2, 3, 4, 5)
    nk = kh * NB_CSPAN

    def row_blk(args):
        r, qr = args
        rs = jnp.clip(r - kh // 2, 0, rows - kh)
        kb = lax.dynamic_slice_in_dim(kg, rs, kh, axis=1)[:, :, kcol]
        vb = lax.dynamic_slice_in_dim(vg, rs, kh, axis=1)[:, :, kcol]
        dr_idx = rs + jnp.arange(kh) - r + NB_H - 1
        bias = rpb[:, dr_idx][:, :, dc_idx].transpose(0, 2, 3, 1, 4).astype(F32)
        s_n = jnp.einsum('bjqhd,bkjshd->bhjqks', qr, kb).astype(F32) * scale + bias[None]
        s_n = jnp.where(col_ok[:, :, None, :], s_n, NEG).reshape(B_, H, ncb, NB_CBLK, nk)
        s_c = jnp.einsum('bjqhd,bshd->bhjqs', qr, kc).astype(F32) * scale
        p = jax.nn.softmax(jnp.concatenate([s_n, s_c], -1), -1).astype(v.dtype)
        pn = p[..., :nk].reshape(B_, H, ncb, NB_CBLK, kh, NB_CSPAN)
        return (jnp.einsum('bhjqks,bkjshd->bjqhd', pn, vb)
                + jnp.einsum('bhjqs,bshd->bjqhd', p[..., nk:], vc))

    o = lax.map(row_blk, (jnp.arange(rows), qg))
    return o.transpose(1, 0, 2, 3, 4, 5).reshape(B_, L, H * D)


def delta_chunked(q, k, v, g, beta, s0):
    B_, H, L, _ = q.shape
    n = L // CHUNK
    ck = lambda t: t.reshape(B_, H, n, CHUNK, *t.shape[3:])
    q, k, v, g, beta = ck(q), ck(k), ck(v), ck(g), ck(beta)
    gc = jnp.cumsum(g, -1)
    tri = jnp.tril(jnp.ones((CHUNK, CHUNK), bool))
    strict = jnp.tril(jnp.ones((CHUNK, CHUNK), bool), -1)
    decay = jnp.exp(jnp.where(tri, gc[..., :, None] - gc[..., None, :], NEG))
    kb = k * beta[..., None]
    lmat = jnp.where(strict, jnp.einsum('bhncd,bhnsd->bhncs', kb, k) * decay, 0.0)
    tmat = lmat + jnp.eye(CHUNK, dtype=F32)
    u = lax.linalg.triangular_solve(tmat, v * beta[..., None], left_side=True, lower=True, unit_diagonal=True)
    w = lax.linalg.triangular_solve(tmat, kb * jnp.exp(gc)[..., None], left_side=True, lower=True, unit_diagonal=True)
    attn = jnp.where(tri, jnp.einsum('bhncd,bhnsd->bhncs', q, k) * decay, 0.0)
    mv = lambda t: jnp.moveaxis(t, 2, 0)

    def step(s, xs):
        qi, ki, ui, wi, gi, ai = xs
        vn = ui - jnp.einsum('bhcd,bhde->bhce', wi, s)
        o = jnp.einsum('bhcd,bhde->bhce', qi * jnp.exp(gi)[..., None], s) + jnp.einsum('bhcs,bhse->bhce', ai, vn)
        gl = gi[..., -1]
        s = s * jnp.exp(gl)[..., None, None] + jnp.einsum('bhcd,bhce->bhde', ki * jnp.exp(gl[..., None] - gi)[..., None], vn)
        return s, o

    s, o = lax.scan(step, s0, (mv(q), mv(k), mv(u), mv(w), mv(gc), mv(attn)))
    return jnp.moveaxis(o, 0, 2).reshape(B_, H, L, -1), s


def gla_chunked(q, k, v, gk, s0):
    B_, H, L, _ = q.shape
    n = L // CHUNK
    chunks = lambda t: jnp.moveaxis(t.reshape(B_, H, n, CHUNK, t.shape[-1]), 2, 0)
    tri = jnp.tril(jnp.ones((CHUNK, CHUNK), bool))

    def step(s, xs):
        qi, ki, vi, gi = xs
        b = jnp.cumsum(gi, axis=2)
        inter = jnp.einsum('bhcd,bhde->bhce', qi * jnp.exp(b), s)
        diff = jnp.where(tri[:, :, None], b[:, :, :, None, :] - b[:, :, None, :, :], NEG)
        a = jnp.einsum('bhcd,bhsd,bhcsd->bhcs', qi, ki, jnp.exp(diff))
        intra = jnp.einsum('bhcs,bhse->bhce', a, vi)
        bl = b[:, :, -1]
        s = s * jnp.exp(bl)[..., None] + jnp.einsum('bhcd,bhce->bhde', ki * jnp.exp(bl[:, :, None] - b), vi)
        return s, inter + intra

    s, o = lax.scan(step, s0, (chunks(q), chunks(k), chunks(v), chunks(gk)))
    return jnp.moveaxis(o, 0, 2).reshape(B_, H, L, -1), s


def flip_seq(t):
    return jnp.flip(t, axis=2)


def mixer_delta(pa, s0, conv_w, a_log, dt_bias, onorm):
    B_, L, _ = pa.shape
    qkv = short_conv(pa[..., :3 * W_A], conv_w)
    z = pa[..., 3 * W_A:4 * W_A]
    heads = lambda t, d: t.astype(F32).reshape(B_, L, H_A, d).transpose(0, 2, 1, 3)
    q = l2norm(heads(qkv[..., :W_A], DK_A)) * DK_A ** -0.5
    k = l2norm(heads(qkv[..., W_A:2 * W_A], DK_A))
    v = heads(qkv[..., 2 * W_A:], DV_A)
    dirs = lambda t: t.astype(F32).reshape(B_, L, 2, H_A).transpose(2, 0, 3, 1)
    beta = jax.nn.sigmoid(dirs(pa[..., 4 * W_A:4 * W_A + 2 * H_A]))
    g = -jnp.exp(a_log.astype(F32))[:, None, :, None] * jax.nn.softplus(
        dirs(pa[..., 4 * W_A + 2 * H_A:]) + dt_bias.astype(F32)[:, None, :, None])
    s0 = s0.astype(F32)
    o_f, s_f = delta_chunked(q, k, v, g[0], beta[0], s0[:, 0])
    o_b, s_b = delta_chunked(flip_seq(q), flip_seq(k), flip_seq(v), flip_seq(g[1]), flip_seq(beta[1]), s0[:, 1])
    o = (o_f + flip_seq(o_b)).transpose(0, 2, 1, 3)
    o = rmsnorm(o, onorm).reshape(B_, L, W_A).astype(pa.dtype) * jax.nn.silu(z)
    return o, jnp.stack([s_f, s_b], 1).astype(pa.dtype)


def mixer_gla(pc, s0, w_glr, b_glr, onorm):
    B_, L, _ = pc.shape
    heads = lambda t, d: t.astype(F32).reshape(B_, L, H_C, d).transpose(0, 2, 1, 3)
    q = heads(pc[..., :QK_C], DK_C) * DK_C ** -0.5
    k = heads(pc[..., QK_C:2 * QK_C], DK_C)
    v = heads(pc[..., 2 * QK_C:2 * QK_C + W_C], DV_C)
    z = pc[..., 2 * QK_C + W_C:2 * QK_C + 2 * W_C]
    lr = pc[..., 2 * QK_C + 2 * W_C:].astype(F32).reshape(B_, L, 2, GLA_RANK)
    gk = jax.nn.log_sigmoid(jnp.einsum('blnr,nrd->nbld', lr, w_glr.astype(F32))
                            + b_glr.astype(F32)[:, None, None, :]) / GLA_TAU
    gk = gk.reshape(2, B_, L, H_C, DK_C).transpose(0, 1, 3, 2, 4)
    s0 = s0.astype(F32)
    o_f, s_f = gla_chunked(q, k, v, gk[0], s0[:, 0])
    o_b, s_b = gla_chunked(flip_seq(q), flip_seq(k), flip_seq(v), flip_seq(gk[1]), s0[:, 1])
    o = (o_f + flip_seq(o_b)).transpose(0, 2, 1, 3)
    o = rmsnorm(o, onorm).reshape(B_, L, W_C).astype(pc.dtype) * jax.nn.silu(z)
    return o, jnp.stack([s_f, s_b], 1).astype(pc.dtype)


def even_layer(h, w_in, conv_w, a_log, dt_bias, onorm, sink, s0, ctx_kv):
    B_, L, _ = h.shape
    p = h @ w_in
    o_a, st = mixer_delta(p[..., :PA_EVEN], s0, conv_w, a_log, dt_bias, onorm)
    pb = p[..., PA_EVEN:]
    q = pb[..., :W_B].reshape(B_, L, H_B, HD)
    k = pb[..., W_B:W_B + HKV_B * HD].reshape(B_, L, HKV_B, HD)
    v = pb[..., W_B + HKV_B * HD:W_B + 2 * HKV_B * HD].reshape(B_, L, HKV_B, HD)
    z = pb[..., W_B + 2 * HKV_B * HD:]
    if ctx_kv is None:
        o_b = dense_attn(q, k, v, sink)
        kv = jnp.stack([k, v], 1)
    else:
        o_b = windowed_attn(rope2d(q), rope2d(k), v, ctx_kv[:, 0], ctx_kv[:, 1], sink)
        kv = None
    o_b = o_b * jax.nn.silu(z)
    return jnp.concatenate([o_a, o_b], -1), st, kv


def odd_layer(h, w_in, w_glr, b_glr, onorm, rpb, s0, ctx_kv):
    B_, L, _ = h.shape
    p = h @ w_in
    o_c, st = mixer_gla(p[..., :PC_ODD], s0, w_glr, b_glr, onorm)
    pd = p[..., PC_ODD:]
    q = pd[..., :W_D].reshape(B_, L, H_D, HD)
    k = pd[..., W_D:2 * W_D].reshape(B_, L, H_D, HD)
    v = pd[..., 2 * W_D:3 * W_D].reshape(B_, L, H_D, HD)
    z = pd[..., 3 * W_D:]
    if ctx_kv is None:
        o_d = dense_attn(q, k, v, None)
        kv = jnp.stack([k, v], 1)
    else:
        o_d = neighbourhood_attn(q, k, v, ctx_kv[:, 0], ctx_kv[:, 1], rpb)
        kv = None
    o_d = o_d * jax.nn.silu(z)
    return jnp.concatenate([o_c, o_d], -1), st, kv


def setup_inputs(seed: int = 0) -> dict:
    key = jax.random.key(seed)
    ks = jax.random.split(key, 25)
    nrm = lambda i, shape, s: jax.random.normal(ks[i], shape, F32) * s
    dt = jnp.exp(jax.random.uniform(ks[14], (N_EVEN, 2, H_A), F32, np.log(1e-3), np.log(1e-1)))
    return {
        'x_prompt': nrm(0, (BATCH, SEQ, D_MODEL), 1.0),
        'x_sample': nrm(1, (DEC_BATCH, DEC_SEQ, D_MODEL), 1.0),
        'state_delta': nrm(2, (DEC_BATCH, N_EVEN, 2, H_A, DK_A, DV_A), 0.1),
        'cache_kv_win': nrm(3, (DEC_BATCH, N_EVEN, 2, PAST_LEN, HKV_B, HD), 1.0),
        'state_gla': nrm(4, (DEC_BATCH, N_ODD, 2, H_C, DK_C, DV_C), 0.1),
        'cache_kv_nbr': nrm(5, (DEC_BATCH, N_ODD, 2, PAST_LEN, H_D, HD), 1.0),
        'c': nrm(6, (DEC_BATCH, D_MODEL), 1.0),
        'c_ctx': nrm(7, (D_MODEL,), 1.0),
        'norm_w': 1.0 + nrm(8, (DEPTH, D_MODEL), 0.02),
        'w_ada': nrm(9, (DEPTH, D_MODEL, 3 * D_MODEL), 0.5 * D_MODEL ** -0.5),
        'b_ada': nrm(10, (DEPTH, 3 * D_MODEL), 0.02),
        'w_in_even': nrm(11, (N_EVEN, D_MODEL, P_EVEN), D_MODEL ** -0.5),
        'conv_a': nrm(12, (N_EVEN, CONV_K, 3 * W_A), CONV_K ** -0.5),
        'a_log_a': jnp.log(jax.random.uniform(ks[13], (N_EVEN, 2, H_A), F32, 1.0, 16.0)),
        'dt_bias_a': jnp.log(jnp.expm1(dt)),
        'onorm_a': 1.0 + nrm(15, (N_EVEN, DV_A), 0.02),
        'sink_b': nrm(16, (N_EVEN, H_B), 1.0),
        'w_out_even': nrm(17, (N_EVEN, MIX_EVEN, D_MODEL), MIX_EVEN ** -0.5),
        'w_in_odd': nrm(18, (N_ODD, D_MODEL, P_ODD), D_MODEL ** -0.5),
        'w_glr_c': nrm(19, (N_ODD, 2, GLA_RANK, QK_C), GLA_RANK ** -0.5),
        'b_glr_c': nrm(20, (N_ODD, 2, QK_C), 0.1),
        'onorm_c': 1.0 + nrm(21, (N_ODD, DV_C), 0.02),
        'rpb_d': nrm(22, (N_ODD, H_D, 2 * NB_H - 1, 2 * NB_W - 1), 0.2),
        'w_out_odd': nrm(23, (N_ODD, MIX_ODD, D_MODEL), MIX_ODD ** -0.5),
        'final_norm_w': 1.0 + nrm(24, (D_MODEL,), 0.02),
    }


def reference(x_prompt, x_sample, state_delta, cache_kv_win, state_gla, cache_kv_nbr, c, c_ctx,
              norm_w, w_ada, b_ada, w_in_even, conv_a, a_log_a, dt_bias_a, onorm_a, sink_b, w_out_even,
              w_in_odd, w_glr_c, b_glr_c, onorm_c, rpb_d, w_out_odd, final_norm_w):
    xc, xl = x_prompt, x_sample
    cc = c_ctx[None, None, :]
    cl = c[:, None, :]
    bc = x_prompt.shape[0]
    new_delta, new_kvw, new_gla, new_kvn = [], [], [], []
    for li in range(DEPTH):
        hc, gate_c = adaln(xc, cc, norm_w[li], w_ada[li], b_ada[li])
        hl, gate_l = adaln(xl, cl, norm_w[li], w_ada[li], b_ada[li])
        if li % 2 == 0:
            e = li // 2
            prm = (w_in_even[e], conv_a[e], a_log_a[e], dt_bias_a[e], onorm_a[e], sink_b[e])
            s_zero = jnp.zeros((bc, 2, H_A, DK_A, DV_A), F32)
            oc, st, kv = even_layer(hc, *prm, s_zero, None)
            ol, _, _ = even_layer(hl, *prm, state_delta[:, e], cache_kv_win[:, e])
            new_delta.append(st)
            new_kvw.append(kv)
            w_out = w_out_even[e]
        else:
            o_i = li // 2
            prm = (w_in_odd[o_i], w_glr_c[o_i], b_glr_c[o_i], onorm_c[o_i], rpb_d[o_i])
            s_zero = jnp.zeros((bc, 2, H_C, DK_C, DV_C), F32)
            oc, st, kv = odd_layer(hc, *prm, s_zero, None)
            ol, _, _ = odd_layer(hl, *prm, state_gla[:, o_i], cache_kv_nbr[:, o_i])
            new_gla.append(st)
            new_kvn.append(kv)
            w_out = w_out_odd[o_i]
        xc = xc + gate_c * (oc @ w_out)
        xl = xl + gate_l * (ol @ w_out)
    y_prompt = rmsnorm(xc, final_norm_w)
    y_sample = rmsnorm(xl, final_norm_w)
    return (y_prompt, y_sample, jnp.stack(new_delta, 1), jnp.stack(new_kvw, 1),
            jnp.stack(new_gla, 1), jnp.stack(new_kvn, 1))
```

```python
import numpy as np
from contextlib import ExitStack
import concourse.bass as bass
import concourse.mybir as mybir
from concourse.bass_utils import run_bass_kernel_spmd

F32 = mybir.dt.float32
BF16 = mybir.dt.bfloat16
AF = mybir.ActivationFunctionType
ALU = mybir.AluOpType
AX = mybir.AxisListType

COMPUTE = ('pe', 'act', 'dve', 'pool')
NDMASEM = 32

D = 2048
NT = 1024
EPS = 1e-6
NEG = -30000.0
PA_EVEN = 4128
P_EVEN = 6688
PC_ODD = 3104
P_ODD = 7200
SCALE = 128 ** -0.5


class FW:
    def __init__(self, nc, stack):
        self.nc = nc
        self.stack = stack
        self.ops = {e: [] for e in ('pe', 'act', 'dve', 'pool', 'sp')}
        self.sem = {}
        for e in COMPUTE:
            self.sem[e] = stack.enter_context(nc.semaphore('s_' + e))
        self.dsem = [stack.enter_context(nc.semaphore('d%d' % i)) for i in range(NDMASEM)]
        self.dexp = [0] * NDMASEM
        self.dnext = 0
        self.dnext_p = 0
        self.cnt = {e: 0 for e in COMPUTE}
        self.waited = {}
        self.res = {}
        self.split = {}
        self.uniq = 0
        self.marks = []
        self.ninst = 0

    def sb(self, name, shape, dt=F32, stack=None, split=None):
        t = (stack or self.stack).enter_context(self.nc.sbuf_tensor('t_' + name, list(shape), dt))
        if split:
            self.split['t_' + name] = split * (2 if dt == BF16 else 4)
        return t

    def ps(self, name, shape, dt=F32, split=None):
        t = self.stack.enter_context(self.nc.psum_tensor('t_' + name, list(shape), dt))
        if split:
            self.split['t_' + name] = split * (2 if dt == BF16 else 4)
        return t

    def keys(self, x):
        if isinstance(x, str):
            return [x]
        name = x.tensor.name
        if name in OUT_SHAPES:
            self.uniq += 1
            return ['%s@%d' % (name, self.uniq)]
        sp = self.split.get(name)
        if sp is None:
            return [name]
        dims = x.ap
        esz = 2 if x.dtype == BF16 else 4
        pstride = dims[0][0]
        off = x.offset % pstride if pstride > 0 else x.offset
        span = 0
        for st, n in dims[1:]:
            span += abs(st) * (n - 1)
        lo = (off * esz) // sp
        hi = ((off + span) * esz) // sp
        return ['%s#%d' % (name, r) for r in range(lo, hi + 1)]

    def _keys(self, lst):
        out = []
        for x in lst:
            if x is None:
                continue
            out.extend(self.keys(x))
        return out

    def _deps(self, reads, writes):
        deps = {}

        def add(d):
            if d is None:
                return
            src, val = d
            if deps.get(src, 0) < val:
                deps[src] = val
        for r in reads:
            st = self.res.get(r)
            if st:
                add(st['w'])
        for w in writes:
            st = self.res.get(w)
            if st:
                add(st['w'])
                for d in st['r']:
                    add(d)
        return deps

    def _update(self, me, reads, writes):
        for r in reads:
            st = self.res.setdefault(r, {'w': None, 'r': []})
            st['r'] = [d for d in st['r'] if d[0] != me[0]] + [me]
        for w in writes:
            self.res[w] = {'w': me, 'r': []}

    def _semof(self, src):
        if isinstance(src, str):
            return self.sem[src]
        return self.dsem[src]

    def _emit_waits(self, eng, deps):
        for src, val in deps.items():
            if src == eng and eng == 'pe':
                continue
            key = (eng, src)
            if self.waited.get(key, 0) >= val:
                continue
            self.waited[key] = val
            self.ops[eng].append(('wait', self._semof(src), val))

    def op(self, eng, fn, ins=(), outs=()):
        reads = self._keys(ins)
        writes = self._keys(outs)
        writes = writes + [k for k in reads if k.startswith('t_ps') and k not in writes]
        deps = self._deps(reads, writes)
        self._emit_waits(eng, deps)
        self.cnt[eng] += 1
        me = (eng, self.cnt[eng])
        self.ops[eng].append(('inst', fn, self.sem[eng], 1))
        self._update(me, reads, writes)
        self.ninst += 1

    def dma(self, q, out, in_, extra_ins=(), **kw):
        reads = self._keys([in_] + list(extra_ins))
        writes = self._keys([out])
        deps = self._deps(reads, writes)
        half = NDMASEM // 2
        if q == 'pool':
            s = half + self.dnext_p
            self.dnext_p = (self.dnext_p + 1) % (NDMASEM - half)
        else:
            s = self.dnext
            self.dnext = (self.dnext + 1) % half
        if self.dexp[s] > 0:
            deps[s] = max(deps.get(s, 0), self.dexp[s])
        self._emit_waits(q, deps)
        self.dexp[s] += 16
        me = (s, self.dexp[s])
        self.ops[q].append(('inst', lambda e: e.dma_start(out=out, in_=in_, **kw), self.dsem[s], 16))
        self._update(me, reads, writes)
        self.ninst += 1

    def mark(self, label):
        self.marks.append((label, dict(self.cnt)))

    def barrier(self):
        deps = {}
        for s in range(NDMASEM):
            if self.dexp[s] > 0:
                deps[s] = self.dexp[s]
        for e in COMPUTE:
            if self.cnt[e] > 0:
                deps[e] = self.cnt[e]
        for e in ('pe', 'act', 'dve', 'pool', 'sp'):
            d = dict(deps)
            d.pop(e, None)
            self._emit_waits(e, d)

    def replay(self):
        nc = self.nc
        engmap = {'pe': 'tensor', 'act': 'scalar', 'dve': 'vector', 'pool': 'gpsimd', 'sp': 'sync'}
        with nc.Block() as block:
            for e, attr in engmap.items():
                ops = self.ops[e]

                def body(engobj, ops=ops):
                    for o in ops:
                        if o[0] == 'wait':
                            engobj.wait_ge(o[1], o[2])
                        else:
                            o[1](engobj).then_inc(o[2], o[3])
                getattr(block, attr)(body)

    def mm(self, out, lhsT, rhs, start=True, stop=True):
        self.op('pe', lambda e: e.matmul(out, lhsT=lhsT, rhs=rhs, start=start, stop=stop), [lhsT, rhs], [out])

    def tr(self, out, in_, ident):
        self.op('pe', lambda e: e.transpose(out, in_, ident), [in_, ident], [out])

    def act(self, out, in_, func, bias=None, scale=None, accum_out=None):
        kw = {}
        ins = [in_]
        if bias is not None:
            kw['bias'] = bias
            if not isinstance(bias, (int, float)):
                ins.append(bias)
        if scale is not None:
            kw['scale'] = scale
            if not isinstance(scale, (int, float)):
                ins.append(scale)
        outs = [out]
        if accum_out is not None:
            kw['accum_out'] = accum_out
            outs.append(accum_out)
        self.op('act', lambda e: e.activation(out=out, in_=in_, func=func, **kw), ins, outs)

    def tt(self, out, in0, in1, op, eng='dve'):
        self.op(eng, lambda e: e.tensor_tensor(out=out, in0=in0, in1=in1, op=op), [in0, in1], [out])

    def ts(self, out, in0, s1, op0, s2=None, op1=None, eng='dve'):
        ins = [in0]
        for s in (s1, s2):
            if s is not None and not isinstance(s, (int, float)):
                ins.append(s)
        if op1 is None:
            self.op(eng, lambda e: e.tensor_scalar(out=out, in0=in0, scalar1=s1, scalar2=None, op0=op0), ins, [out])
        else:
            self.op(eng, lambda e: e.tensor_scalar(out=out, in0=in0, scalar1=s1, scalar2=s2, op0=op0, op1=op1), ins, [out])

    def stt(self, out, in0, scalar, in1, op0, op1):
        ins = [in0, in1]
        if not isinstance(scalar, (int, float)):
            ins.append(scalar)
        self.op('dve', lambda e: e.scalar_tensor_tensor(out=out, in0=in0, scalar=scalar, in1=in1, op0=op0, op1=op1), ins, [out])

    def cp(self, out, in_, eng='dve'):
        if eng == 'act':
            self.op('act', lambda e: e.copy(out=out, in_=in_), [in_], [out])
        else:
            self.op(eng, lambda e: e.tensor_copy(out=out, in_=in_), [in_], [out])

    def recip(self, out, in_):
        self.op('dve', lambda e: e.reciprocal(out=out, in_=in_), [in_], [out])

    def memset(self, ap, val, eng='dve'):
        self.op(eng, lambda e: e.memset(ap, val), [], [ap])

    def rmax(self, out, in_):
        self.op('dve', lambda e: e.reduce_max(out=out, in_=in_, axis=AX.X), [in_], [out])


def _consts(role):
    lat = (role == 1)
    c = {}
    c['ident'] = np.eye(128, dtype=np.float32)
    p = np.arange(128)
    dr = p // 64
    t = p % 64
    same = dr[:, None] == dr[None, :]
    before_eq = np.where(dr[:, None] == 0, t[:, None] <= t[None, :], t[:, None] >= t[None, :])
    c['cum'] = (same & before_eq).astype(np.float32)
    c['blk'] = same.astype(np.float32)
    c['dirf'] = np.repeat((dr == 0).astype(np.float32)[:, None], 128, 1)
    c['dirb'] = np.repeat((dr == 1).astype(np.float32)[:, None], 128, 1)
    s_before_c = np.where(dr[:, None] == 0, t[None, :] < t[:, None], t[None, :] > t[:, None])
    negB = np.where(same & s_before_c, 0.0, NEG).astype(np.float32)
    c['negB'] = negB
    c['negA'] = negB.T.copy()
    s_beq_c = np.where(dr[:, None] == 0, t[None, :] <= t[:, None], t[None, :] >= t[:, None])
    negBi = np.where(same & s_beq_c, 0.0, NEG).astype(np.float32)
    c['negAi'] = negBi.T.copy()
    c['m01Ai'] = (negBi.T == 0.0).astype(np.float32)
    R = np.zeros((128, 128), np.float32)
    for b0 in (0, 64):
        for i in range(32):
            R[b0 + 32 + i, b0 + i] = -1.0
            R[b0 + i, b0 + 32 + i] = 1.0
    c['rperm'] = R
    J = np.zeros((128, 128), np.float32)
    for rq in range(2):
        for cc in range(64):
            J[rq * 64 + 63 - cc, rq * 64 + cc] = 1.0
    c['jrev'] = J
    tok = np.arange(NT)
    cos = np.ones((128, NT), np.float32)
    sin = np.zeros((128, NT), np.float32)
    if lat:
        for d in range(128):
            i = d % 32
            freq = np.float32(10000.0) ** np.float32(-i / 32.0)
            pos = (tok // 64) if d < 64 else (tok % 64)
            ang = pos.astype(np.float32) * freq
            cos[d] = np.cos(ang)
            sin[d] = np.sin(ang)
    c['cos'] = cos
    c['sin'] = sin
    mw = np.full((8, 128, 5, 128), NEG, np.float32)
    q = np.arange(128)
    for j in range(8):
        if lat:
            for si, m in enumerate((j - 1, j, j + 1)):
                if 0 <= m < 8:
                    qpos = j * 128 + q[:, None]
                    kpos = m * 128 + q[None, :]
                    mw[j, :, si, :] = np.where(np.abs(qpos - kpos) <= 128, 0.0, NEG)
            mw[j, :, 3:5, :] = 0.0
        else:
            seq = j // 2
            for si, m in enumerate((j - 1, j, j + 1)):
                if 0 <= m < 8 and m // 2 == seq:
                    mw[j, :, si, :] = 0.0
    c['maskw'] = (mw / SCALE).reshape(8, 128, 640).astype(np.float32)
    mn = np.full((8, 128, 7, 128), NEG, np.float32)
    for j in range(8):
        blks = NBLK[j]
        for si, m in enumerate(blks):
            if lat:
                r = 2 * j + q // 64
                cq = q % 64
                rs = np.clip(r - 4, 0, 8)
                cst = np.clip(cq - 8, 0, 48)
                kr = 2 * m + q // 64
                kc = q % 64
                ok = ((kr[None, :] >= rs[:, None]) & (kr[None, :] < rs[:, None] + 8) &
                      (kc[None, :] >= cst[:, None]) & (kc[None, :] < cst[:, None] + 16))
                mn[j, :, si, :] = np.where(ok, 0.0, NEG)
            else:
                if m // 2 == j // 2:
                    mn[j, :, si, :] = 0.0
        if lat:
            mn[j, :, 5:7, :] = 0.0
    c['maskn'] = (mn / SCALE).reshape(8, 128, 896).astype(np.float32)
    c['flag'] = np.full((128, 1), 1.0 if lat else 0.0, np.float32)
    return c


NBLK = {0: [0, 1, 2, 3], 1: [0, 1, 2, 3], 2: [0, 1, 2, 3, 4], 3: [1, 2, 3, 4, 5], 4: [2, 3, 4, 5, 6],
        5: [3, 4, 5, 6, 7], 6: [4, 5, 6, 7], 7: [4, 5, 6, 7]}

CONST_SHAPES = {'ident': (128, 128), 'cum': (128, 128), 'blk': (128, 128), 'dirf': (128, 128), 'dirb': (128, 128),
                'negB': (128, 128), 'negA': (128, 128), 'negAi': (128, 128), 'm01Ai': (128, 128),
                'rperm': (128, 128), 'jrev': (128, 128), 'cos': (128, NT), 'sin': (128, NT), 'maskw': (8, 128, 640),
                'maskn': (8, 128, 896), 'flag': (128, 1)}

IN_SHAPES = {
    'x': (NT, D), 'cond': (128, 16), 'st_delta': (2, 2, 8, 128, 128), 'kvw': (2, 2, 256, 2, 128),
    'st_gla': (2, 2, 4, 128, 256), 'kvn': (2, 2, 256, 8, 128),
    'norm_w': (128, 4, 16), 'w_ada': (4, D, 3 * D), 'b_ada': (128, 4, 48),
    'w_in_even': (2, D, P_EVEN), 'conv_a': (128, 2, 24, 5), 'a_log': (128, 2, 8), 'dt_bias': (128, 2, 8),
    'onorm_a': (128, 2), 'sink_b': (128, 2, 8), 'w_out_even': (2, D, D),
    'w_in_odd': (2, D, P_ODD), 'w_glr': (2, 2, 16, 512), 'b_glr': (2, 2, 512), 'onorm_c': (128, 2, 2),
    'rpb': (2, 8, 15, 31), 'w_out_odd': (2, D, D), 'final_w': (128, 16),
}
OUT_SHAPES = {
    'y': (NT, D), 'nsd': (4, 2, 2, 8, 128, 128), 'nkw': (4, 2, 2, 256, 2, 128),
    'nsg': (4, 2, 2, 4, 128, 256), 'nkn': (4, 2, 2, 256, 8, 128),
}


class Prog:
    def __init__(self, layers=(0, 1, 2, 3), final_norm=True, mixers='ABCD'):
        self.layers = tuple(layers)
        self.final_norm = final_norm
        self.mixers = mixers
        self.nc = bass.Bass("TRN2", target_bir_lowering=False)
        nc = self.nc
        self.din = {}
        for k, shp in list(IN_SHAPES.items()) + list(CONST_SHAPES.items()):
            self.din[k] = nc.dram_tensor(k, list(shp), F32, kind="ExternalInput").ap()
        self.dout = {}
        for k, shp in OUT_SHAPES.items():
            self.dout[k] = nc.dram_tensor(k, list(shp), F32, kind="ExternalOutput").ap()
        self.rpbp = nc.dram_tensor("rpbp", [8, 15, 128], F32, kind="Internal").ap()
        self.wq = 0
        with ExitStack() as st:
            self.st = st
            self.fw = FW(nc, st)
            self.build()
            self.fw.barrier()
            self.fw.replay()

    def psum(self, ncols=512):
        skip = self.psum_skip
        if ncols <= 512:
            while True:
                i = self.pi
                self.pi = (self.pi + 1) % 8
                if i not in skip:
                    break
            return self.pst[i // 2][:, (i % 2) * 512:(i % 2) * 512 + ncols]
        while True:
            if self.pi % 2:
                self.pi = (self.pi + 1) % 8
            i = self.pi
            self.pi = (self.pi + 2) % 8
            if i not in skip and (i + 1) not in skip:
                break
        return self.pst[i // 2][:, 0:ncols]

    def bg_step(self, n=1):
        for _ in range(n):
            if self.bg is None:
                return
            try:
                next(self.bg)
            except StopIteration:
                self.bg = None

    def wload(self, src, ncols):
        slab = self.wslab[self.wq]
        self.wq = (self.wq + 1) % len(self.wslab)
        self.fw.dma('pool', slab[:, :, 0:ncols], src.rearrange("(kc p) c -> p kc c", p=128))
        return slab

    def wload_rows(self, src, k0, nk, ncols):
        slab = self.wslab[self.wq]
        self.wq = (self.wq + 1) % len(self.wslab)
        self.fw.dma('pool', slab[:, 0:nk, 0:ncols], src.rearrange("(kc p) c -> p kc c", p=128)[:, k0:k0 + nk, :])
        return slab

    def proj_fm(self, slab, c0, evac):
        fw = self.fw
        for tt in range(2):
            ps = self.psum(512)
            for kc in range(16):
                fw.mm(ps, slab[:, kc, c0:c0 + 128], self.hT[:, kc, tt * 512:(tt + 1) * 512], start=(kc == 0), stop=(kc == 15))
            evac(ps, tt)

    def proj_tm(self, slab, c0, ncols, tb, evac):
        fw = self.fw
        ps = self.psum(ncols)
        for kc in range(16):
            fw.mm(ps, self.hT[:, kc, tb * 128:(tb + 1) * 128], slab[:, kc, c0:c0 + ncols], start=(kc == 0), stop=(kc == 15))
        evac(ps)

    def outproj(self, w_out, k0):
        fw = self.fw
        for oc2 in range(8):
            slab = self.wload_rows(w_out[:, oc2 * 256:(oc2 + 1) * 256], k0, 8, 256)
            for j in range(2):
                oc = oc2 * 2 + j
                for tt in range(2):
                    ps = self.psum(512)
                    for kc in range(8):
                        fw.mm(ps, slab[:, kc, j * 128:(j + 1) * 128], self.mixT[:, kc, tt * 512:(tt + 1) * 512],
                              start=(kc == 0), stop=(kc == 7))
                    xs = self.xT[:, oc, tt * 512:(tt + 1) * 512]
                    fw.stt(xs, ps, self.gate[:, oc:oc + 1], xs, ALU.mult, ALU.add)

    def rms_bcast(self, srcs, n, inv_n, dst_rstd, sqpool):
        fw = self.fw
        ps = self.psum(n)
        for i, s in enumerate(srcs):
            sq = sqpool[i % len(sqpool)][:, 0:n]
            fw.act(sq, s, AF.Square)
            fw.mm(ps, self.ones_b[:], sq, start=(i == 0), stop=(i == len(srcs) - 1))
        fw.act(dst_rstd, ps, AF.Ln, bias=self.epscol[:], scale=inv_n)
        fw.act(dst_rstd, dst_rstd, AF.Exp, scale=-0.5)

    def build(self):
        fw = self.fw
        nc = self.nc
        di = self.din
        self.pi = 0
        self.psum_skip = set()
        self.pst = [fw.ps('ps%d' % i, [128, 1024], F32, split=512) for i in range(4)]
        self.xT = fw.sb('xT', [128, 16, NT], F32, split=512)
        self.hT = fw.sb('hT', [128, 16, NT], BF16, split=512)
        self.mixT = fw.sb('mixT', [128, 8, NT], BF16, split=512)
        self.wslab = [fw.sb('wslab%d' % i, [128, 16, 256], BF16) for i in range(2)]
        cf = {}
        for k in ('ident', 'cum', 'blk', 'dirf', 'dirb', 'negB', 'negA', 'negAi'):
            cf[k] = fw.sb('c_' + k, [128, 128], F32)
            fw.dma('sp', cf[k][:], di[k])
        self.cf = cf
        self.ident_b = fw.sb('ident_b', [128, 128], BF16)
        fw.dma('pool', self.ident_b[:], di['ident'])
        self.m01 = fw.sb('m01', [128, 128], BF16)
        fw.dma('pool', self.m01[:], di['m01Ai'])
        self.rperm = fw.sb('rperm', [128, 128], BF16)
        fw.dma('pool', self.rperm[:], di['rperm'])
        self.jrev = fw.sb('jrev', [128, 128], BF16)
        fw.dma('pool', self.jrev[:], di['jrev'])
        self.ones_b = fw.sb('ones_b', [128, 128], BF16)
        fw.memset(self.ones_b[:], 1.0)
        self.ones_f = fw.sb('ones_f', [128, 128], F32)
        fw.memset(self.ones_f[:], 1.0)
        self.epscol = fw.sb('epscol', [128, 1], F32)
        fw.memset(self.epscol[:], EPS)
        self.flag = fw.sb('flag', [128, 1], F32)
        fw.dma('sp', self.flag[:], di['flag'])
        sm = {}
        for k in ('norm_w', 'b_ada', 'conv_a', 'a_log', 'dt_bias', 'onorm_a', 'sink_b', 'onorm_c', 'final_w', 'cond'):
            shp = IN_SHAPES[k]
            sm[k] = fw.sb('p_' + k, list(shp), F32)
            fw.dma('sp', sm[k][:], di[k])
        self.sm = sm
        self.scond = fw.sb('scond', [128, 16], BF16)
        fw.act(self.scond[:], sm['cond'][:], AF.Silu)
        self.modall = fw.sb('modall', [128, 4, 48], F32)
        self.mod_ready = set()
        self.bg = None
        self.Aw = fw.sb('Aw', [128, 16], F32)
        self.negsk = fw.sb('negsk', [128, 2, 8], F32)
        fw.ts(self.negsk[:], sm['sink_b'][:], -1.0, ALU.mult)
        self.onescol = fw.sb('onescol', [128, 1], F32)
        fw.memset(self.onescol[:], 1.0)
        self.dirsel = fw.sb('dirsel', [128, 2], F32)
        fw.cp(self.dirsel[:, 0:1], cf['dirf'][:, 0:1])
        fw.cp(self.dirsel[:, 1:2], cf['dirb'][:, 0:1])
        self.negbig = fw.sb('negbig', [128, 1], F32)
        fw.memset(self.negbig[:], 30000.0)

        with ExitStack() as ph:
            stage = [fw.sb('stage%d' % i, [128, D], F32, stack=ph) for i in range(2)]
            for tb in range(8):
                sg = stage[tb % 2]
                fw.dma('sp', sg[:], di['x'][tb * 128:(tb + 1) * 128, :])
                for c4 in range(4):
                    ps = self.psum(512)
                    for j in range(4):
                        c = c4 * 4 + j
                        fw.tr(ps[:, j * 128:(j + 1) * 128], sg[:, c * 128:(c + 1) * 128], cf['ident'][:])
                    dst = self.xT[:, c4 * 4:(c4 + 1) * 4, tb * 128:(tb + 1) * 128]
                    src = ps.rearrange("p (a b) -> p a b", a=4)
                    if c4 % 2 == 0:
                        fw.cp(dst, src, 'dve')
                    else:
                        fw.cp(dst, src, 'act')
            fw.barrier()

        for li in self.layers:
            self.fw.mark('layer%d' % li)
            self.layer(li)
        self.fw.mark('final')

        with ExitStack() as ph:
            stage = [fw.sb('ostage%d' % i, [128, D], F32, stack=ph) for i in range(2)]
            sqp = [fw.sb('fsq%d' % i, [128, 512], BF16, stack=ph) for i in range(2)]
            rstd = fw.sb('frstd', [128, 512], F32, stack=ph)
            for tt in range(2):
                sl = slice(tt * 512, (tt + 1) * 512)
                if self.final_norm:
                    self.rms_bcast([self.xT[:, c, sl] for c in range(16)], 512, 1.0 / D, rstd[:], sqp)
                    for c in range(16):
                        fw.stt(self.xT[:, c, sl], self.xT[:, c, sl], sm['final_w'][:, c:c + 1], rstd[:], ALU.mult, ALU.mult)
                for t4 in range(4):
                    tb = tt * 4 + t4
                    sg = stage[tb % 2]
                    for c4 in range(4):
                        ps = self.psum(512)
                        for j in range(4):
                            c = c4 * 4 + j
                            fw.tr(ps[:, j * 128:(j + 1) * 128], self.xT[:, c, tb * 128:(tb + 1) * 128], cf['ident'][:])
                        if c4 % 2 == 0:
                            fw.cp(sg[:, c4 * 512:(c4 + 1) * 512], ps, 'dve')
                        else:
                            fw.cp(sg[:, c4 * 512:(c4 + 1) * 512], ps, 'act')
                    fw.dma('sp', self.dout['y'][tb * 128:(tb + 1) * 128, :], sg[:])
            fw.barrier()

    def mod_gen(self, layers, ring, psm):
        fw = self.fw
        di = self.din
        for li in layers:
            slabs = {}

            def issue(sidx):
                slab = ring[sidx % len(ring)]
                fw.dma('pool', slab[:, :, 0:256], di['w_ada'][li][:, sidx * 256:(sidx + 1) * 256].rearrange("(kc p) c -> p kc c", p=128))
                slabs[sidx] = slab
            issue(0)
            for sidx in range(24):
                if sidx + 1 < 24:
                    issue(sidx + 1)
                yield
                slab = slabs.pop(sidx)
                for j in range(2):
                    col = sidx * 2 + j
                    pc = psm[:, (col % 2):(col % 2) + 1] if psm.shape[1] < 48 else psm[:, col:col + 1]
                    for kc in range(16):
                        fw.mm(pc, slab[:, kc, j * 128:(j + 1) * 128], self.scond[:, kc:kc + 1], start=(kc == 0), stop=(kc == 15))
                    fw.act(self.modall[:, li, col:col + 1], pc, AF.Identity, bias=self.sm['b_ada'][:, li, col:col + 1])
                yield
            self.mod_ready.add(li)

    def layer(self, li):
        fw = self.fw
        di = self.din
        sm = self.sm
        self.mod = self.modall[:, li, :]
        self.gate = self.mod[:, 32:48]
        if li not in self.mod_ready:
            for _ in self.mod_gen([li], self.wslab, self.psum(48)):
                pass
        fw.stt(self.Aw[:], self.mod[:, 16:32], 1.0, sm['norm_w'][:, li, :], ALU.add, ALU.mult)
        fw.mark('L%d.norm' % li)
        with ExitStack() as ph:
            sqp = [fw.sb('nsq%d_%d' % (li, i), [128, 512], BF16, stack=ph) for i in range(3)]
            rstd = fw.sb('nrstd%d' % li, [128, 512], F32, stack=ph)
            tmp = [fw.sb('ntmp%d_%d' % (li, i), [128, 512], F32, stack=ph) for i in range(3)]
            for tt in range(2):
                sl = slice(tt * 512, (tt + 1) * 512)
                self.rms_bcast([self.xT[:, c, sl] for c in range(16)], 512, 1.0 / D, rstd[:], sqp)
                for c in range(16):
                    t = tmp[c % 3]
                    fw.tt(t[:], self.xT[:, c, sl], rstd[:], ALU.mult)
                    fw.act(self.hT[:, c, sl], t[:], AF.Identity, bias=self.mod[:, c:c + 1], scale=self.Aw[:, c:c + 1])
            fw.barrier()
        if li % 2 == 0:
            e = li // 2
            w_in = di['w_in_even'][e]
            w_out = di['w_out_even'][e]
            fw.mark('L%d.mixA' % li)
            if 'A' in self.mixers:
                self.mixer_delta(e, w_in)
                fw.mark('L%d.outA' % li)
                self.outproj(w_out, 0)
            fw.mark('L%d.mixB' % li)
            if 'B' in self.mixers:
                self.mixer_win(e, w_in)
                fw.mark('L%d.outB' % li)
                self.outproj(w_out, 8)
        else:
            o = li // 2
            w_in = di['w_in_odd'][o]
            w_out = di['w_out_odd'][o]
            fw.mark('L%d.mixC' % li)
            if 'C' in self.mixers:
                self.mixer_gla(o, w_in)
                fw.mark('L%d.outC' % li)
                self.outproj(w_out, 0)
            fw.mark('L%d.mixD' % li)
            if 'D' in self.mixers:
                self.mixer_nbr(o, w_in)
                fw.mark('L%d.outD' % li)
                self.outproj(w_out, 8)
        fw.barrier()

    def pipeline(self, gens, depth=3):
        it = iter(gens)
        active = []
        done = False
        while True:
            if not done and len(active) < depth:
                try:
                    active.append(next(it))
                except StopIteration:
                    done = True
            if self.bg is not None:
                try:
                    next(self.bg)
                except StopIteration:
                    self.bg = None
            for g in reversed(list(active)):
                try:
                    next(g)
                except StopIteration:
                    active.remove(g)
            if done and not active:
                break

    def attention(self, pfx, qT, slots, sink_neg, zs, mix_dst, work, pre=None, uidx=0, runs=None):
        fw = self.fw
        if pre is not None:
            pre()
        ns = len(slots)
        W = ns * 128
        S = self.pst[uidx % 3][:, 0:W]
        if runs is None:
            for i, (kT, v, mask, bias) in enumerate(slots):
                cs = S[:, i * 128:(i + 1) * 128]
                fw.mm(cs, qT, kT, start=True, stop=False)
                fw.mm(cs, self.ident_b[:], mask, start=False, stop=(bias is None))
                if bias is not None:
                    fw.mm(cs, self.ident_b[:], bias, start=False, stop=True)
        else:
            for (c0, nc_, kTr_, mk_, bs_) in runs:
                cs = S[:, c0:c0 + nc_]
                fw.mm(cs, qT, kTr_, start=True, stop=False)
                fw.mm(cs, self.ident_b[:], mk_, start=False, stop=(bs_ is None))
                if bs_ is not None:
                    fw.mm(cs, self.ident_b[:], bs_, start=False, stop=True)
        yield
        mx, nm, rs, es, E, ET = work
        fw.rmax(mx[:], S)
        fw.ts(nm[:], mx[:], -SCALE, ALU.mult, sink_neg, ALU.min)
        fw.act(E[:, 0:W], S, AF.Exp, bias=nm[:], scale=SCALE, accum_out=rs[:])
        fw.act(es[:], sink_neg, AF.Exp, bias=nm[:], scale=-1.0)
        fw.tt(rs[:], rs[:], es[:], ALU.add)
        fw.recip(rs[:], rs[:])
        fw.ts(E[:, 0:W], E[:, 0:W], rs[:], ALU.mult)
        yield
        PT = self.pst[3][:, (uidx % 2) * 512:(uidx % 2) * 512 + 512]
        PTb = PT.bitcast(BF16)
        for i in range(ns):
            fw.tr(PTb[:, i * 128:(i + 1) * 128], E[:, i * 128:(i + 1) * 128], self.ident_b[:])
        fw.cp(ET[:, 0:W], PTb[:, 0:W], 'act')
        yield
        O = self.pst[uidx % 3][:, 896:1024]
        for i, (kT, v, mask, bias) in enumerate(slots):
            fw.mm(O, v, ET[:, i * 128:(i + 1) * 128], start=(i == 0), stop=(i == ns - 1))
        fw.tt(mix_dst, O, zs, ALU.mult)

    def mixer_win(self, e, w_in):
        fw = self.fw
        di = self.din
        with ExitStack() as ph:
            cos = fw.sb('cos%d' % e, [128, NT], F32, stack=ph)
            sin = fw.sb('sin%d' % e, [128, NT], F32, stack=ph)
            fw.dma('sp', cos[:], di['cos'])
            fw.dma('sp', sin[:], di['sin'])
            kTr = fw.sb('kTr%d' % e, [128, 2, NT], BF16, stack=ph)
            vtok = fw.sb('vtok%d' % e, [128, 8, 256], BF16, stack=ph)
            kcT = fw.sb('kcT%d' % e, [128, 2, 256], BF16, stack=ph)
            vc = fw.sb('vc%d' % e, [128, 2, 256], BF16, stack=ph)
            kctok = fw.sb('kctok%d' % e, [128, 2, 256], BF16, stack=ph)
            kvst = [fw.sb('kvst%d_%d' % (e, i), [128, 512], F32, stack=ph) for i in range(2)]
            q0 = [fw.sb('q0_%d_%d' % (e, i), [128, 512], BF16, stack=ph) for i in range(2)]
            t1 = [fw.sb('rt1_%d_%d' % (e, i), [128, 512], F32, stack=ph) for i in range(1)] * 2
            t2 = [fw.sb('rt2_%d_%d' % (e, i), [128, 512], F32, stack=ph) for i in range(1)] * 2
            qTr = fw.sb('qTr%d' % e, [128, 4, NT], BF16, stack=ph, split=128)
            zs = fw.sb('zsB%d' % e, [128, 4, NT], BF16, stack=ph, split=128)
            maskt = [fw.sb('maskw%d_%d' % (e, i), [128, 640], BF16, stack=ph) for i in range(2)]
            mx = [fw.sb('amx%d_%d' % (e, i), [128, 1], F32, stack=ph) for i in range(2)]
            nm = [fw.sb('anm%d_%d' % (e, i), [128, 1], F32, stack=ph) for i in range(2)]
            rs = [fw.sb('ars%d_%d' % (e, i), [128, 1], F32, stack=ph) for i in range(2)]
            es = [fw.sb('aes%d_%d' % (e, i), [128, 1], F32, stack=ph) for i in range(2)]
            E = [fw.sb('aE%d_%d' % (e, i), [128, 640], BF16, stack=ph) for i in range(2)]
            ET = [fw.sb('aET%d_%d' % (e, i), [128, 640], BF16, stack=ph) for i in range(2)]
            cB = PA_EVEN
            rc = [0]
            nxt_layers = [l for l in ((1, 2) if e == 0 else (3,)) if l in self.layers and l not in self.mod_ready]
            if nxt_layers:
                mring = [fw.sb('mring%d_%d' % (e, i), [128, 16, 256], BF16, stack=ph) for i in range(2)]
                self.bg = self.mod_gen(nxt_layers, mring, self.pst[0][:, 768:770])
                self.psum_skip = {1}

            def rope_evac(dst_fn):
                def ev(ps, tt):
                    i = rc[0] % 2
                    rc[0] += 1
                    sl = slice(tt * 512, (tt + 1) * 512)
                    fw.cp(q0[i][:], ps, 'act')
                    rot = self.psum(512)
                    fw.mm(rot, self.rperm[:], q0[i][:])
                    fw.tt(t1[i][:], q0[i][:], cos[:, sl], ALU.mult)
                    fw.tt(t2[i][:], rot, sin[:, sl], ALU.mult)
                    fw.tt(dst_fn(sl), t1[i][:], t2[i][:], ALU.add)
                return ev
            slab = self.wload(w_in[:, cB + 1024:cB + 1280], 256)
            for g in range(2):
                self.proj_fm(slab, g * 128, rope_evac(lambda sl, g=g: kTr[:, g, sl]))
                self.bg_step()
            slabv = self.wload(w_in[:, cB + 1280:cB + 1536], 256)
            for tb in range(8):
                sg = kvst[tb % 2]
                self.proj_tm(slab, 0, 256, tb, lambda ps, sg=sg: fw.cp(sg[:, 0:256], ps, 'act'))
                self.proj_tm(slabv, 0, 256, tb, lambda ps, sg=sg: fw.cp(sg[:, 256:512], ps, 'dve'))
                self.bg_step()
                fw.cp(vtok[:, tb, :], sg[:, 256:512], 'act')
                seq, half = tb // 2, tb % 2
                for kv in range(2):
                    fw.dma('sp', self.dout['nkw'][seq, e, kv, half * 128:(half + 1) * 128].rearrange("t g d -> t (g d)"),
                           sg[:, kv * 256:(kv + 1) * 256])
            for blk in range(2):
                fw.dma('pool', kctok[:, blk, :], di['kvw'][e, 0, blk * 128:(blk + 1) * 128].rearrange("t g d -> t (g d)"))
                fw.dma('pool', vc[:, blk, :], di['kvw'][e, 1, blk * 128:(blk + 1) * 128].rearrange("t g d -> t (g d)"))
            pt = self.psum(512).bitcast(BF16)
            for blk in range(2):
                for g in range(2):
                    fw.tr(pt[:, (g * 2 + blk) * 128:(g * 2 + blk + 1) * 128], kctok[:, blk, g * 128:(g + 1) * 128], self.ident_b[:])
            fw.cp(kcT[:].rearrange("p g k -> p (g k)"), pt[:, 0:512], 'dve')
            for hg in range(2):
                for hh2 in range(2):
                    slabq = self.wload(w_in[:, cB + hg * 512 + hh2 * 256:cB + hg * 512 + (hh2 + 1) * 256], 256)
                    slabz = self.wload(w_in[:, cB + 1536 + hg * 512 + hh2 * 256:cB + 1536 + hg * 512 + (hh2 + 1) * 256], 256)
                    for j2 in range(2):
                        hh = hh2 * 2 + j2
                        self.proj_fm(slabq, j2 * 128, rope_evac(lambda sl, hh=hh: qTr[:, hh, sl]))
                        self.proj_fm(slabz, j2 * 128, lambda ps, tt, hh=hh: fw.act(zs[:, hh, tt * 512:(tt + 1) * 512], ps, AF.Silu))
                        self.bg_step(2)
                it = 0
                units = []
                for j in range(8):
                    mk = maskt[j % 2]
                    for hh in range(4):
                        h = hg * 4 + hh
                        slots = []
                        for si, m in enumerate((j - 1, j, j + 1)):
                            if 0 <= m < 8:
                                slots.append((kTr[:, hg, m * 128:(m + 1) * 128], vtok[:, m, hg * 128:(hg + 1) * 128],
                                              mk[:, si * 128:(si + 1) * 128], None))
                        for cb in range(2):
                            slots.append((kcT[:, hg, cb * 128:(cb + 1) * 128], vc[:, cb, hg * 128:(hg + 1) * 128],
                                          mk[:, (3 + cb) * 128:(4 + cb) * 128], None))
                        w = it % 2
                        it += 1
                        pre = (lambda mk=mk, j=j: fw.dma('pool', mk[:], di['maskw'][j])) if hh == 0 else None
                        units.append(self.attention('B', qTr[:, hh, j * 128:(j + 1) * 128], slots, self.negsk[:, e, h:h + 1],
                                                    zs[:, hh, j * 128:(j + 1) * 128], self.mixT[:, h, j * 128:(j + 1) * 128],
                                                    (mx[w], nm[w], rs[w], es[w], E[w], ET[w]), pre=pre, uidx=it))
                self.pipeline(units, 4)
            if self.bg is not None:
                for _ in self.bg:
                    pass
                self.bg = None
            self.psum_skip = set()
            fw.barrier()


    def mixer_delta(self, e, w_in):
        fw = self.fw
        di = self.din
        cf = self.cf
        sm = self.sm
        with ExitStack() as ph:
            xb = fw.sb('dxb%d' % e, [128, 8, 16], F32, stack=ph)
            xg = fw.sb('dxg%d' % e, [128, 8, 16], F32, stack=ph)
            beta = fw.sb('dbeta%d' % e, [128, 8, 16], F32, stack=ph)
            lnb = fw.sb('dlnb%d' % e, [128, 8, 16], F32, stack=ph)
            g = fw.sb('dg%d' % e, [128, 8, 16], F32, stack=ph)
            nal = fw.sb('dnal%d' % e, [128, 8], F32, stack=ph)
            slab = self.wload(w_in[:, 4096:4128], 32)
            ps = self.psum(512)
            for n in range(16):
                for d in range(2):
                    for kc in range(16):
                        fw.mm(ps[d * 64:(d + 1) * 64, n * 32:(n + 1) * 32], self.hT[:, kc, n * 64:(n + 1) * 64], slab[:, kc, 0:32],
                              start=(kc == 0), stop=(kc == 15))
            p3 = ps.rearrange("p (n c) -> p n c", n=16)
            for d in range(2):
                R = slice(d * 64, (d + 1) * 64)
                fw.cp(xb[R].rearrange("p h n -> p n h"), p3[R, :, d * 8:(d + 1) * 8], 'dve')
                fw.cp(xg[R].rearrange("p h n -> p n h"), p3[R, :, 16 + d * 8:16 + (d + 1) * 8], 'dve')
            fw.act(beta[:], xb[:], AF.Exp, scale=-1.0)
            fw.ts(beta[:], beta[:], 1.0, ALU.add)
            fw.act(lnb[:], beta[:], AF.Ln)
            fw.ts(lnb[:], lnb[:], -1.0, ALU.mult)
            fw.recip(beta[:], beta[:])
            fw.tt(xg[:], xg[:], sm['dt_bias'][:, e, :].unsqueeze(2).to_broadcast([128, 8, 16]), ALU.add)
            fw.act(xg[:], xg[:], AF.Exp)
            fw.act(xg[:], xg[:], AF.Ln, bias=self.onescol[:])
            fw.act(nal[:], sm['a_log'][:, e, :], AF.Exp)
            fw.ts(nal[:], nal[:], -1.0, ALU.mult)
            fw.tt(g[:], xg[:], nal[:].unsqueeze(2).to_broadcast([128, 8, 16]), ALU.mult)
            for h in range(8):
                self.delta_head(e, h, w_in, beta, lnb, g)
            fw.barrier()

    def delta_head(self, e, h, w_in, beta, lnb, g):
        fw = self.fw
        di = self.din
        cf = self.cf
        sm = self.sm
        tg = '%d_%d' % (e, h)
        if h <= 1: self.fw.mark('D%d.h%d.start' % (e, h))
        bc_last = lambda ap: ap.unsqueeze(2).to_broadcast([128, 4, 128])
        bc_n = lambda ap: ap.unsqueeze(1).to_broadcast([128, 4, 128])
        v3 = lambda ap: ap.rearrange("p (n c) -> p n c", n=4)
        with ExitStack() as hd:
            u = fw.sb('du' + tg, [128, 2048], BF16, stack=hd, split=128)
            wT = fw.sb('dw' + tg, [128, 2048], BF16, stack=hd, split=128)
            attnT = fw.sb('dat' + tg, [128, 2048], BF16, stack=hd, split=128)
            qgT = fw.sb('dqg' + tg, [128, 2048], BF16, stack=hd, split=128)
            kdec = fw.sb('dkd' + tg, [128, 2048], BF16, stack=hd, split=128)
            egl = fw.sb('degl' + tg, [128, 2, 16], F32, stack=hd)
            oacc = fw.sb('doacc' + tg, [128, NT], F32, stack=hd, split=64)
            S2 = [[fw.sb('dS%s_%d_%d' % (tg, d, i), [128, 128], F32, stack=hd) for i in range(2)] for d in range(2)]
            Sb2 = [[fw.sb('dSb%s_%d_%d' % (tg, d, i), [128, 128], BF16, stack=hd) for i in range(2)] for d in range(2)]
            vnw = [[fw.sb('dvn%s_%d_%d' % (tg, d, i), [128, 128], BF16, stack=hd) for i in range(2)] for d in range(2)]
            fw.memset(oacc[:], 0.0, 'pool')
            for d in range(2):
                fw.dma('sp', S2[d][0][:], di['st_delta'][e, d, h])
                fw.ts(S2[d][0][:], S2[d][0][:], self.flag[:], ALU.mult)
                fw.cp(Sb2[d][0][:], S2[d][0][:], 'act')
            with ExitStack() as it:
                qkv = [fw.sb('dqkv%s_%d' % (tg, i), [128, NT], BF16, stack=it) for i in range(3)]
                gcc = fw.sb('dgcc' + tg, [128, 16], F32, stack=it)
                gbc = fw.sb('dgbc' + tg, [128, 16], F32, stack=it)
                bge = fw.sb('dbge' + tg, [128, 16], F32, stack=it)
                edk = fw.sb('dedk' + tg, [128, 16], F32, stack=it)
                with ExitStack() as cv:
                    cvs = []
                    for ci in range(2):
                        xpad = fw.sb('dxp%s_%d' % (tg, ci), [128, 4, 260], BF16, stack=cv)
                        acc = fw.sb('dacc%s_%d' % (tg, ci), [128, NT], F32, stack=cv)
                        rn = fw.sb('drn%s_%d' % (tg, ci), [128, 512], F32, stack=cv)
                        sqp = [fw.sb('dsq%s_%d' % (tg, ci), [128, 512], BF16, stack=cv)]
                        dk = fw.sb('ddk%s_%d' % (tg, ci), [128, 5, 128], BF16, stack=cv)
                        fw.memset(xpad[:], 0.0, 'pool' if ci else 'dve')
                        cvs.append((xpad, acc, rn, sqp, dk))

                    def front_q(qi, cvset):
                        xpad, acc, rn, sqp, dk = cvset
                        col = qi * 1024 + h * 128
                        slab = self.wload(w_in[:, col:col + 128], 128)
                        cw = sm['conv_a'][:, e, qi * 8 + h, :]
                        for k in range(5):
                            fw.act(dk[:, k, :], self.ident_b[:], AF.Copy, scale=cw[:, k:k + 1])
                        self.proj_fm(slab, 0, lambda ps, tt: fw.cp(xpad[:, tt * 2:tt * 2 + 2, 2:258], ps.rearrange("p (s t) -> p s t", s=2), 'act'))
                        yield
                        fw.ts(xpad[:, 1:4, 0:2], xpad[:, 0:3, 256:258], self.flag[:], ALU.mult)
                        fw.ts(xpad[:, 0:3, 258:260], xpad[:, 1:4, 2:4], self.flag[:], ALU.mult)
                        pcv = self.psum(1024)
                        for sg in range(4):
                            for k in range(5):
                                fw.mm(pcv[:, sg * 256:(sg + 1) * 256], dk[:, k, :], xpad[:, sg, k:k + 256], start=(k == 0), stop=(k == 4))
                        if qi < 2:
                            fw.act(acc[:], pcv, AF.Silu)
                        else:
                            fw.act(qkv[2][:], pcv, AF.Silu)
                        yield
                        if qi < 2:
                            for tt in range(2):
                                sl = slice(tt * 512, (tt + 1) * 512)
                                self.rms_bcast([acc[:, sl]], 512, 1.0, rn[:], sqp)
                                fw.stt(qkv[qi][:, sl], acc[:, sl], SCALE if qi == 0 else 1.0, rn[:], ALU.mult, ALU.mult)
                    self.pipeline([front_q(0, cvs[0]), front_q(1, cvs[1]), front_q(2, cvs[0])], 2)
                    fw.barrier()
                if h == 0: fw.mark('D%d.h0.batches' % e)
                TP = []
                for pp in range(2):
                    P = [fw.sb('dT%s_%d_%d' % (tg, pp, i), [128, 512], F32, stack=it) for i in range(6)]
                    P += [fw.sb('dB%s_%d_%d' % (tg, pp, i), [128, 512], BF16, stack=it) for i in range(3)]
                    TP.append(P)
                qn, kn, vn = qkv
                gh = g[:, h, :]
                pc = self.psum(64)
                fw.mm(pc[:, 0:16], cf['cum'][:], gh)
                fw.mm(pc[:, 16:32], cf['blk'][:], gh)
                fw.mm(pc[:, 32:48], cf['dirf'][:], gh)
                fw.mm(pc[:, 48:64], cf['dirb'][:], gh)
                fw.cp(gcc[:], pc[:, 0:16], 'dve')
                fw.act(egl[:].rearrange("p a b -> p (a b)"), pc[:, 32:64], AF.Exp)
                fw.tt(gbc[:], gcc[:], lnb[:, h, :], ALU.add)
                fw.act(bge[:], gcc[:], AF.Exp)
                fw.tt(bge[:], bge[:], beta[:, h, :], ALU.mult)
                fw.tt(edk[:], pc[:, 16:32], gcc[:], ALU.subtract)
                fw.act(edk[:], edk[:], AF.Exp)
                def dbatch(nb, P):
                    t0, t1, tE, a0, b0, b1, kbg, vb, Xf = P
                    if nb == 1:
                        for _ in range(8):
                            yield
                    pp = nb % 2
                    bk = [self.pst[2 * pp][:, 0:512], self.pst[2 * pp][:, 512:1024],
                          self.pst[2 * pp + 1][:, 0:512], self.pst[2 * pp + 1][:, 512:1024]]
                    bs = slice(nb * 4, (nb + 1) * 4)
                    pk = slice(nb * 512, (nb + 1) * 512)
                    tok = slice(nb * 256, (nb + 1) * 256)
                    fw.tt(v3(t0[:]), bc_n(cf['cum'][:]), bc_last(gh[:, bs]), ALU.mult)
                    pg = bk[0]
                    fw.mm(pg, self.ones_f[:], t0[:])
                    fw.tt(v3(t1[:]), bc_n(cf['ident'][:]), bc_last(lnb[:, h, bs]), ALU.mult)
                    fw.tt(t1[:], t1[:], t0[:], ALU.add)
                    pgb = bk[1]
                    fw.mm(pgb, self.ones_f[:], t1[:])
                    pkk = bk[2]
                    pkq = bk[3]
                    for j in range(4):
                        n = nb * 4 + j
                        cs = slice(n * 64, (n + 1) * 64)
                        for d in range(2):
                            for d2 in range(2):
                                fw.mm(pkk[d * 64:(d + 1) * 64, j * 128 + d2 * 64:j * 128 + (d2 + 1) * 64], kn[:, cs], kn[:, cs])
                                fw.mm(pkq[d * 64:(d + 1) * 64, j * 128 + d2 * 64:j * 128 + (d2 + 1) * 64], kn[:, cs], qn[:, cs])
                    yield
                    fw.stt(v3(tE[:]), v3(pg), -1.0, bc_n(cf['negB'][:]), ALU.mult, ALU.add)
                    fw.tt(v3(tE[:]), v3(tE[:]), bc_last(gbc[:, bs]), ALU.add)
                    fw.act(tE[:], tE[:], AF.Exp)
                    fw.stt(b0[:], pkk, -1.0, tE[:], ALU.mult, ALU.mult)
                    yield
                    fw.tt(v3(tE[:]), v3(pgb), bc_n(cf['negA'][:]), ALU.add)
                    fw.tt(v3(tE[:]), v3(tE[:]), bc_last(gcc[:, bs]), ALU.subtract)
                    fw.act(tE[:], tE[:], AF.Exp)
                    fw.stt(a0[:], pkk, -1.0, tE[:], ALU.mult, ALU.mult)
                    fw.tt(v3(t0[:]), v3(a0[:]), bc_n(cf['ident'][:]), ALU.add)
                    yield
                    fw.tt(v3(tE[:]), v3(pg), bc_n(cf['negAi'][:]), ALU.add)
                    fw.tt(v3(tE[:]), v3(tE[:]), bc_last(gcc[:, bs]), ALU.subtract)
                    fw.act(tE[:], tE[:], AF.Exp)
                    fw.tt(attnT[:, pk], pkq, tE[:], ALU.mult)
                    yield
                    fw.act(tE[:], pg, AF.Exp)
                    q4 = qn[:, tok].rearrange("p (n t) -> p n t", n=4).unsqueeze(2).to_broadcast([128, 4, 2, 64])
                    fw.tt(qgT[:, pk].rearrange("p (n d t) -> p n d t", n=4, d=2), q4,
                          tE[:].rearrange("p (n d t) -> p n d t", n=4, d=2), ALU.mult)
                    yield
                    cur = (a0, b0)
                    nxt = (tE, b1)
                    xc, xn = t0, t1

                    def mm4(pd, lt, rt):
                        for j in range(4):
                            c = slice(j * 128, (j + 1) * 128)
                            fw.mm(pd[:, c], lt[:, c], rt[:, c])
                    for lvl in range(5):
                        A, B = cur
                        if lvl < 4:
                            pa = bk[0]
                            mm4(pa, B, A)
                            pb = bk[1]
                            mm4(pb, A, B)
                            fw.cp(nxt[0][:], pa, 'act')
                            fw.cp(nxt[1][:], pb, 'act')
                        else:
                            pb = bk[1]
                            mm4(pb, A, B)
                            fw.cp(nxt[1][:], pb, 'dve')
                        yield
                        px = bk[2]
                        mm4(px, nxt[1], xc)
                        if lvl < 4:
                            fw.tt(xn[:], px, xc[:], ALU.add)
                        else:
                            fw.tt(Xf[:], px, xc[:], ALU.add)
                        cur, nxt = nxt, cur
                        xc, xn = xn, xc
                        yield
                    ptk = bk[3].bitcast(BF16)
                    ptv = bk[0].bitcast(BF16)
                    for j in range(4):
                        n = nb * 4 + j
                        for d in range(2):
                            fw.tr(ptk[d * 64:(d + 1) * 64, j * 128:(j + 1) * 128], kn[:, n * 64:(n + 1) * 64], self.ident_b[:])
                            fw.tr(ptv[d * 64:(d + 1) * 64, j * 128:(j + 1) * 128], vn[:, n * 64:(n + 1) * 64], self.ident_b[:])
                    fw.tt(v3(kbg[:]), v3(ptk[:, 0:512]), bc_last(bge[:, bs]), ALU.mult)
                    fw.tt(v3(kdec[:, pk]), v3(ptk[:, 0:512]), bc_last(edk[:, bs]), ALU.mult)
                    fw.tt(v3(vb[:]), v3(ptv[:, 0:512]), bc_last(beta[:, h, bs]), ALU.mult)
                    yield
                    pu = bk[1]
                    pw = bk[2]
                    for j in range(4):
                        c = slice(j * 128, (j + 1) * 128)
                        fw.mm(pu[:, c], Xf[:, c], vb[:, c])
                        fw.mm(pw[:, c], kbg[:, c], Xf[:, c])
                    fw.cp(u[:, pk], pu, 'act')
                    fw.cp(wT[:, pk], pw, 'dve')
                self.pipeline([dbatch(nb, TP[nb % 2]) for nb in range(4)], 2)
                fw.barrier()
            if h == 0: fw.mark('D%d.h0.scan' % e)
            cur = 0
            for i in range(16):
                nn = [i, 15 - i]
                Rr = [slice(0, 64), slice(64, 128)]
                Sc = [S2[d][cur] for d in range(2)]
                Sn = [S2[d][1 - cur] for d in range(2)]
                Sbc = [Sb2[d][cur] for d in range(2)]
                Sbn = [Sb2[d][1 - cur] for d in range(2)]
                if i > 0 and i % 4 == 0:
                    for d in range(2):
                        fw.ts(Sc[d][:], Sc[d][:], self.flag[:], ALU.mult)
                        fw.cp(Sbc[d][:], Sc[d][:], 'act')
                ccs = [slice(nn[d] * 128 + d * 64, nn[d] * 128 + d * 64 + 64) for d in range(2)]
                PSs = [self.pst[d][:, (i % 2) * 512:(i % 2) * 512 + 512] for d in range(2)]
                vws = [vnw[d][i % 2] for d in range(2)]
                for d in range(2):
                    fw.mm(PSs[d][Rr[d], 0:128], wT[:, ccs[d]], Sbc[d][:])
                for d in range(2):
                    n = nn[d]
                    fw.tt(vws[d][Rr[d], :], u[Rr[d], n * 128:(n + 1) * 128], PSs[d][Rr[d], 0:128], ALU.subtract)
                for d in range(2):
                    n = nn[d]
                    fw.mm(PSs[d][:, 128:256], kdec[Rr[d], n * 128:(n + 1) * 128], vws[d][Rr[d], :])
                for d in range(2):
                    po = PSs[d][:, 256:320]
                    fw.mm(po, Sbc[d][:], qgT[:, ccs[d]], start=True, stop=False)
                    fw.mm(po, vws[d][Rr[d], :], attnT[Rr[d], ccs[d]], start=False, stop=True)
                for d in range(2):
                    fw.stt(Sbn[d][:], Sc[d][:], egl[:, d, nn[d]:nn[d] + 1], PSs[d][:, 128:256], ALU.mult, ALU.add)
                for d in range(2):
                    fw.stt(Sn[d][:], Sc[d][:], egl[:, d, nn[d]:nn[d] + 1], PSs[d][:, 128:256], ALU.mult, ALU.add)
                for d in range(2):
                    n = nn[d]
                    oc = oacc[:, n * 64:(n + 1) * 64]
                    fw.tt(oc, oc, PSs[d][:, 256:320], ALU.add)
                    if i % 4 == 3:
                        fw.dma('sp', self.dout['nsd'][n // 4, e, d, h], Sn[d][:])
                cur = 1 - cur
            if h == 0: fw.mark('D%d.h0.post' % e)
            with ExitStack() as post:
                zs = fw.sb('dzs' + tg, [128, NT], BF16, stack=post)
                sqp = [fw.sb('dpsq%s_%d' % (tg, i), [128, 512], BF16, stack=post) for i in range(2)]
                rstd = fw.sb('dprs' + tg, [128, 512], F32, stack=post)
                tmp = fw.sb('dptmp' + tg, [128, 512], F32, stack=post)
                slab = self.wload(w_in[:, 3072 + h * 128:3072 + h * 128 + 128], 128)
                self.proj_fm(slab, 0, lambda ps, tt: fw.act(zs[:, tt * 512:(tt + 1) * 512], ps, AF.Silu))
                for tt in range(2):
                    sl = slice(tt * 512, (tt + 1) * 512)
                    self.rms_bcast([oacc[:, sl]], 512, 1.0 / 128.0, rstd[:], sqp)
                    fw.stt(tmp[:], oacc[:, sl], sm['onorm_a'][:, e:e + 1], rstd[:], ALU.mult, ALU.mult)
                    fw.tt(self.mixT[:, h, sl], tmp[:], zs[:, sl], ALU.mult)
                fw.barrier()

    def mixer_gla(self, o, w_in):
        fw = self.fw
        di = self.din
        cf = self.cf
        with ExitStack() as ph:
            lrT = [fw.sb('lrT%d_%d' % (o, d), [32, NT], BF16, stack=ph) for d in range(2)]
            waug = [fw.sb('waug%d_%d' % (o, d), [32, 512], BF16, stack=ph) for d in range(2)]
            for d in range(2):
                fw.memset(lrT[d][:], 1.0)
                fw.dma('pool', waug[d][0:16, :], di['w_glr'][o, d])
                fw.dma('pool', waug[d][16:17, :], di['b_glr'][o, d:d + 1, :])
            slab = self.wload(w_in[:, 3072:3104], 32)
            for d in range(2):
                for tt in range(2):
                    ps = self.psum(512)
                    for kc in range(16):
                        fw.mm(ps[0:16, :], slab[:, kc, d * 16:(d + 1) * 16], self.hT[:, kc, tt * 512:(tt + 1) * 512],
                              start=(kc == 0), stop=(kc == 15))
                    fw.cp(lrT[d][0:16, tt * 512:(tt + 1) * 512], ps[0:16, :], 'act')
            for h in range(4):
                self.gla_head(o, h, w_in, lrT, waug)
            fw.barrier()

    def gla_head(self, o, h, w_in, lrT, waug):
        fw = self.fw
        di = self.din
        cf = self.cf
        tg = '%d_%d' % (o, h)
        if h <= 1: self.fw.mark('G%d.h%d.start' % (o, h))
        with ExitStack() as hd:
            qg = fw.sb('gqg' + tg, [128, 2048], BF16, stack=hd, split=128)
            kg = fw.sb('gkg' + tg, [128, 2048], BF16, stack=hd, split=128)
            attnT = fw.sb('gat' + tg, [128, 2048], BF16, stack=hd, split=128)
            kdec = fw.sb('gkd' + tg, [128, 2048], BF16, stack=hd, split=128)
            v2tok = fw.sb('gv2' + tg, [128, 16, 256], BF16, stack=hd, split=256)
            ebl = fw.sb('gebl' + tg, [128, 16, 2], F32, stack=hd)
            oacc = fw.sb('goacc' + tg, [128, 2, NT], F32, stack=hd, split=64)
            S2 = [[fw.sb('gS%s_%d_%d' % (tg, d, i), [128, 256], F32, stack=hd) for i in range(2)] for d in range(2)]
            Sb2 = [[fw.sb('gSb%s_%d_%d' % (tg, d, i), [128, 256], BF16, stack=hd) for i in range(2)] for d in range(2)]
            fw.memset(oacc[:], 0.0, 'pool')
            for d in range(2):
                fw.dma('sp', S2[d][0][:], di['st_gla'][o, d, h])
                fw.ts(S2[d][0][:], S2[d][0][:], self.flag[:], ALU.mult)
                fw.cp(Sb2[d][0][:], S2[d][0][:], 'act')
            with ExitStack() as it:
                qT = fw.sb('gq' + tg, [128, NT], BF16, stack=it)
                kT = fw.sb('gk' + tg, [128, NT], BF16, stack=it)
                vT = fw.sb('gv' + tg, [128, 2, NT], BF16, stack=it)
                gk2 = fw.sb('ggk' + tg, [128, 16, 128], F32, stack=it, split=512)
                tA = fw.sb('gtA' + tg, [128, 512], F32, stack=it)
                tB = fw.sb('gtB' + tg, [128, 512], F32, stack=it)
                tC = fw.sb('gtC' + tg, [128, 512], F32, stack=it)
                for nb in range(4):
                    ps = self.psum(512)
                    for j in range(4):
                        n = nb * 4 + j
                        for d in range(2):
                            fw.mm(ps[d * 64:(d + 1) * 64, j * 128:(j + 1) * 128], lrT[d][0:17, n * 64:(n + 1) * 64],
                                  waug[d][0:17, h * 128:(h + 1) * 128])
                    fw.act(tA[:], ps, AF.Exp, scale=-1.0)
                    fw.act(tA[:], tA[:], AF.Ln, bias=self.onescol[:])
                    fw.ts(gk2[:, nb * 4:(nb + 1) * 4, :].rearrange("p a b -> p (a b)"), tA[:], -1.0 / 16.0, ALU.mult)
                pe = self.psum(32)
                for n in range(16):
                    fw.mm(pe[:, n * 2:n * 2 + 2], gk2[:, n, :], self.dirsel[:])
                fw.act(ebl[:].rearrange("p a b -> p (a b)"), pe, AF.Exp)
                slab = self.wload(w_in[:, h * 128:h * 128 + 128], 128)
                self.proj_fm(slab, 0, lambda ps, tt: fw.ts(qT[:, tt * 512:(tt + 1) * 512], ps, SCALE, ALU.mult))
                slab = self.wload(w_in[:, 512 + h * 128:512 + h * 128 + 128], 128)
                self.proj_fm(slab, 0, lambda ps, tt: fw.cp(kT[:, tt * 512:(tt + 1) * 512], ps, 'act'))
                slab = self.wload(w_in[:, 1024 + h * 256:1024 + h * 256 + 256], 256)
                for hf in range(2):
                    self.proj_fm(slab, hf * 128, lambda ps, tt, hf=hf: fw.cp(vT[:, hf, tt * 512:(tt + 1) * 512], ps, 'act'))
                for nb in range(4):
                    tok = slice(nb * 256, (nb + 1) * 256)
                    pk = slice(nb * 512, (nb + 1) * 512)
                    ps = self.psum(512)
                    for j in range(4):
                        n = nb * 4 + j
                        fw.mm(ps[:, j * 128:(j + 1) * 128], gk2[:, n, :], cf['cum'][:])
                    fw.act(tA[:], ps, AF.Exp)
                    fw.act(tB[:], ps, AF.Exp, scale=-1.0)
                    v4 = lambda ap: ap.rearrange("p (n d t) -> p n d t", n=4, d=2)
                    b4 = lambda ap: ap.rearrange("p (n t) -> p n t", n=4).unsqueeze(2).to_broadcast([128, 4, 2, 64])
                    fw.tt(v4(qg[:, pk]), b4(qT[:, tok]), v4(tA[:]), ALU.mult)
                    fw.tt(v4(kg[:, pk]), b4(kT[:, tok]), v4(tB[:]), ALU.mult)
                    pa = self.psum(512)
                    for j in range(4):
                        n = nb * 4 + j
                        cs = slice(n * 128, (n + 1) * 128)
                        fw.mm(pa[:, j * 128:(j + 1) * 128], kg[:, cs], qg[:, cs])
                    fw.tt(attnT[:, pk].rearrange("p (n c) -> p n c", n=4), pa.rearrange("p (n c) -> p n c", n=4),
                          self.m01[:].unsqueeze(1).to_broadcast([128, 4, 128]), ALU.mult)
                    pb = self.psum(512)
                    fw.mm(pb, cf['cum'][:], gk2[:, nb * 4:(nb + 1) * 4, :].rearrange("p a b -> p (a b)"))
                    pl = self.psum(512)
                    fw.mm(pl, cf['blk'][:], gk2[:, nb * 4:(nb + 1) * 4, :].rearrange("p a b -> p (a b)"))
                    fw.cp(tC[:], pb, 'act')
                    fw.tt(tC[:], pl, tC[:], ALU.subtract)
                    fw.act(tC[:], tC[:], AF.Exp)
                    pt = self.psum(512).bitcast(BF16)
                    for j in range(4):
                        n = nb * 4 + j
                        for d in range(2):
                            fw.tr(pt[d * 64:(d + 1) * 64, j * 128:(j + 1) * 128], kT[:, n * 64:(n + 1) * 64], self.ident_b[:])
                    fw.tt(kdec[:, pk], pt[:, 0:512], tC[:], ALU.mult)
                    pv = self.psum(512).bitcast(BF16)
                    for j in range(4):
                        n = nb * 4 + j
                        for hf in range(2):
                            for d in range(2):
                                fw.tr(pv[d * 64:(d + 1) * 64, j * 256 + hf * 128:j * 256 + (hf + 1) * 128],
                                      vT[:, hf, n * 64:(n + 1) * 64], self.ident_b[:])
                    fw.cp(v2tok[:, nb * 4:(nb + 1) * 4, :].rearrange("p a b -> p (a b)"), pv[:, 0:1024], 'dve')
                fw.barrier()
            if h == 0: fw.mark('G%d.h0.scan' % o)
            Rr = [slice(0, 64), slice(64, 128)]

            def issue_pss(i, d):
                n = i if d == 0 else 15 - i
                p = self.pst[d][:, (i % 2) * 256:(i % 2) * 256 + 256]
                fw.mm(p, kdec[Rr[d], n * 128:(n + 1) * 128], v2tok[Rr[d], n, :])
                return p
            pq = [issue_pss(0, 0), issue_pss(0, 1)]
            cur = 0
            for i in range(16):
                nn = [i, 15 - i]
                Sc = [S2[d][cur] for d in range(2)]
                Sn = [S2[d][1 - cur] for d in range(2)]
                Sbc = [Sb2[d][cur] for d in range(2)]
                Sbn = [Sb2[d][1 - cur] for d in range(2)]
                if i > 0 and i % 4 == 0:
                    for d in range(2):
                        fw.ts(Sc[d][:], Sc[d][:], self.flag[:], ALU.mult)
                        fw.cp(Sbc[d][:], Sc[d][:], 'act')
                for d in range(2):
                    fw.stt(Sn[d][:], Sc[d][:], ebl[:, nn[d], d:d + 1], pq[d], ALU.mult, ALU.add)
                for d in range(2):
                    fw.cp(Sbn[d][:], Sn[d][:], 'act')
                pos = []
                for d in range(2):
                    n = nn[d]
                    cc = slice(n * 128 + d * 64, n * 128 + d * 64 + 64)
                    for hf in range(2):
                        pc0 = 512 + ((2 * i + hf) % 4) * 64
                        po = self.pst[d][:, pc0:pc0 + 64]
                        fw.mm(po, Sbc[d][:, hf * 128:(hf + 1) * 128], qg[:, cc], start=True, stop=False)
                        fw.mm(po, v2tok[Rr[d], n, hf * 128:(hf + 1) * 128], attnT[Rr[d], cc], start=False, stop=True)
                        pos.append((po, oacc[:, hf, n * 64:(n + 1) * 64]))
                if i + 1 < 16:
                    pq = [issue_pss(i + 1, 0), issue_pss(i + 1, 1)]
                for po, oc in pos:
                    fw.tt(oc, oc, po, ALU.add)
                if i % 4 == 3:
                    for d in range(2):
                        fw.dma('sp', self.dout['nsg'][nn[d] // 4, o, d, h], Sn[d][:])
                cur = 1 - cur
            if h == 0: fw.mark('G%d.h0.post' % o)
            with ExitStack() as post:
                zs = fw.sb('gzs' + tg, [128, 2, NT], BF16, stack=post)
                sqp = [fw.sb('gsq%s_%d' % (tg, i), [128, 512], BF16, stack=post) for i in range(2)]
                rstd = fw.sb('grs' + tg, [128, 512], F32, stack=post)
                tmp = fw.sb('gtmp' + tg, [128, 512], F32, stack=post)
                slab = self.wload(w_in[:, 2048 + h * 256:2048 + h * 256 + 256], 256)
                for hf in range(2):
                    self.proj_fm(slab, hf * 128, lambda ps, tt, hf=hf: fw.act(zs[:, hf, tt * 512:(tt + 1) * 512], ps, AF.Silu))
                for tt in range(2):
                    sl = slice(tt * 512, (tt + 1) * 512)
                    self.rms_bcast([oacc[:, 0, sl], oacc[:, 1, sl]], 512, 1.0 / 256.0, rstd[:], sqp)
                    for hf in range(2):
                        fw.stt(tmp[:], oacc[:, hf, sl], self.sm['onorm_c'][:, o, hf:hf + 1], rstd[:], ALU.mult, ALU.mult)
                        fw.tt(self.mixT[:, h * 2 + hf, sl], tmp[:], zs[:, hf, sl], ALU.mult)
            fw.barrier()

    def mixer_nbr(self, o, w_in):
        fw = self.fw
        di = self.din
        with ExitStack() as ph:
            kT = fw.sb('nkT%d' % o, [128, 8, NT], BF16, stack=ph, split=NT)
            vtok = fw.sb('nvtok%d' % o, [128, 8, 1024], BF16, stack=ph, split=256)
            kctok = fw.sb('nkctok%d' % o, [128, 2, 1024], BF16, stack=ph)
            vc = fw.sb('nvc%d' % o, [128, 2, 1024], BF16, stack=ph)
            kcT = fw.sb('nkcT%d' % o, [128, 8, 256], BF16, stack=ph)
            kvst = [fw.sb('nkvst%d_%d' % (o, i), [128, 256], F32, stack=ph) for i in range(2)]
            qT = [fw.sb('nqT%d_%d' % (o, i), [128, NT], BF16, stack=ph) for i in range(1)] * 2
            zs = [fw.sb('nzs%d_%d' % (o, i), [128, NT], BF16, stack=ph) for i in range(1)] * 2
            biasT = [fw.sb('nbias%d_%d' % (o, i), [128, 7, 128], BF16, stack=ph) for i in range(2)]
            maskt = [fw.sb('nmask%d_%d' % (o, i), [128, 896], BF16, stack=ph) for i in range(3)]
            brevs = [fw.sb('nbrev%d_%d' % (o, i), [128, 7, 128], BF16, stack=ph) for i in range(2)]
            zero = fw.sb('nzero%d' % o, [120, 128], F32, stack=ph)
            mx = [fw.sb('nmx%d_%d' % (o, i), [128, 1], F32, stack=ph) for i in range(2)]
            nm = [fw.sb('nnm%d_%d' % (o, i), [128, 1], F32, stack=ph) for i in range(2)]
            rs = [fw.sb('nrs%d_%d' % (o, i), [128, 1], F32, stack=ph) for i in range(2)]
            es = [fw.sb('nes%d_%d' % (o, i), [128, 1], F32, stack=ph) for i in range(2)]
            E = [fw.sb('nE%d_%d' % (o, i), [128, 896], BF16, stack=ph) for i in range(2)]
            ET = [fw.sb('nET%d_%d' % (o, i), [128, 896], BF16, stack=ph) for i in range(2)]
            cD = PC_ODD
            fw.memset(zero[:], 0.0)
            fw.dma('sp', self.rpbp.rearrange("h r c -> (h r) c"), zero[:])
            fw.dma('sp', self.rpbp[:, :, 48:79], di['rpb'][o])
            for blk in range(2):
                fw.dma('pool', kctok[:, blk, :], di['kvn'][o, 0, blk * 128:(blk + 1) * 128].rearrange("t g d -> t (g d)"))
                fw.dma('pool', vc[:, blk, :], di['kvn'][o, 1, blk * 128:(blk + 1) * 128].rearrange("t g d -> t (g d)"))
            for g4 in range(2):
                pt = self.psum(512).bitcast(BF16)
                for gg in range(4):
                    g = g4 * 4 + gg
                    for blk in range(2):
                        fw.tr(pt[:, (gg * 2 + blk) * 128:(gg * 2 + blk + 1) * 128], kctok[:, blk, g * 128:(g + 1) * 128], self.ident_b[:])
                fw.cp(kcT[:, g4 * 4:(g4 + 1) * 4, :].rearrange("p g k -> p (g k)"), pt[:, 0:1024], 'dve')
            ci = [0]
            for s2 in range(4):
                slab = self.wload(w_in[:, cD + 1024 + s2 * 256:cD + 1024 + (s2 + 1) * 256], 256)
                for j2 in range(2):
                    g = s2 * 2 + j2
                    self.proj_fm(slab, j2 * 128, lambda ps, tt, g=g: fw.cp(kT[:, g, tt * 512:(tt + 1) * 512], ps, 'act'))
                for tb in range(8):
                    sg = kvst[ci[0] % 2]
                    ci[0] += 1
                    self.proj_tm(slab, 0, 256, tb, lambda ps, sg=sg: fw.cp(sg[:], ps, 'dve'))
                    fw.dma('sp', self.dout['nkn'][tb // 2, o, 0, (tb % 2) * 128:(tb % 2 + 1) * 128, s2 * 2:s2 * 2 + 2, :].rearrange("t g d -> t (g d)"), sg[:])
            for s2 in range(4):
                slab = self.wload(w_in[:, cD + 2048 + s2 * 256:cD + 2048 + (s2 + 1) * 256], 256)
                for tb in range(8):
                    sg = kvst[ci[0] % 2]
                    ci[0] += 1
                    self.proj_tm(slab, 0, 256, tb, lambda ps, sg=sg: fw.cp(sg[:], ps, 'dve'))
                    fw.cp(vtok[:, tb, s2 * 256:(s2 + 1) * 256], sg[:], 'act')
                    fw.dma('sp', self.dout['nkn'][tb // 2, o, 1, (tb % 2) * 128:(tb % 2 + 1) * 128, s2 * 2:s2 * 2 + 2, :].rearrange("t g d -> t (g d)"), sg[:])
            def bias_load(hh):
                brev = brevs[hh % 2]
                for dd in range(7):
                    dl = dd - 3
                    for rq in range(2):
                        off = hh * 15 * 128 + (2 * dl - rq + 7) * 128
                        src = bass.AP(self.rpbp.tensor, off, [[1, 64], [128, 2], [1, 64]])
                        fw.dma('pool', brev[rq * 64:(rq + 1) * 64, dd, :].rearrange("p (a b) -> p a b", a=2), src,
                               extra_ins=[self.rpbp])

            def bias_finish(hh):
                brev = brevs[hh % 2]
                bt_ = biasT[hh % 2]
                for (c0, c1) in ((0, 512), (512, 896)):
                    pj = self.psum(c1 - c0)
                    fw.mm(pj, self.jrev[:], brev[:].rearrange("p a b -> p (a b)")[:, c0:c1])
                    fw.ts(bt_[:].rearrange("p a b -> p (a b)")[:, c0:c1], pj, self.flag[:], ALU.mult, 1.0 / SCALE, ALU.mult)
            it = 0
            for h in range(8):
                w2 = h % 2
                if h % 2 == 0:
                    slabq = self.wload(w_in[:, cD + h * 128:cD + h * 128 + 256], 256)
                    slabz = self.wload(w_in[:, cD + 3072 + h * 128:cD + 3072 + h * 128 + 256], 256)
                self.proj_fm(slabq, w2 * 128, lambda ps, tt, w2=w2: fw.cp(qT[w2][:, tt * 512:(tt + 1) * 512], ps, 'act'))
                self.proj_fm(slabz, w2 * 128, lambda ps, tt, w2=w2: fw.act(zs[w2][:, tt * 512:(tt + 1) * 512], ps, AF.Silu))
                bt = biasT[w2]
                if h == 0:
                    bias_load(0)
                bias_finish(h)
                if h + 1 < 8:
                    bias_load(h + 1)
                units = []
                for j in range(8):
                    mk = maskt[it % 3]
                    slots = []
                    for si, m in enumerate(NBLK[j]):
                        slots.append((kT[:, h, m * 128:(m + 1) * 128], vtok[:, m, h * 128:(h + 1) * 128],
                                      mk[:, si * 128:(si + 1) * 128], bt[:, m - j + 3, :]))
                    for cb in range(2):
                        slots.append((kcT[:, h, cb * 128:(cb + 1) * 128], vc[:, cb, h * 128:(h + 1) * 128],
                                      mk[:, (5 + cb) * 128:(6 + cb) * 128], None))
                    w = it % 2
                    it += 1
                    nbl = NBLK[j]
                    m0, nw = nbl[0], len(nbl)
                    runs = []
                    c = 0
                    while c < nw:
                        n_ = min(nw - c, 4 - (c % 4))
                        runs.append((c * 128, n_ * 128, kT[:, h, (m0 + c) * 128:(m0 + c + n_) * 128], mk[:, c * 128:(c + n_) * 128],
                                     bt[:, m0 + c - j + 3:m0 + c + n_ - j + 3, :].rearrange("p a b -> p (a b)")))
                        c += n_
                    for cb in range(2):
                        cc = nw + cb
                        if cb == 0 and (cc % 4) != 3:
                            runs.append((cc * 128, 256, kcT[:, h, 0:256], mk[:, 5 * 128:7 * 128], None))
                            break
                        runs.append((cc * 128, 128, kcT[:, h, cb * 128:(cb + 1) * 128], mk[:, (5 + cb) * 128:(6 + cb) * 128], None))
                    pre = (lambda mk=mk, j=j: fw.dma('pool', mk[:], di['maskn'][j]))
                    units.append(self.attention('D', qT[w2][:, j * 128:(j + 1) * 128], slots, self.negbig[:],
                                                zs[w2][:, j * 128:(j + 1) * 128], self.mixT[:, h, j * 128:(j + 1) * 128],
                                                (mx[w], nm[w], rs[w], es[w], E[w], ET[w]), pre=pre, uidx=it, runs=runs))
                self.pipeline(units, 4)
            fw.barrier()


_PROG = {}


def _get_prog(**kw):
    key = tuple(sorted(kw.items()))
    if key not in _PROG:
        _PROG[key] = Prog(**kw)
    return _PROG[key]


def _prep_inputs(inp):
    f = lambda a: np.ascontiguousarray(np.asarray(a, dtype=np.float32))
    x_prompt, x_sample = f(inp['x_prompt']), f(inp['x_sample'])
    shared = {}
    shared['norm_w'] = f(inp['norm_w']).reshape(4, 16, 128).transpose(2, 0, 1)
    shared['w_ada'] = f(inp['w_ada'])
    shared['b_ada'] = f(inp['b_ada']).reshape(4, 48, 128).transpose(2, 0, 1)
    shared['w_in_even'] = f(inp['w_in_even'])
    shared['conv_a'] = f(inp['conv_a']).reshape(2, 5, 24, 128).transpose(3, 0, 2, 1)
    shared['a_log'] = np.repeat(f(inp['a_log_a']).transpose(1, 0, 2), 64, axis=0)
    shared['dt_bias'] = np.repeat(f(inp['dt_bias_a']).transpose(1, 0, 2), 64, axis=0)
    shared['onorm_a'] = f(inp['onorm_a']).T
    shared['sink_b'] = np.broadcast_to(f(inp['sink_b'])[None], (128, 2, 8))
    shared['w_out_even'] = f(inp['w_out_even'])
    shared['w_in_odd'] = f(inp['w_in_odd'])
    shared['w_glr'] = f(inp['w_glr_c'])
    shared['b_glr'] = f(inp['b_glr_c'])
    shared['onorm_c'] = f(inp['onorm_c']).reshape(2, 2, 128).transpose(2, 0, 1)
    shared['rpb'] = f(inp['rpb_d'])
    shared['w_out_odd'] = f(inp['w_out_odd'])
    shared['final_w'] = f(inp['final_norm_w']).reshape(16, 128).T
    shared = {k: np.ascontiguousarray(v) for k, v in shared.items()}
    consts = [_consts(0), _consts(1)]
    maps = []
    for core in range(8):
        role = 0 if core < 4 else 1
        m = dict(shared)
        m.update(consts[role])
        if role == 0:
            m['x'] = x_prompt[4 * core:4 * core + 4].reshape(NT, D)
            cond = f(inp['c_ctx'])
            b = 0
        else:
            b = core - 4
            m['x'] = x_sample[b]
            cond = f(inp['c'])[b]
        m['cond'] = np.ascontiguousarray(cond.reshape(16, 128).T)
        m['st_delta'] = f(inp['state_delta'])[b]
        m['kvw'] = f(inp['cache_kv_win'])[b]
        m['st_gla'] = f(inp['state_gla'])[b]
        m['kvn'] = f(inp['cache_kv_nbr'])[b]
        maps.append({k: np.ascontiguousarray(v) for k, v in m.items()})
    return maps


def _run(inp, cores=None, xover=None, **kw):
    prog = _get_prog(**kw)
    maps = _prep_inputs(inp)
    if xover is not None:
        for c in range(8):
            maps[c]['x'] = np.ascontiguousarray(xover[c], dtype=np.float32)
    if cores is not None:
        maps = [maps[c] for c in cores]
    res = run_bass_kernel_spmd(prog.nc, maps, core_ids=list(range(len(maps))))
    return res.results


def kernel(**inp):
    r = _run(inp)
    y_prompt = np.concatenate([r[c]['y'].reshape(4, 256, D) for c in range(4)], 0)
    y_sample = np.stack([r[c]['y'] for c in range(4, 8)], 0)
    nsd = np.concatenate([r[c]['nsd'] for c in range(4)], 0)
    nkw = np.concatenate([r[c]['nkw'] for c in range(4)], 0)
    nsg = np.concatenate([r[c]['nsg'] for c in range(4)], 0)
    nkn = np.concatenate([r[c]['nkn'] for c in range(4)], 0)
    return (y_prompt.astype(np.float32), y_sample.astype(np.float32), nsd.astype(np.float32),
            nkw.astype(np.float32), nsg.astype(np.float32), nkn.astype(np.float32))
```

```python
import numpy as np
from contextlib import ExitStack
import concourse.bass as bass
import concourse.mybir as mybir
from concourse.bass_utils import run_bass_kernel_spmd

F32 = mybir.dt.float32
BF16 = mybir.dt.bfloat16
AF = mybir.ActivationFunctionType
ALU = mybir.AluOpType
AX = mybir.AxisListType

COMPUTE = ('pe', 'act', 'dve', 'pool')
NDMASEM = 32

D = 2048
NT = 1024
EPS = 1e-6
NEG = -30000.0
PA_EVEN = 4128
P_EVEN = 6688
PC_ODD = 3104
P_ODD = 7200
SCALE = 128 ** -0.5


class FW:
    def __init__(self, nc, stack):
        self.nc = nc
        self.stack = stack
        self.ops = {e: [] for e in ('pe', 'act', 'dve', 'pool', 'sp')}
        self.sem = {}
        for e in COMPUTE:
            self.sem[e] = stack.enter_context(nc.semaphore('s_' + e))
        self.dsem = [stack.enter_context(nc.semaphore('d%d' % i)) for i in range(NDMASEM)]
        self.dexp = [0] * NDMASEM
        self.dnext = 0
        self.dnext_p = 0
        self.cnt = {e: 0 for e in COMPUTE}
        self.waited = {}
        self.res = {}
        self.split = {}
        self.uniq = 0
        self.marks = []
        self.ninst = 0

    def sb(self, name, shape, dt=F32, stack=None, split=None):
        t = (stack or self.stack).enter_context(self.nc.sbuf_tensor('t_' + name, list(shape), dt))
        if split:
            self.split['t_' + name] = split * (2 if dt == BF16 else 4)
        return t

    def ps(self, name, shape, dt=F32, split=None):
        t = self.stack.enter_context(self.nc.psum_tensor('t_' + name, list(shape), dt))
        if split:
            self.split['t_' + name] = split * (2 if dt == BF16 else 4)
        return t

    def keys(self, x):
        if isinstance(x, str):
            return [x]
        name = x.tensor.name
        if name in OUT_SHAPES:
            self.uniq += 1
            return ['%s@%d' % (name, self.uniq)]
        sp = self.split.get(name)
        if sp is None:
            return [name]
        dims = x.ap
        esz = 2 if x.dtype == BF16 else 4
        pstride = dims[0][0]
        off = x.offset % pstride if pstride > 0 else x.offset
        span = 0
        for st, n in dims[1:]:
            span += abs(st) * (n - 1)
        lo = (off * esz) // sp
        hi = ((off + span) * esz) // sp
        return ['%s#%d' % (name, r) for r in range(lo, hi + 1)]

    def _keys(self, lst):
        out = []
        for x in lst:
            if x is None:
                continue
            out.extend(self.keys(x))
        return out

    def _deps(self, reads, writes):
        deps = {}

        def add(d):
            if d is None:
                return
            src, val = d
            if deps.get(src, 0) < val:
                deps[src] = val
        for r in reads:
            st = self.res.get(r)
            if st:
                add(st['w'])
        for w in writes:
            st = self.res.get(w)
            if st:
                add(st['w'])
                for d in st['r']:
                    add(d)
        return deps

    def _update(self, me, reads, writes):
        for r in reads:
            st = self.res.setdefault(r, {'w': None, 'r': []})
            st['r'] = [d for d in st['r'] if d[0] != me[0]] + [me]
        for w in writes:
            self.res[w] = {'w': me, 'r': []}

    def _semof(self, src):
        if isinstance(src, str):
            return self.sem[src]
        return self.dsem[src]

    def _emit_waits(self, eng, deps):
        for src, val in deps.items():
            if src == eng and eng == 'pe':
                continue
            key = (eng, src)
            if self.waited.get(key, 0) >= val:
                continue
            self.waited[key] = val
            self.ops[eng].append(('wait', self._semof(src), val))

    def op(self, eng, fn, ins=(), outs=()):
        reads = self._keys(ins)
        writes = self._keys(outs)
        writes = writes + [k for k in reads if k.startswith('t_ps') and k not in writes]
        deps = self._deps(reads, writes)
        self._emit_waits(eng, deps)
        self.cnt[eng] += 1
        me = (eng, self.cnt[eng])
        self.ops[eng].append(('inst', fn, self.sem[eng], 1))
        self._update(me, reads, writes)
        self.ninst += 1

    def dma(self, q, out, in_, extra_ins=(), **kw):
        reads = self._keys([in_] + list(extra_ins))
        writes = self._keys([out])
        deps = self._deps(reads, writes)
        half = NDMASEM // 2
        if q == 'pool':
            s = half + self.dnext_p
            self.dnext_p = (self.dnext_p + 1) % (NDMASEM - half)
        else:
            s = self.dnext
            self.dnext = (self.dnext + 1) % half
        if self.dexp[s] > 0:
            deps[s] = max(deps.get(s, 0), self.dexp[s])
        self._emit_waits(q, deps)
        self.dexp[s] += 16
        me = (s, self.dexp[s])
        self.ops[q].append(('inst', lambda e: e.dma_start(out=out, in_=in_, **kw), self.dsem[s], 16))
        self._update(me, reads, writes)
        self.ninst += 1

    def mark(self, label):
        self.marks.append((label, dict(self.cnt)))

    def barrier(self):
        deps = {}
        for s in range(NDMASEM):
            if self.dexp[s] > 0:
                deps[s] = self.dexp[s]
        for e in COMPUTE:
            if self.cnt[e] > 0:
                deps[e] = self.cnt[e]
        for e in ('pe', 'act', 'dve', 'pool', 'sp'):
            d = dict(deps)
            d.pop(e, None)
            self._emit_waits(e, d)

    def replay(self):
        nc = self.nc
        engmap = {'pe': 'tensor', 'act': 'scalar', 'dve': 'vector', 'pool': 'gpsimd', 'sp': 'sync'}
        with nc.Block() as block:
            for e, attr in engmap.items():
                ops = self.ops[e]

                def body(engobj, ops=ops):
                    for o in ops:
                        if o[0] == 'wait':
                            engobj.wait_ge(o[1], o[2])
                        else:
                            o[1](engobj).then_inc(o[2], o[3])
                getattr(block, attr)(body)

    def mm(self, out, lhsT, rhs, start=True, stop=True):
        self.op('pe', lambda e: e.matmul(out, lhsT=lhsT, rhs=rhs, start=start, stop=stop), [lhsT, rhs], [out])

    def tr(self, out, in_, ident):
        self.op('pe', lambda e: e.transpose(out, in_, ident), [in_, ident], [out])

    def act(self, out, in_, func, bias=None, scale=None, accum_out=None):
        kw = {}
        ins = [in_]
        if bias is not None:
            kw['bias'] = bias
            if not isinstance(bias, (int, float)):
                ins.append(bias)
        if scale is not None:
            kw['scale'] = scale
            if not isinstance(scale, (int, float)):
                ins.append(scale)
        outs = [out]
        if accum_out is not None:
            kw['accum_out'] = accum_out
            outs.append(accum_out)
        self.op('act', lambda e: e.activation(out=out, in_=in_, func=func, **kw), ins, outs)

    def tt(self, out, in0, in1, op, eng='dve'):
        self.op(eng, lambda e: e.tensor_tensor(out=out, in0=in0, in1=in1, op=op), [in0, in1], [out])

    def ts(self, out, in0, s1, op0, s2=None, op1=None, eng='dve'):
        ins = [in0]
        for s in (s1, s2):
            if s is not None and not isinstance(s, (int, float)):
                ins.append(s)
        if op1 is None:
            self.op(eng, lambda e: e.tensor_scalar(out=out, in0=in0, scalar1=s1, scalar2=None, op0=op0), ins, [out])
        else:
            self.op(eng, lambda e: e.tensor_scalar(out=out, in0=in0, scalar1=s1, scalar2=s2, op0=op0, op1=op1), ins, [out])

    def stt(self, out, in0, scalar, in1, op0, op1):
        ins = [in0, in1]
        if not isinstance(scalar, (int, float)):
            ins.append(scalar)
        self.op('dve', lambda e: e.scalar_tensor_tensor(out=out, in0=in0, scalar=scalar, in1=in1, op0=op0, op1=op1), ins, [out])

    def cp(self, out, in_, eng='dve'):
        if eng == 'act':
            self.op('act', lambda e: e.copy(out=out, in_=in_), [in_], [out])
        else:
            self.op(eng, lambda e: e.tensor_copy(out=out, in_=in_), [in_], [out])

    def recip(self, out, in_):
        self.op('dve', lambda e: e.reciprocal(out=out, in_=in_), [in_], [out])

    def memset(self, ap, val, eng='dve'):
        self.op(eng, lambda e: e.memset(ap, val), [], [ap])

    def rmax(self, out, in_):
        self.op('dve', lambda e: e.reduce_max(out=out, in_=in_, axis=AX.X), [in_], [out])


def _consts(role):
    lat = (role == 1)
    c = {}
    c['ident'] = np.eye(128, dtype=np.float32)
    p = np.arange(128)
    dr = p // 64
    t = p % 64
    same = dr[:, None] == dr[None, :]
    before_eq = np.where(dr[:, None] == 0, t[:, None] <= t[None, :], t[:, None] >= t[None, :])
    c['cum'] = (same & before_eq).astype(np.float32)
    c['blk'] = same.astype(np.float32)
    c['dirf'] = np.repeat((dr == 0).astype(np.float32)[:, None], 128, 1)
    c['dirb'] = np.repeat((dr == 1).astype(np.float32)[:, None], 128, 1)
    s_before_c = np.where(dr[:, None] == 0, t[None, :] < t[:, None], t[None, :] > t[:, None])
    negB = np.where(same & s_before_c, 0.0, NEG).astype(np.float32)
    c['negB'] = negB
    c['negA'] = negB.T.copy()
    s_beq_c = np.where(dr[:, None] == 0, t[None, :] <= t[:, None], t[None, :] >= t[:, None])
    negBi = np.where(same & s_beq_c, 0.0, NEG).astype(np.float32)
    c['negAi'] = negBi.T.copy()
    c['m01Ai'] = (negBi.T == 0.0).astype(np.float32)
    R = np.zeros((128, 128), np.float32)
    for b0 in (0, 64):
        for i in range(32):
            R[b0 + 32 + i, b0 + i] = -1.0
            R[b0 + i, b0 + 32 + i] = 1.0
    c['rperm'] = R
    J = np.zeros((128, 128), np.float32)
    for rq in range(2):
        for cc in range(64):
            J[rq * 64 + 63 - cc, rq * 64 + cc] = 1.0
    c['jrev'] = J
    tok = np.arange(NT)
    cos = np.ones((128, NT), np.float32)
    sin = np.zeros((128, NT), np.float32)
    if lat:
        for d in range(128):
            i = d % 32
            freq = np.float32(10000.0) ** np.float32(-i / 32.0)
            pos = (tok // 64) if d < 64 else (tok % 64)
            ang = pos.astype(np.float32) * freq
            cos[d] = np.cos(ang)
            sin[d] = np.sin(ang)
    c['cos'] = cos
    c['sin'] = sin
    mw = np.full((8, 128, 5, 128), NEG, np.float32)
    q = np.arange(128)
    for j in range(8):
        if lat:
            for si, m in enumerate((j - 1, j, j + 1)):
                if 0 <= m < 8:
                    qpos = j * 128 + q[:, None]
                    kpos = m * 128 + q[None, :]
                    mw[j, :, si, :] = np.where(np.abs(qpos - kpos) <= 128, 0.0, NEG)
            mw[j, :, 3:5, :] = 0.0
        else:
            seq = j // 2
            for si, m in enumerate((j - 1, j, j + 1)):
                if 0 <= m < 8 and m // 2 == seq:
                    mw[j, :, si, :] = 0.0
    c['maskw'] = (mw / SCALE).reshape(8, 128, 640).astype(np.float32)
    mn = np.full((8, 128, 7, 128), NEG, np.float32)
    for j in range(8):
        blks = NBLK[j]
        for si, m in enumerate(blks):
            if lat:
                r = 2 * j + q // 64
                cq = q % 64
                rs = np.clip(r - 4, 0, 8)
                cst = np.clip(cq - 8, 0, 48)
                kr = 2 * m + q // 64
                kc = q % 64
                ok = ((kr[None, :] >= rs[:, None]) & (kr[None, :] < rs[:, None] + 8) &
                      (kc[None, :] >= cst[:, None]) & (kc[None, :] < cst[:, None] + 16))
                mn[j, :, si, :] = np.where(ok, 0.0, NEG)
            else:
                if m // 2 == j // 2:
                    mn[j, :, si, :] = 0.0
        if lat:
            mn[j, :, 5:7, :] = 0.0
    c['maskn'] = (mn / SCALE).reshape(8, 128, 896).astype(np.float32)
    c['flag'] = np.full((128, 1), 1.0 if lat else 0.0, np.float32)
    return c


NBLK = {0: [0, 1, 2, 3], 1: [0, 1, 2, 3], 2: [0, 1, 2, 3, 4], 3: [1, 2, 3, 4, 5], 4: [2, 3, 4, 5, 6],
        5: [3, 4, 5, 6, 7], 6: [4, 5, 6, 7], 7: [4, 5, 6, 7]}

CONST_SHAPES = {'ident': (128, 128), 'cum': (128, 128), 'blk': (128, 128), 'dirf': (128, 128), 'dirb': (128, 128),
                'negB': (128, 128), 'negA': (128, 128), 'negAi': (128, 128), 'm01Ai': (128, 128),
                'rperm': (128, 128), 'jrev': (128, 128), 'cos': (128, NT), 'sin': (128, NT), 'maskw': (8, 128, 640),
                'maskn': (8, 128, 896), 'flag': (128, 1)}

IN_SHAPES = {
    'x': (NT, D), 'cond': (128, 16), 'st_delta': (2, 2, 8, 128, 128), 'kvw': (2, 2, 256, 2, 128),
    'st_gla': (2, 2, 4, 128, 256), 'kvn': (2, 2, 256, 8, 128),
    'norm_w': (128, 4, 16), 'w_ada': (4, D, 3 * D), 'b_ada': (128, 4, 48),
    'w_in_even': (2, D, P_EVEN), 'conv_a': (128, 2, 24, 5), 'a_log': (128, 2, 8), 'dt_bias': (128, 2, 8),
    'onorm_a': (128, 2), 'sink_b': (128, 2, 8), 'w_out_even': (2, D, D),
    'w_in_odd': (2, D, P_ODD), 'w_glr': (2, 2, 16, 512), 'b_glr': (2, 2, 512), 'onorm_c': (128, 2, 2),
    'rpb': (2, 8, 15, 31), 'w_out_odd': (2, D, D), 'final_w': (128, 16),
}
OUT_SHAPES = {
    'y': (NT, D), 'nsd': (4, 2, 2, 8, 128, 128), 'nkw': (4, 2, 2, 256, 2, 128),
    'nsg': (4, 2, 2, 4, 128, 256), 'nkn': (4, 2, 2, 256, 8, 128),
}


class Prog:
    def __init__(self, layers=(0, 1, 2, 3), final_norm=True, mixers='ABCD'):
        self.layers = tuple(layers)
        self.final_norm = final_norm
        self.mixers = mixers
        self.nc = bass.Bass("TRN2", target_bir_lowering=False)
        nc = self.nc
        self.din = {}
        for k, shp in list(IN_SHAPES.items()) + list(CONST_SHAPES.items()):
            self.din[k] = nc.dram_tensor(k, list(shp), F32, kind="ExternalInput").ap()
        self.dout = {}
        for k, shp in OUT_SHAPES.items():
            self.dout[k] = nc.dram_tensor(k, list(shp), F32, kind="ExternalOutput").ap()
        self.rpbp = nc.dram_tensor("rpbp", [8, 15, 128], F32, kind="Internal").ap()
        self.wq = 0
        with ExitStack() as st:
            self.st = st
            self.fw = FW(nc, st)
            self.build()
            self.fw.barrier()
            self.fw.replay()

    def psum(self, ncols=512):
        skip = self.psum_skip
        if ncols <= 512:
            while True:
                i = self.pi
                self.pi = (self.pi + 1) % 8
                if i not in skip:
                    break
            return self.pst[i // 2][:, (i % 2) * 512:(i % 2) * 512 + ncols]
        while True:
            if self.pi % 2:
                self.pi = (self.pi + 1) % 8
            i = self.pi
            self.pi = (self.pi + 2) % 8
            if i not in skip and (i + 1) not in skip:
                break
        return self.pst[i // 2][:, 0:ncols]

    def bg_step(self, n=1):
        for _ in range(n):
            if self.bg is None:
                return
            try:
                next(self.bg)
            except StopIteration:
                self.bg = None

    def wload(self, src, ncols):
        slab = self.wslab[self.wq]
        self.wq = (self.wq + 1) % len(self.wslab)
        self.fw.dma('pool', slab[:, :, 0:ncols], src.rearrange("(kc p) c -> p kc c", p=128))
        return slab

    def wload_rows(self, src, k0, nk, ncols):
        slab = self.wslab[self.wq]
        self.wq = (self.wq + 1) % len(self.wslab)
        self.fw.dma('pool', slab[:, 0:nk, 0:ncols], src.rearrange("(kc p) c -> p kc c", p=128)[:, k0:k0 + nk, :])
        return slab

    def proj_fm(self, slab, c0, evac):
        fw = self.fw
        for tt in range(2):
            ps = self.psum(512)
            for kc in range(16):
                fw.mm(ps, slab[:, kc, c0:c0 + 128], self.hT[:, kc, tt * 512:(tt + 1) * 512], start=(kc == 0), stop=(kc == 15))
            evac(ps, tt)

    def proj_tm(self, slab, c0, ncols, tb, evac):
        fw = self.fw
        ps = self.psum(ncols)
        for kc in range(16):
            fw.mm(ps, self.hT[:, kc, tb * 128:(tb + 1) * 128], slab[:, kc, c0:c0 + ncols], start=(kc == 0), stop=(kc == 15))
        evac(ps)

    def outproj(self, w_out, k0):
        fw = self.fw
        for oc2 in range(8):
            slab = self.wload_rows(w_out[:, oc2 * 256:(oc2 + 1) * 256], k0, 8, 256)
            for j in range(2):
                oc = oc2 * 2 + j
                for tt in range(2):
                    ps = self.psum(512)
                    for kc in range(8):
                        fw.mm(ps, slab[:, kc, j * 128:(j + 1) * 128], self.mixT[:, kc, tt * 512:(tt + 1) * 512],
                              start=(kc == 0), stop=(kc == 7))
                    xs = self.xT[:, oc, tt * 512:(tt + 1) * 512]
                    fw.stt(xs, ps, self.gate[:, oc:oc + 1], xs, ALU.mult, ALU.add)

    def rms_bcast(self, srcs, n, inv_n, dst_rstd, sqpool):
        fw = self.fw
        ps = self.psum(n)
        for i, s in enumerate(srcs):
            sq = sqpool[i % len(sqpool)][:, 0:n]
            fw.act(sq, s, AF.Square)
            fw.mm(ps, self.ones_b[:], sq, start=(i == 0), stop=(i == len(srcs) - 1))
        fw.act(dst_rstd, ps, AF.Ln, bias=self.epscol[:], scale=inv_n)
        fw.act(dst_rstd, dst_rstd, AF.Exp, scale=-0.5)

    def build(self):
        fw = self.fw
        nc = self.nc
        di = self.din
        self.pi = 0
        self.psum_skip = set()
        self.pst = [fw.ps('ps%d' % i, [128, 1024], F32, split=512) for i in range(4)]
        self.xT = fw.sb('xT', [128, 16, NT], F32, split=512)
        self.hT = fw.sb('hT', [128, 16, NT], BF16, split=512)
        self.mixT = fw.sb('mixT', [128, 8, NT], BF16, split=512)
        self.wslab = [fw.sb('wslab%d' % i, [128, 16, 256], BF16) for i in range(2)]
        cf = {}
        for k in ('ident', 'cum', 'blk', 'dirf', 'dirb', 'negB', 'negA', 'negAi'):
            cf[k] = fw.sb('c_' + k, [128, 128], F32)
            fw.dma('sp', cf[k][:], di[k])
        self.cf = cf
        self.ident_b = fw.sb('ident_b', [128, 128], BF16)
        fw.dma('pool', self.ident_b[:], di['ident'])
        self.m01 = fw.sb('m01', [128, 128], BF16)
        fw.dma('pool', self.m01[:], di['m01Ai'])
        self.rperm = fw.sb('rperm', [128, 128], BF16)
        fw.dma('pool', self.rperm[:], di['rperm'])
        self.jrev = fw.sb('jrev', [128, 128], BF16)
        fw.dma('pool', self.jrev[:], di['jrev'])
        self.ones_b = fw.sb('ones_b', [128, 128], BF16)
        fw.memset(self.ones_b[:], 1.0)
        self.ones_f = fw.sb('ones_f', [128, 128], F32)
        fw.memset(self.ones_f[:], 1.0)
        self.epscol = fw.sb('epscol', [128, 1], F32)
        fw.memset(self.epscol[:], EPS)
        self.flag = fw.sb('flag', [128, 1], F32)
        fw.dma('sp', self.flag[:], di['flag'])
        sm = {}
        for k in ('norm_w', 'b_ada', 'conv_a', 'a_log', 'dt_bias', 'onorm_a', 'sink_b', 'onorm_c', 'final_w', 'cond'):
            shp = IN_SHAPES[k]
            sm[k] = fw.sb('p_' + k, list(shp), F32)
            fw.dma('sp', sm[k][:], di[k])
        self.sm = sm
        self.scond = fw.sb('scond', [128, 16], BF16)
        fw.act(self.scond[:], sm['cond'][:], AF.Silu)
        self.modall = fw.sb('modall', [128, 4, 48], F32)
        self.mod_ready = set()
        self.bg = None
        self.Aw = fw.sb('Aw', [128, 16], F32)
        self.negsk = fw.sb('negsk', [128, 2, 8], F32)
        fw.ts(self.negsk[:], sm['sink_b'][:], -1.0, ALU.mult)
        self.onescol = fw.sb('onescol', [128, 1], F32)
        fw.memset(self.onescol[:], 1.0)
        self.dirsel = fw.sb('dirsel', [128, 2], F32)
        fw.cp(self.dirsel[:, 0:1], cf['dirf'][:, 0:1])
        fw.cp(self.dirsel[:, 1:2], cf['dirb'][:, 0:1])
        self.negbig = fw.sb('negbig', [128, 1], F32)
        fw.memset(self.negbig[:], 30000.0)

        with ExitStack() as ph:
            stage = [fw.sb('stage%d' % i, [128, D], F32, stack=ph) for i in range(2)]
            for tb in range(8):
                sg = stage[tb % 2]
                fw.dma('sp', sg[:], di['x'][tb * 128:(tb + 1) * 128, :])
                for c4 in range(4):
                    ps = self.psum(512)
                    for j in range(4):
                        c = c4 * 4 + j
                        fw.tr(ps[:, j * 128:(j + 1) * 128], sg[:, c * 128:(c + 1) * 128], cf['ident'][:])
                    dst = self.xT[:, c4 * 4:(c4 + 1) * 4, tb * 128:(tb + 1) * 128]
                    src = ps.rearrange("p (a b) -> p a b", a=4)
                    if c4 % 2 == 0:
                        fw.cp(dst, src, 'dve')
                    else:
                        fw.cp(dst, src, 'act')
            fw.barrier()

        for li in self.layers:
            self.fw.mark('layer%d' % li)
            self.layer(li)
        self.fw.mark('final')

        with ExitStack() as ph:
            stage = [fw.sb('ostage%d' % i, [128, D], F32, stack=ph) for i in range(2)]
            sqp = [fw.sb('fsq%d' % i, [128, 512], BF16, stack=ph) for i in range(2)]
            rstd = fw.sb('frstd', [128, 512], F32, stack=ph)
            for tt in range(2):
                sl = slice(tt * 512, (tt + 1) * 512)
                if self.final_norm:
                    self.rms_bcast([self.xT[:, c, sl] for c in range(16)], 512, 1.0 / D, rstd[:], sqp)
                    for c in range(16):
                        fw.stt(self.xT[:, c, sl], self.xT[:, c, sl], sm['final_w'][:, c:c + 1], rstd[:], ALU.mult, ALU.mult)
                for t4 in range(4):
                    tb = tt * 4 + t4
                    sg = stage[tb % 2]
                    for c4 in range(4):
                        ps = self.psum(512)
                        for j in range(4):
                            c = c4 * 4 + j
                            fw.tr(ps[:, j * 128:(j + 1) * 128], self.xT[:, c, tb * 128:(tb + 1) * 128], cf['ident'][:])
                        if c4 % 2 == 0:
                            fw.cp(sg[:, c4 * 512:(c4 + 1) * 512], ps, 'dve')
                        else:
                            fw.cp(sg[:, c4 * 512:(c4 + 1) * 512], ps, 'act')
                    fw.dma('sp', self.dout['y'][tb * 128:(tb + 1) * 128, :], sg[:])
            fw.barrier()

    def mod_gen(self, layers, ring, psm):
        fw = self.fw
        di = self.din
        for li in layers:
            slabs = {}

            def issue(sidx):
                slab = ring[sidx % len(ring)]
                fw.dma('pool', slab[:, :, 0:256], di['w_ada'][li][:, sidx * 256:(sidx + 1) * 256].rearrange("(kc p) c -> p kc c", p=128))
                slabs[sidx] = slab
            issue(0)
            for sidx in range(24):
                if sidx + 1 < 24:
                    issue(sidx + 1)
                yield
                slab = slabs.pop(sidx)
                for j in range(2):
                    col = sidx * 2 + j
                    pc = psm[:, (col % 2):(col % 2) + 1] if psm.shape[1] < 48 else psm[:, col:col + 1]
                    for kc in range(16):
                        fw.mm(pc, slab[:, kc, j * 128:(j + 1) * 128], self.scond[:, kc:kc + 1], start=(kc == 0), stop=(kc == 15))
                    fw.act(self.modall[:, li, col:col + 1], pc, AF.Identity, bias=self.sm['b_ada'][:, li, col:col + 1])
                yield
            self.mod_ready.add(li)

    def layer(self, li):
        fw = self.fw
        di = self.din
        sm = self.sm
        self.mod = self.modall[:, li, :]
        self.gate = self.mod[:, 32:48]
        if li not in self.mod_ready:
            for _ in self.mod_gen([li], self.wslab, self.psum(48)):
                pass
        fw.stt(self.Aw[:], self.mod[:, 16:32], 1.0, sm['norm_w'][:, li, :], ALU.add, ALU.mult)
        fw.mark('L%d.norm' % li)
        with ExitStack() as ph:
            sqp = [fw.sb('nsq%d_%d' % (li, i), [128, 512], BF16, stack=ph) for i in range(3)]
            rstd = fw.sb('nrstd%d' % li, [128, 512], F32, stack=ph)
            tmp = [fw.sb('ntmp%d_%d' % (li, i), [128, 512], F32, stack=ph) for i in range(3)]
            for tt in range(2):
                sl = slice(tt * 512, (tt + 1) * 512)
                self.rms_bcast([self.xT[:, c, sl] for c in range(16)], 512, 1.0 / D, rstd[:], sqp)
                for c in range(16):
                    t = tmp[c % 3]
                    fw.tt(t[:], self.xT[:, c, sl], rstd[:], ALU.mult)
                    fw.act(self.hT[:, c, sl], t[:], AF.Identity, bias=self.mod[:, c:c + 1], scale=self.Aw[:, c:c + 1])
            fw.barrier()
        if li % 2 == 0:
            e = li // 2
            w_in = di['w_in_even'][e]
            w_out = di['w_out_even'][e]
            fw.mark('L%d.mixA' % li)
            if 'A' in self.mixers:
                self.mixer_delta(e, w_in)
                fw.mark('L%d.outA' % li)
                self.outproj(w_out, 0)
            fw.mark('L%d.mixB' % li)
            if 'B' in self.mixers:
                self.mixer_win(e, w_in)
                fw.mark('L%d.outB' % li)
                self.outproj(w_out, 8)
        else:
            o = li // 2
            w_in = di['w_in_odd'][o]
            w_out = di['w_out_odd'][o]
            fw.mark('L%d.mixC' % li)
            if 'C' in self.mixers:
                self.mixer_gla(o, w_in)
                fw.mark('L%d.outC' % li)
                self.outproj(w_out, 0)
            fw.mark('L%d.mixD' % li)
            if 'D' in self.mixers:
                self.mixer_nbr(o, w_in)
                fw.mark('L%d.outD' % li)
                self.outproj(w_out, 8)
        fw.barrier()

    def pipeline(self, gens, depth=3):
        it = iter(gens)
        active = []
        done = False
        while True:
            if not done and len(active) < depth:
                try:
                    active.append(next(it))
                except StopIteration:
                    done = True
            if self.bg is not None:
                try:
                    next(self.bg)
                except StopIteration:
                    self.bg = None
            for g in reversed(list(active)):
                try:
                    next(g)
                except StopIteration:
                    active.remove(g)
            if done and not active:
                break

    def attention(self, pfx, qT, slots, sink_neg, zs, mix_dst, work, pre=None, uidx=0, runs=None):
        fw = self.fw
        if pre is not None:
            pre()
        ns = len(slots)
        W = ns * 128
        S = self.pst[uidx % 3][:, 0:W]
        if runs is None:
            for i, (kT, v, mask, bias) in enumerate(slots):
                cs = S[:, i * 128:(i + 1) * 128]
                fw.mm(cs, qT, kT, start=True, stop=False)
                fw.mm(cs, self.ident_b[:], mask, start=False, stop=(bias is None))
                if bias is not None:
                    fw.mm(cs, self.ident_b[:], bias, start=False, stop=True)
        else:
            for (c0, nc_, kTr_, mk_, bs_) in runs:
                cs = S[:, c0:c0 + nc_]
                fw.mm(cs, qT, kTr_, start=True, stop=False)
                fw.mm(cs, self.ident_b[:], mk_, start=False, stop=(bs_ is None))
                if bs_ is not None:
                    fw.mm(cs, self.ident_b[:], bs_, start=False, stop=True)
        yield
        mx, nm, rs, es, E, ET = work
        fw.rmax(mx[:], S)
        fw.ts(nm[:], mx[:], -SCALE, ALU.mult, sink_neg, ALU.min)
        fw.act(E[:, 0:W], S, AF.Exp, bias=nm[:], scale=SCALE, accum_out=rs[:])
        fw.act(es[:], sink_neg, AF.Exp, bias=nm[:], scale=-1.0)
        fw.tt(rs[:], rs[:], es[:], ALU.add)
        fw.recip(rs[:], rs[:])
        fw.ts(E[:, 0:W], E[:, 0:W], rs[:], ALU.mult)
        yield
        PT = self.pst[3][:, (uidx % 2) * 512:(uidx % 2) * 512 + 512]
        PTb = PT.bitcast(BF16)
        for i in range(ns):
            fw.tr(PTb[:, i * 128:(i + 1) * 128], E[:, i * 128:(i + 1) * 128], self.ident_b[:])
        fw.cp(ET[:, 0:W], PTb[:, 0:W], 'act')
        yield
        O = self.pst[uidx % 3][:, 896:1024]
        for i, (kT, v, mask, bias) in enumerate(slots):
            fw.mm(O, v, ET[:, i * 128:(i + 1) * 128], start=(i == 0), stop=(i == ns - 1))
        fw.tt(mix_dst, O, zs, ALU.mult)

    def mixer_win(self, e, w_in):
        fw = self.fw
        di = self.din
        with ExitStack() as ph:
            cos = fw.sb('cos%d' % e, [128, NT], F32, stack=ph)
            sin = fw.sb('sin%d' % e, [128, NT], F32, stack=ph)
            fw.dma('sp', cos[:], di['cos'])
            fw.dma('sp', sin[:], di['sin'])
            kTr = fw.sb('kTr%d' % e, [128, 2, NT], BF16, stack=ph)
            vtok = fw.sb('vtok%d' % e, [128, 8, 256], BF16, stack=ph)
            kcT = fw.sb('kcT%d' % e, [128, 2, 256], BF16, stack=ph)
            vc = fw.sb('vc%d' % e, [128, 2, 256], BF16, stack=ph)
            kctok = fw.sb('kctok%d' % e, [128, 2, 256], BF16, stack=ph)
            kvst = [fw.sb('kvst%d_%d' % (e, i), [128, 512], F32, stack=ph) for i in range(2)]
            q0 = [fw.sb('q0_%d_%d' % (e, i), [128, 512], BF16, stack=ph) for i in range(2)]
            t1 = [fw.sb('rt1_%d_%d' % (e, i), [128, 512], F32, stack=ph) for i in range(1)] * 2
            t2 = [fw.sb('rt2_%d_%d' % (e, i), [128, 512], F32, stack=ph) for i in range(1)] * 2
            qTr = fw.sb('qTr%d' % e, [128, 4, NT], BF16, stack=ph, split=128)
            zs = fw.sb('zsB%d' % e, [128, 4, NT], BF16, stack=ph, split=128)
            maskt = [fw.sb('maskw%d_%d' % (e, i), [128, 640], BF16, stack=ph) for i in range(2)]
            mx = [fw.sb('amx%d_%d' % (e, i), [128, 1], F32, stack=ph) for i in range(2)]
            nm = [fw.sb('anm%d_%d' % (e, i), [128, 1], F32, stack=ph) for i in range(2)]
            rs = [fw.sb('ars%d_%d' % (e, i), [128, 1], F32, stack=ph) for i in range(2)]
            es = [fw.sb('aes%d_%d' % (e, i), [128, 1], F32, stack=ph) for i in range(2)]
            E = [fw.sb('aE%d_%d' % (e, i), [128, 640], BF16, stack=ph) for i in range(2)]
            ET = [fw.sb('aET%d_%d' % (e, i), [128, 640], BF16, stack=ph) for i in range(2)]
            cB = PA_EVEN
            rc = [0]
            nxt_layers = [l for l in ((1, 2) if e == 0 else (3,)) if l in self.layers and l not in self.mod_ready]
            if nxt_layers:
                mring = [fw.sb('mring%d_%d' % (e, i), [128, 16, 256], BF16, stack=ph) for i in range(2)]
                self.bg = self.mod_gen(nxt_layers, mring, self.pst[0][:, 768:770])
                self.psum_skip = {1}

            def rope_evac(dst_fn):
                def ev(ps, tt):
                    i = rc[0] % 2
                    rc[0] += 1
                    sl = slice(tt * 512, (tt + 1) * 512)
                    fw.cp(q0[i][:], ps, 'act')
                    rot = self.psum(512)
                    fw.mm(rot, self.rperm[:], q0[i][:])
                    fw.tt(t1[i][:], q0[i][:], cos[:, sl], ALU.mult)
                    fw.tt(t2[i][:], rot, sin[:, sl], ALU.mult)
                    fw.tt(dst_fn(sl), t1[i][:], t2[i][:], ALU.add)
                return ev
            slab = self.wload(w_in[:, cB + 1024:cB + 1280], 256)
            for g in range(2):
                self.proj_fm(slab, g * 128, rope_evac(lambda sl, g=g: kTr[:, g, sl]))
                self.bg_step()
            slabv = self.wload(w_in[:, cB + 1280:cB + 1536], 256)
            for tb in range(8):
                sg = kvst[tb % 2]
                self.proj_tm(slab, 0, 256, tb, lambda ps, sg=sg: fw.cp(sg[:, 0:256], ps, 'act'))
                self.proj_tm(slabv, 0, 256, tb, lambda ps, sg=sg: fw.cp(sg[:, 256:512], ps, 'dve'))
                self.bg_step()
                fw.cp(vtok[:, tb, :], sg[:, 256:512], 'act')
                seq, half = tb // 2, tb % 2
                for kv in range(2):
                    fw.dma('sp', self.dout['nkw'][seq, e, kv, half * 128:(half + 1) * 128].rearrange("t g d -> t (g d)"),
                           sg[:, kv * 256:(kv + 1) * 256])
            for blk in range(2):
                fw.dma('pool', kctok[:, blk, :], di['kvw'][e, 0, blk * 128:(blk + 1) * 128].rearrange("t g d -> t (g d)"))
                fw.dma('pool', vc[:, blk, :], di['kvw'][e, 1, blk * 128:(blk + 1) * 128].rearrange("t g d -> t (g d)"))
            pt = self.psum(512).bitcast(BF16)
            for blk in range(2):
                for g in range(2):
                    fw.tr(pt[:, (g * 2 + blk) * 128:(g * 2 + blk + 1) * 128], kctok[:, blk, g * 128:(g + 1) * 128], self.ident_b[:])
            fw.cp(kcT[:].rearrange("p g k -> p (g k)"), pt[:, 0:512], 'dve')
            for hg in range(2):
                for hh2 in range(2):
                    slabq = self.wload(w_in[:, cB + hg * 512 + hh2 * 256:cB + hg * 512 + (hh2 + 1) * 256], 256)
                    slabz = self.wload(w_in[:, cB + 1536 + hg * 512 + hh2 * 256:cB + 1536 + hg * 512 + (hh2 + 1) * 256], 256)
                    for j2 in range(2):
                        hh = hh2 * 2 + j2
                        self.proj_fm(slabq, j2 * 128, rope_evac(lambda sl, hh=hh: qTr[:, hh, sl]))
                        self.proj_fm(slabz, j2 * 128, lambda ps, tt, hh=hh: fw.act(zs[:, hh, tt * 512:(tt + 1) * 512], ps, AF.Silu))
                        self.bg_step(2)
                it = 0
                units = []
                for j in range(8):
                    mk = maskt[j % 2]
                    for hh in range(4):
                        h = hg * 4 + hh
                        slots = []
                        for si, m in enumerate((j - 1, j, j + 1)):
                            if 0 <= m < 8:
                                slots.append((kTr[:, hg, m * 128:(m + 1) * 128], vtok[:, m, hg * 128:(hg + 1) * 128],
                                              mk[:, si * 128:(si + 1) * 128], None))
                        for cb in range(2):
                            slots.append((kcT[:, hg, cb * 128:(cb + 1) * 128], vc[:, cb, hg * 128:(hg + 1) * 128],
                                          mk[:, (3 + cb) * 128:(4 + cb) * 128], None))
                        w = it % 2
                        it += 1
                        pre = (lambda mk=mk, j=j: fw.dma('pool', mk[:], di['maskw'][j])) if hh == 0 else None
                        units.append(self.attention('B', qTr[:, hh, j * 128:(j + 1) * 128], slots, self.negsk[:, e, h:h + 1],
                                                    zs[:, hh, j * 128:(j + 1) * 128], self.mixT[:, h, j * 128:(j + 1) * 128],
                                                    (mx[w], nm[w], rs[w], es[w], E[w], ET[w]), pre=pre, uidx=it))
                self.pipeline(units, 4)
            if self.bg is not None:
                for _ in self.bg:
                    pass
                self.bg = None
            self.psum_skip = set()
            fw.barrier()


    def mixer_delta(self, e, w_in):
        fw = self.fw
        di = self.din
        cf = self.cf
        sm = self.sm
        with ExitStack() as ph:
            xb = fw.sb('dxb%d' % e, [128, 8, 16], F32, stack=ph)
            xg = fw.sb('dxg%d' % e, [128, 8, 16], F32, stack=ph)
            beta = fw.sb('dbeta%d' % e, [128, 8, 16], F32, stack=ph)
            lnb = fw.sb('dlnb%d' % e, [128, 8, 16], F32, stack=ph)
            g = fw.sb('dg%d' % e, [128, 8, 16], F32, stack=ph)
            nal = fw.sb('dnal%d' % e, [128, 8], F32, stack=ph)
            slab = self.wload(w_in[:, 4096:4128], 32)
            ps = self.psum(512)
            for n in range(16):
                for d in range(2):
                    for kc in range(16):
                        fw.mm(ps[d * 64:(d + 1) * 64, n * 32:(n + 1) * 32], self.hT[:, kc, n * 64:(n + 1) * 64], slab[:, kc, 0:32],
                              start=(kc == 0), stop=(kc == 15))
            p3 = ps.rearrange("p (n c) -> p n c", n=16)
            for d in range(2):
                R = slice(d * 64, (d + 1) * 64)
                fw.cp(xb[R].rearrange("p h n -> p n h"), p3[R, :, d * 8:(d + 1) * 8], 'dve')
                fw.cp(xg[R].rearrange("p h n -> p n h"), p3[R, :, 16 + d * 8:16 + (d + 1) * 8], 'dve')
            fw.act(beta[:], xb[:], AF.Exp, scale=-1.0)
            fw.ts(beta[:], beta[:], 1.0, ALU.add)
            fw.act(lnb[:], beta[:], AF.Ln)
            fw.ts(lnb[:], lnb[:], -1.0, ALU.mult)
            fw.recip(beta[:], beta[:])
            fw.tt(xg[:], xg[:], sm['dt_bias'][:, e, :].unsqueeze(2).to_broadcast([128, 8, 16]), ALU.add)
            fw.act(xg[:], xg[:], AF.Exp)
            fw.act(xg[:], xg[:], AF.Ln, bias=self.onescol[:])
            fw.act(nal[:], sm['a_log'][:, e, :], AF.Exp)
            fw.ts(nal[:], nal[:], -1.0, ALU.mult)
            fw.tt(g[:], xg[:], nal[:].unsqueeze(2).to_broadcast([128, 8, 16]), ALU.mult)
            for h in range(8):
                self.delta_head(e, h, w_in, beta, lnb, g)
            fw.barrier()

    def delta_head(self, e, h, w_in, beta, lnb, g):
        fw = self.fw
        di = self.din
        cf = self.cf
        sm = self.sm
        tg = '%d_%d' % (e, h)
        if h <= 1: self.fw.mark('D%d.h%d.start' % (e, h))
        bc_last = lambda ap: ap.unsqueeze(2).to_broadcast([128, 4, 128])
        bc_n = lambda ap: ap.unsqueeze(1).to_broadcast([128, 4, 128])
        v3 = lambda ap: ap.rearrange("p (n c) -> p n c", n=4)
        with ExitStack() as hd:
            u = fw.sb('du' + tg, [128, 2048], BF16, stack=hd, split=128)
            wT = fw.sb('dw' + tg, [128, 2048], BF16, stack=hd, split=128)
            attnT = fw.sb('dat' + tg, [128, 2048], BF16, stack=hd, split=128)
            qgT = fw.sb('dqg' + tg, [128, 2048], BF16, stack=hd, split=128)
            kdec = fw.sb('dkd' + tg, [128, 2048], BF16, stack=hd, split=128)
            egl = fw.sb('degl' + tg, [128, 2, 16], F32, stack=hd)
            oacc = fw.sb('doacc' + tg, [128, NT], F32, stack=hd, split=64)
            S2 = [[fw.sb('dS%s_%d_%d' % (tg, d, i), [128, 128], F32, stack=hd) for i in range(2)] for d in range(2)]
            Sb2 = [[fw.sb('dSb%s_%d_%d' % (tg, d, i), [128, 128], BF16, stack=hd) for i in range(2)] for d in range(2)]
            vnw = [[fw.sb('dvn%s_%d_%d' % (tg, d, i), [128, 128], BF16, stack=hd) for i in range(2)] for d in range(2)]
            fw.memset(oacc[:], 0.0, 'pool')
            for d in range(2):
                fw.dma('sp', S2[d][0][:], di['st_delta'][e, d, h])
                fw.ts(S2[d][0][:], S2[d][0][:], self.flag[:], ALU.mult)
                fw.cp(Sb2[d][0][:], S2[d][0][:], 'act')
            with ExitStack() as it:
                qkv = [fw.sb('dqkv%s_%d' % (tg, i), [128, NT], BF16, stack=it) for i in range(3)]
                gcc = fw.sb('dgcc' + tg, [128, 16], F32, stack=it)
                gbc = fw.sb('dgbc' + tg, [128, 16], F32, stack=it)
                bge = fw.sb('dbge' + tg, [128, 16], F32, stack=it)
                edk = fw.sb('dedk' + tg, [128, 16], F32, stack=it)
                with ExitStack() as cv:
                    cvs = []
                    for ci in range(2):
                        xpad = fw.sb('dxp%s_%d' % (tg, ci), [128, 4, 260], BF16, stack=cv)
                        acc = fw.sb('dacc%s_%d' % (tg, ci), [128, NT], F32, stack=cv)
                        rn = fw.sb('drn%s_%d' % (tg, ci), [128, 512], F32, stack=cv)
                        sqp = [fw.sb('dsq%s_%d' % (tg, ci), [128, 512], BF16, stack=cv)]
                        dk = fw.sb('ddk%s_%d' % (tg, ci), [128, 5, 128], BF16, stack=cv)
                        fw.memset(xpad[:], 0.0, 'pool' if ci else 'dve')
                        cvs.append((xpad, acc, rn, sqp, dk))

                    def front_q(qi, cvset):
                        xpad, acc, rn, sqp, dk = cvset
                        col = qi * 1024 + h * 128
                        slab = self.wload(w_in[:, col:col + 128], 128)
                        cw = sm['conv_a'][:, e, qi * 8 + h, :]
                        for k in range(5):
                            fw.act(dk[:, k, :], self.ident_b[:], AF.Copy, scale=cw[:, k:k + 1])
                        self.proj_fm(slab, 0, lambda ps, tt: fw.cp(xpad[:, tt * 2:tt * 2 + 2, 2:258], ps.rearrange("p (s t) -> p s t", s=2), 'act'))
                        yield
                        fw.ts(xpad[:, 1:4, 0:2], xpad[:, 0:3, 256:258], self.flag[:], ALU.mult)
                        fw.ts(xpad[:, 0:3, 258:260], xpad[:, 1:4, 2:4], self.flag[:], ALU.mult)
                        pcv = self.psum(1024)
                        for sg in range(4):
                            for k in range(5):
                                fw.mm(pcv[:, sg * 256:(sg + 1) * 256], dk[:, k, :], xpad[:, sg, k:k + 256], start=(k == 0), stop=(k == 4))
                        if qi < 2:
                            fw.act(acc[:], pcv, AF.Silu)
                        else:
                            fw.act(qkv[2][:], pcv, AF.Silu)
                        yield
                        if qi < 2:
                            for tt in range(2):
                                sl = slice(tt * 512, (tt + 1) * 512)
                                self.rms_bcast([acc[:, sl]], 512, 1.0, rn[:], sqp)
                                fw.stt(qkv[qi][:, sl], acc[:, sl], SCALE if qi == 0 else 1.0, rn[:], ALU.mult, ALU.mult)
                    self.pipeline([front_q(0, cvs[0]), front_q(1, cvs[1]), front_q(2, cvs[0])], 2)
                    fw.barrier()
                if h == 0: fw.mark('D%d.h0.batches' % e)
                TP = []
                for pp in range(2):
                    P = [fw.sb('dT%s_%d_%d' % (tg, pp, i), [128, 512], F32, stack=it) for i in range(6)]
                    P += [fw.sb('dB%s_%d_%d' % (tg, pp, i), [128, 512], BF16, stack=it) for i in range(3)]
                    TP.append(P)
                qn, kn, vn = qkv
                gh = g[:, h, :]
                pc = self.psum(64)
                fw.mm(pc[:, 0:16], cf['cum'][:], gh)
                fw.mm(pc[:, 16:32], cf['blk'][:], gh)
                fw.mm(pc[:, 32:48], cf['dirf'][:], gh)
                fw.mm(pc[:, 48:64], cf['dirb'][:], gh)
                fw.cp(gcc[:], pc[:, 0:16], 'dve')
                fw.act(egl[:].rearrange("p a b -> p (a b)"), pc[:, 32:64], AF.Exp)
                fw.tt(gbc[:], gcc[:], lnb[:, h, :], ALU.add)
                fw.act(bge[:], gcc[:], AF.Exp)
                fw.tt(bge[:], bge[:], beta[:, h, :], ALU.mult)
                fw.tt(edk[:], pc[:, 16:32], gcc[:], ALU.subtract)
                fw.act(edk[:], edk[:], AF.Exp)
                def dbatch(nb, P):
                    t0, t1, tE, a0, b0, b1, kbg, vb, Xf = P
                    pp = nb % 2
                    bk = [self.pst[2 * pp][:, 0:512], self.pst[2 * pp][:, 512:1024],
                          self.pst[2 * pp + 1][:, 0:512], self.pst[2 * pp + 1][:, 512:1024]]
                    bs = slice(nb * 4, (nb + 1) * 4)
                    pk = slice(nb * 512, (nb + 1) * 512)
                    tok = slice(nb * 256, (nb + 1) * 256)
                    fw.tt(v3(t0[:]), bc_n(cf['cum'][:]), bc_last(gh[:, bs]), ALU.mult)
                    pg = bk[0]
                    fw.mm(pg, self.ones_f[:], t0[:])
                    fw.tt(v3(t1[:]), bc_n(cf['ident'][:]), bc_last(lnb[:, h, bs]), ALU.mult)
                    fw.tt(t1[:], t1[:], t0[:], ALU.add)
                    pgb = bk[1]
                    fw.mm(pgb, self.ones_f[:], t1[:])
                    pkk = bk[2]
                    pkq = bk[3]
                    for j in range(4):
                        n = nb * 4 + j
                        cs = slice(n * 64, (n + 1) * 64)
                        for d in range(2):
                            for d2 in range(2):
                                fw.mm(pkk[d * 64:(d + 1) * 64, j * 128 + d2 * 64:j * 128 + (d2 + 1) * 64], kn[:, cs], kn[:, cs])
                                fw.mm(pkq[d * 64:(d + 1) * 64, j * 128 + d2 * 64:j * 128 + (d2 + 1) * 64], kn[:, cs], qn[:, cs])
                    yield
                    fw.stt(v3(tE[:]), v3(pg), -1.0, bc_n(cf['negB'][:]), ALU.mult, ALU.add)
                    fw.tt(v3(tE[:]), v3(tE[:]), bc_last(gbc[:, bs]), ALU.add)
                    fw.act(tE[:], tE[:], AF.Exp)
                    fw.stt(b0[:], pkk, -1.0, tE[:], ALU.mult, ALU.mult)
                    yield
                    fw.tt(v3(tE[:]), v3(pgb), bc_n(cf['negA'][:]), ALU.add)
                    fw.tt(v3(tE[:]), v3(tE[:]), bc_last(gcc[:, bs]), ALU.subtract)
                    fw.act(tE[:], tE[:], AF.Exp)
                    fw.stt(a0[:], pkk, -1.0, tE[:], ALU.mult, ALU.mult)
                    fw.tt(v3(t0[:]), v3(a0[:]), bc_n(cf['ident'][:]), ALU.add)
                    yield
                    fw.tt(v3(tE[:]), v3(pg), bc_n(cf['negAi'][:]), ALU.add)
                    fw.tt(v3(tE[:]), v3(tE[:]), bc_last(gcc[:, bs]), ALU.subtract)
                    fw.act(tE[:], tE[:], AF.Exp)
                    fw.tt(attnT[:, pk], pkq, tE[:], ALU.mult)
                    yield
                    fw.act(tE[:], pg, AF.Exp)
                    q4 = qn[:, tok].rearrange("p (n t) -> p n t", n=4).unsqueeze(2).to_broadcast([128, 4, 2, 64])
                    fw.tt(qgT[:, pk].rearrange("p (n d t) -> p n d t", n=4, d=2), q4,
                          tE[:].rearrange("p (n d t) -> p n d t", n=4, d=2), ALU.mult)
                    yield
                    cur = (a0, b0)
                    nxt = (tE, b1)
                    xc, xn = t0, t1

                    def mm4(pd, lt, rt):
                        for j in range(4):
                            c = slice(j * 128, (j + 1) * 128)
                            fw.mm(pd[:, c], lt[:, c], rt[:, c])
                    for lvl in range(5):
                        A, B = cur
                        if lvl < 4:
                            pa = bk[0]
                            mm4(pa, B, A)
                            pb = bk[1]
                            mm4(pb, A, B)
                            fw.cp(nxt[0][:], pa, 'act')
                            fw.cp(nxt[1][:], pb, 'act')
                        else:
                            pb = bk[1]
                            mm4(pb, A, B)
                            fw.cp(nxt[1][:], pb, 'dve')
                        yield
                        px = bk[2]
                        mm4(px, nxt[1], xc)
                        if lvl < 4:
                            fw.tt(xn[:], px, xc[:], ALU.add)
                        else:
                            fw.tt(Xf[:], px, xc[:], ALU.add)
                        cur, nxt = nxt, cur
                        xc, xn = xn, xc
                        yield
                    ptk = bk[3].bitcast(BF16)
                    ptv = bk[0].bitcast(BF16)
                    for j in range(4):
                        n = nb * 4 + j
                        for d in range(2):
                            fw.tr(ptk[d * 64:(d + 1) * 64, j * 128:(j + 1) * 128], kn[:, n * 64:(n + 1) * 64], self.ident_b[:])
                            fw.tr(ptv[d * 64:(d + 1) * 64, j * 128:(j + 1) * 128], vn[:, n * 64:(n + 1) * 64], self.ident_b[:])
                    fw.tt(v3(kbg[:]), v3(ptk[:, 0:512]), bc_last(bge[:, bs]), ALU.mult)
                    fw.tt(v3(kdec[:, pk]), v3(ptk[:, 0:512]), bc_last(edk[:, bs]), ALU.mult)
                    fw.tt(v3(vb[:]), v3(ptv[:, 0:512]), bc_last(beta[:, h, bs]), ALU.mult)
                    yield
                    pu = bk[1]
                    pw = bk[2]
                    for j in range(4):
                        c = slice(j * 128, (j + 1) * 128)
                        fw.mm(pu[:, c], Xf[:, c], vb[:, c])
                        fw.mm(pw[:, c], kbg[:, c], Xf[:, c])
                    fw.cp(u[:, pk], pu, 'act')
                    fw.cp(wT[:, pk], pw, 'dve')
                self.pipeline([dbatch(nb, TP[nb % 2]) for nb in range(4)], 2)
                fw.barrier()
            if h == 0: fw.mark('D%d.h0.scan' % e)
            cur = 0
            for i in range(16):
                nn = [i, 15 - i]
                Rr = [slice(0, 64), slice(64, 128)]
                Sc = [S2[d][cur] for d in range(2)]
                Sn = [S2[d][1 - cur] for d in range(2)]
                Sbc = [Sb2[d][cur] for d in range(2)]
                Sbn = [Sb2[d][1 - cur] for d in range(2)]
                if i > 0 and i % 4 == 0:
                    for d in range(2):
                        fw.ts(Sc[d][:], Sc[d][:], self.flag[:], ALU.mult)
                        fw.cp(Sbc[d][:], Sc[d][:], 'act')
                ccs = [slice(nn[d] * 128 + d * 64, nn[d] * 128 + d * 64 + 64) for d in range(2)]
                PSs = [self.pst[d][:, (i % 2) * 512:(i % 2) * 512 + 512] for d in range(2)]
                vws = [vnw[d][i % 2] for d in range(2)]
                for d in range(2):
                    fw.mm(PSs[d][Rr[d], 0:128], wT[:, ccs[d]], Sbc[d][:])
                for d in range(2):
                    n = nn[d]
                    fw.tt(vws[d][Rr[d], :], u[Rr[d], n * 128:(n + 1) * 128], PSs[d][Rr[d], 0:128], ALU.subtract)
                for d in range(2):
                    n = nn[d]
                    fw.mm(PSs[d][:, 128:256], kdec[Rr[d], n * 128:(n + 1) * 128], vws[d][Rr[d], :])
                for d in range(2):
                    po = PSs[d][:, 256:320]
                    fw.mm(po, Sbc[d][:], qgT[:, ccs[d]], start=True, stop=False)
                    fw.mm(po, vws[d][Rr[d], :], attnT[Rr[d], ccs[d]], start=False, stop=True)
                for d in range(2):
                    fw.stt(Sbn[d][:], Sc[d][:], egl[:, d, nn[d]:nn[d] + 1], PSs[d][:, 128:256], ALU.mult, ALU.add)
                for d in range(2):
                    fw.stt(Sn[d][:], Sc[d][:], egl[:, d, nn[d]:nn[d] + 1], PSs[d][:, 128:256], ALU.mult, ALU.add)
                for d in range(2):
                    n = nn[d]
                    oc = oacc[:, n * 64:(n + 1) * 64]
                    fw.tt(oc, oc, PSs[d][:, 256:320], ALU.add)
                    if i % 4 == 3:
                        fw.dma('sp', self.dout['nsd'][n // 4, e, d, h], Sn[d][:])
                cur = 1 - cur
            if h == 0: fw.mark('D%d.h0.post' % e)
            with ExitStack() as post:
                zs = fw.sb('dzs' + tg, [128, NT], BF16, stack=post)
                sqp = [fw.sb('dpsq%s_%d' % (tg, i), [128, 512], BF16, stack=post) for i in range(2)]
                rstd = fw.sb('dprs' + tg, [128, 512], F32, stack=post)
                tmp = fw.sb('dptmp' + tg, [128, 512], F32, stack=post)
                slab = self.wload(w_in[:, 3072 + h * 128:3072 + h * 128 + 128], 128)
                self.proj_fm(slab, 0, lambda ps, tt: fw.act(zs[:, tt * 512:(tt + 1) * 512], ps, AF.Silu))
                for tt in range(2):
                    sl = slice(tt * 512, (tt + 1) * 512)
                    self.rms_bcast([oacc[:, sl]], 512, 1.0 / 128.0, rstd[:], sqp)
                    fw.stt(tmp[:], oacc[:, sl], sm['onorm_a'][:, e:e + 1], rstd[:], ALU.mult, ALU.mult)
                    fw.tt(self.mixT[:, h, sl], tmp[:], zs[:, sl], ALU.mult)
                fw.barrier()

    def mixer_gla(self, o, w_in):
        fw = self.fw
        di = self.din
        cf = self.cf
        with ExitStack() as ph:
            lrT = [fw.sb('lrT%d_%d' % (o, d), [32, NT], BF16, stack=ph) for d in range(2)]
            waug = [fw.sb('waug%d_%d' % (o, d), [32, 512], BF16, stack=ph) for d in range(2)]
            for d in range(2):
                fw.memset(lrT[d][:], 1.0)
                fw.dma('pool', waug[d][0:16, :], di['w_glr'][o, d])
                fw.dma('pool', waug[d][16:17, :], di['b_glr'][o, d:d + 1, :])
            slab = self.wload(w_in[:, 3072:3104], 32)
            for d in range(2):
                for tt in range(2):
                    ps = self.psum(512)
                    for kc in range(16):
                        fw.mm(ps[0:16, :], slab[:, kc, d * 16:(d + 1) * 16], self.hT[:, kc, tt * 512:(tt + 1) * 512],
                              start=(kc == 0), stop=(kc == 15))
                    fw.cp(lrT[d][0:16, tt * 512:(tt + 1) * 512], ps[0:16, :], 'act')
            for h in range(4):
                self.gla_head(o, h, w_in, lrT, waug)
            fw.barrier()

    def gla_head(self, o, h, w_in, lrT, waug):
        fw = self.fw
        di = self.din
        cf = self.cf
        tg = '%d_%d' % (o, h)
        if h <= 1: self.fw.mark('G%d.h%d.start' % (o, h))
        with ExitStack() as hd:
            qg = fw.sb('gqg' + tg, [128, 2048], BF16, stack=hd, split=128)
            kg = fw.sb('gkg' + tg, [128, 2048], BF16, stack=hd, split=128)
            attnT = fw.sb('gat' + tg, [128, 2048], BF16, stack=hd, split=128)
            kdec = fw.sb('gkd' + tg, [128, 2048], BF16, stack=hd, split=128)
            v2tok = fw.sb('gv2' + tg, [128, 16, 256], BF16, stack=hd, split=256)
            ebl = fw.sb('gebl' + tg, [128, 16, 2], F32, stack=hd)
            oacc = fw.sb('goacc' + tg, [128, 2, NT], F32, stack=hd, split=64)
            S2 = [[fw.sb('gS%s_%d_%d' % (tg, d, i), [128, 256], F32, stack=hd) for i in range(2)] for d in range(2)]
            Sb2 = [[fw.sb('gSb%s_%d_%d' % (tg, d, i), [128, 256], BF16, stack=hd) for i in range(2)] for d in range(2)]
            fw.memset(oacc[:], 0.0, 'pool')
            for d in range(2):
                fw.dma('sp', S2[d][0][:], di['st_gla'][o, d, h])
                fw.ts(S2[d][0][:], S2[d][0][:], self.flag[:], ALU.mult)
                fw.cp(Sb2[d][0][:], S2[d][0][:], 'act')
            with ExitStack() as it:
                qT = fw.sb('gq' + tg, [128, NT], BF16, stack=it)
                kT = fw.sb('gk' + tg, [128, NT], BF16, stack=it)
                vT = fw.sb('gv' + tg, [128, 2, NT], BF16, stack=it)
                gk2 = fw.sb('ggk' + tg, [128, 16, 128], F32, stack=it, split=512)
                tA = fw.sb('gtA' + tg, [128, 512], F32, stack=it)
                tB = fw.sb('gtB' + tg, [128, 512], F32, stack=it)
                tC = fw.sb('gtC' + tg, [128, 512], F32, stack=it)
                for nb in range(4):
                    ps = self.psum(512)
                    for j in range(4):
                        n = nb * 4 + j
                        for d in range(2):
                            fw.mm(ps[d * 64:(d + 1) * 64, j * 128:(j + 1) * 128], lrT[d][0:17, n * 64:(n + 1) * 64],
                                  waug[d][0:17, h * 128:(h + 1) * 128])
                    fw.act(tA[:], ps, AF.Exp, scale=-1.0)
                    fw.act(tA[:], tA[:], AF.Ln, bias=self.onescol[:])
                    fw.ts(gk2[:, nb * 4:(nb + 1) * 4, :].rearrange("p a b -> p (a b)"), tA[:], -1.0 / 16.0, ALU.mult)
                pe = self.psum(32)
                for n in range(16):
                    fw.mm(pe[:, n * 2:n * 2 + 2], gk2[:, n, :], self.dirsel[:])
                fw.act(ebl[:].rearrange("p a b -> p (a b)"), pe, AF.Exp)
                slab = self.wload(w_in[:, h * 128:h * 128 + 128], 128)
                self.proj_fm(slab, 0, lambda ps, tt: fw.ts(qT[:, tt * 512:(tt + 1) * 512], ps, SCALE, ALU.mult))
                slab = self.wload(w_in[:, 512 + h * 128:512 + h * 128 + 128], 128)
                self.proj_fm(slab, 0, lambda ps, tt: fw.cp(kT[:, tt * 512:(tt + 1) * 512], ps, 'act'))
                slab = self.wload(w_in[:, 1024 + h * 256:1024 + h * 256 + 256], 256)
                for hf in range(2):
                    self.proj_fm(slab, hf * 128, lambda ps, tt, hf=hf: fw.cp(vT[:, hf, tt * 512:(tt + 1) * 512], ps, 'act'))
                for nb in range(4):
                    tok = slice(nb * 256, (nb + 1) * 256)
                    pk = slice(nb * 512, (nb + 1) * 512)
                    ps = self.psum(512)
                    for j in range(4):
                        n = nb * 4 + j
                        fw.mm(ps[:, j * 128:(j + 1) * 128], gk2[:, n, :], cf['cum'][:])
                    fw.act(tA[:], ps, AF.Exp)
                    fw.act(tB[:], ps, AF.Exp, scale=-1.0)
                    v4 = lambda ap: ap.rearrange("p (n d t) -> p n d t", n=4, d=2)
                    b4 = lambda ap: ap.rearrange("p (n t) -> p n t", n=4).unsqueeze(2).to_broadcast([128, 4, 2, 64])
                    fw.tt(v4(qg[:, pk]), b4(qT[:, tok]), v4(tA[:]), ALU.mult)
                    fw.tt(v4(kg[:, pk]), b4(kT[:, tok]), v4(tB[:]), ALU.mult)
                    pa = self.psum(512)
                    for j in range(4):
                        n = nb * 4 + j
                        cs = slice(n * 128, (n + 1) * 128)
                        fw.mm(pa[:, j * 128:(j + 1) * 128], kg[:, cs], qg[:, cs])
                    fw.tt(attnT[:, pk].rearrange("p (n c) -> p n c", n=4), pa.rearrange("p (n c) -> p n c", n=4),
                          self.m01[:].unsqueeze(1).to_broadcast([128, 4, 128]), ALU.mult)
                    pb = self.psum(512)
                    fw.mm(pb, cf['cum'][:], gk2[:, nb * 4:(nb + 1) * 4, :].rearrange("p a b -> p (a b)"))
                    pl = self.psum(512)
                    fw.mm(pl, cf['blk'][:], gk2[:, nb * 4:(nb + 1) * 4, :].rearrange("p a b -> p (a b)"))
                    fw.cp(tC[:], pb, 'act')
                    fw.tt(tC[:], pl, tC[:], ALU.subtract)
                    fw.act(tC[:], tC[:], AF.Exp)
                    pt = self.psum(512).bitcast(BF16)
                    for j in range(4):
                        n = nb * 4 + j
                        for d in range(2):
                            fw.tr(pt[d * 64:(d + 1) * 64, j * 128:(j + 1) * 128], kT[:, n * 64:(n + 1) * 64], self.ident_b[:])
                    fw.tt(kdec[:, pk], pt[:, 0:512], tC[:], ALU.mult)
                    pv = self.psum(512).bitcast(BF16)
                    for j in range(4):
                        n = nb * 4 + j
                        for hf in range(2):
                            for d in range(2):
                                fw.tr(pv[d * 64:(d + 1) * 64, j * 256 + hf * 128:j * 256 + (hf + 1) * 128],
                                      vT[:, hf, n * 64:(n + 1) * 64], self.ident_b[:])
                    fw.cp(v2tok[:, nb * 4:(nb + 1) * 4, :].rearrange("p a b -> p (a b)"), pv[:, 0:1024], 'dve')
                fw.barrier()
            if h == 0: fw.mark('G%d.h0.scan' % o)
            Rr = [slice(0, 64), slice(64, 128)]

            def issue_pss(i, d):
                n = i if d == 0 else 15 - i
                p = self.pst[d][:, (i % 2) * 256:(i % 2) * 256 + 256]
                fw.mm(p, kdec[Rr[d], n * 128:(n + 1) * 128], v2tok[Rr[d], n, :])
                return p
            pq = [issue_pss(0, 0), issue_pss(0, 1)]
            cur = 0
            for i in range(16):
                nn = [i, 15 - i]
                Sc = [S2[d][cur] for d in range(2)]
                Sn = [S2[d][1 - cur] for d in range(2)]
                Sbc = [Sb2[d][cur] for d in range(2)]
                Sbn = [Sb2[d][1 - cur] for d in range(2)]
                if i > 0 and i % 4 == 0:
                    for d in range(2):
                        fw.ts(Sc[d][:], Sc[d][:], self.flag[:], ALU.mult)
                        fw.cp(Sbc[d][:], Sc[d][:], 'act')
                for d in range(2):
                    fw.stt(Sn[d][:], Sc[d][:], ebl[:, nn[d], d:d + 1], pq[d], ALU.mult, ALU.add)
                for d in range(2):
                    fw.cp(Sbn[d][:], Sn[d][:], 'act')
                pos = []
                for d in range(2):
                    n = nn[d]
                    cc = slice(n * 128 + d * 64, n * 128 + d * 64 + 64)
                    for hf in range(2):
                        pc0 = 512 + ((2 * i + hf) % 4) * 64
                        po = self.pst[d][:, pc0:pc0 + 64]
                        fw.mm(po, Sbc[d][:, hf * 128:(hf + 1) * 128], qg[:, cc], start=True, stop=False)
                        fw.mm(po, v2tok[Rr[d], n, hf * 128:(hf + 1) * 128], attnT[Rr[d], cc], start=False, stop=True)
                        pos.append((po, oacc[:, hf, n * 64:(n + 1) * 64]))
                if i + 1 < 16:
                    pq = [issue_pss(i + 1, 0), issue_pss(i + 1, 1)]
                for po, oc in pos:
                    fw.tt(oc, oc, po, ALU.add)
                if i % 4 == 3:
                    for d in range(2):
                        fw.dma('sp', self.dout['nsg'][nn[d] // 4, o, d, h], Sn[d][:])
                cur = 1 - cur
            if h == 0: fw.mark('G%d.h0.post' % o)
            with ExitStack() as post:
                zs = fw.sb('gzs' + tg, [128, 2, NT], BF16, stack=post)
                sqp = [fw.sb('gsq%s_%d' % (tg, i), [128, 512], BF16, stack=post) for i in range(2)]
                rstd = fw.sb('grs' + tg, [128, 512], F32, stack=post)
                tmp = fw.sb('gtmp' + tg, [128, 512], F32, stack=post)
                slab = self.wload(w_in[:, 2048 + h * 256:2048 + h * 256 + 256], 256)
                for hf in range(2):
                    self.proj_fm(slab, hf * 128, lambda ps, tt, hf=hf: fw.act(zs[:, hf, tt * 512:(tt + 1) * 512], ps, AF.Silu))
                for tt in range(2):
                    sl = slice(tt * 512, (tt + 1) * 512)
                    self.rms_bcast([oacc[:, 0, sl], oacc[:, 1, sl]], 512, 1.0 / 256.0, rstd[:], sqp)
                    for hf in range(2):
                        fw.stt(tmp[:], oacc[:, hf, sl], self.sm['onorm_c'][:, o, hf:hf + 1], rstd[:], ALU.mult, ALU.mult)
                        fw.tt(self.mixT[:, h * 2 + hf, sl], tmp[:], zs[:, hf, sl], ALU.mult)
            fw.barrier()

    def mixer_nbr(self, o, w_in):
        fw = self.fw
        di = self.din
        with ExitStack() as ph:
            kT = fw.sb('nkT%d' % o, [128, 8, NT], BF16, stack=ph, split=NT)
            vtok = fw.sb('nvtok%d' % o, [128, 8, 1024], BF16, stack=ph, split=256)
            kctok = fw.sb('nkctok%d' % o, [128, 2, 1024], BF16, stack=ph)
            vc = fw.sb('nvc%d' % o, [128, 2, 1024], BF16, stack=ph)
            kcT = fw.sb('nkcT%d' % o, [128, 8, 256], BF16, stack=ph)
            kvst = [fw.sb('nkvst%d_%d' % (o, i), [128, 256], F32, stack=ph) for i in range(2)]
            qT = [fw.sb('nqT%d_%d' % (o, i), [128, NT], BF16, stack=ph) for i in range(1)] * 2
            zs = [fw.sb('nzs%d_%d' % (o, i), [128, NT], BF16, stack=ph) for i in range(1)] * 2
            biasT = [fw.sb('nbias%d_%d' % (o, i), [128, 7, 128], BF16, stack=ph) for i in range(2)]
            maskt = [fw.sb('nmask%d_%d' % (o, i), [128, 896], BF16, stack=ph) for i in range(3)]
            brevs = [fw.sb('nbrev%d_%d' % (o, i), [128, 7, 128], BF16, stack=ph) for i in range(2)]
            zero = fw.sb('nzero%d' % o, [120, 128], F32, stack=ph)
            mx = [fw.sb('nmx%d_%d' % (o, i), [128, 1], F32, stack=ph) for i in range(2)]
            nm = [fw.sb('nnm%d_%d' % (o, i), [128, 1], F32, stack=ph) for i in range(2)]
            rs = [fw.sb('nrs%d_%d' % (o, i), [128, 1], F32, stack=ph) for i in range(2)]
            es = [fw.sb('nes%d_%d' % (o, i), [128, 1], F32, stack=ph) for i in range(2)]
            E = [fw.sb('nE%d_%d' % (o, i), [128, 896], BF16, stack=ph) for i in range(2)]
            ET = [fw.sb('nET%d_%d' % (o, i), [128, 896], BF16, stack=ph) for i in range(2)]
            cD = PC_ODD
            fw.memset(zero[:], 0.0)
            fw.dma('sp', self.rpbp.rearrange("h r c -> (h r) c"), zero[:])
            fw.dma('sp', self.rpbp[:, :, 48:79], di['rpb'][o])
            for blk in range(2):
                fw.dma('pool', kctok[:, blk, :], di['kvn'][o, 0, blk * 128:(blk + 1) * 128].rearrange("t g d -> t (g d)"))
                fw.dma('pool', vc[:, blk, :], di['kvn'][o, 1, blk * 128:(blk + 1) * 128].rearrange("t g d -> t (g d)"))
            for g4 in range(2):
                pt = self.psum(512).bitcast(BF16)
                for gg in range(4):
                    g = g4 * 4 + gg
                    for blk in range(2):
                        fw.tr(pt[:, (gg * 2 + blk) * 128:(gg * 2 + blk + 1) * 128], kctok[:, blk, g * 128:(g + 1) * 128], self.ident_b[:])
                fw.cp(kcT[:, g4 * 4:(g4 + 1) * 4, :].rearrange("p g k -> p (g k)"), pt[:, 0:1024], 'dve')
            ci = [0]
            for s2 in range(4):
                slab = self.wload(w_in[:, cD + 1024 + s2 * 256:cD + 1024 + (s2 + 1) * 256], 256)
                for j2 in range(2):
                    g = s2 * 2 + j2
                    self.proj_fm(slab, j2 * 128, lambda ps, tt, g=g: fw.cp(kT[:, g, tt * 512:(tt + 1) * 512], ps, 'act'))
                for tb in range(8):
                    sg = kvst[ci[0] % 2]
                    ci[0] += 1
                    self.proj_tm(slab, 0, 256, tb, lambda ps, sg=sg: fw.cp(sg[:], ps, 'dve'))
                    fw.dma('sp', self.dout['nkn'][tb // 2, o, 0, (tb % 2) * 128:(tb % 2 + 1) * 128, s2 * 2:s2 * 2 + 2, :].rearrange("t g d -> t (g d)"), sg[:])
            for s2 in range(4):
                slab = self.wload(w_in[:, cD + 2048 + s2 * 256:cD + 2048 + (s2 + 1) * 256], 256)
                for tb in range(8):
                    sg = kvst[ci[0] % 2]
                    ci[0] += 1
                    self.proj_tm(slab, 0, 256, tb, lambda ps, sg=sg: fw.cp(sg[:], ps, 'dve'))
                    fw.cp(vtok[:, tb, s2 * 256:(s2 + 1) * 256], sg[:], 'act')
                    fw.dma('sp', self.dout['nkn'][tb // 2, o, 1, (tb % 2) * 128:(tb % 2 + 1) * 128, s2 * 2:s2 * 2 + 2, :].rearrange("t g d -> t (g d)"), sg[:])
            def bias_load(hh):
                brev = brevs[hh % 2]
                for dd in range(7):
                    dl = dd - 3
                    for rq in range(2):
                        off = hh * 15 * 128 + (2 * dl - rq + 7) * 128
                        src = bass.AP(self.rpbp.tensor, off, [[1, 64], [128, 2], [1, 64]])
                        fw.dma('pool', brev[rq * 64:(rq + 1) * 64, dd, :].rearrange("p (a b) -> p a b", a=2), src,
                               extra_ins=[self.rpbp])

            def bias_finish(hh):
                brev = brevs[hh % 2]
                bt_ = biasT[hh % 2]
                for (c0, c1) in ((0, 512), (512, 896)):
                    pj = self.psum(c1 - c0)
                    fw.mm(pj, self.jrev[:], brev[:].rearrange("p a b -> p (a b)")[:, c0:c1])
                    fw.ts(bt_[:].rearrange("p a b -> p (a b)")[:, c0:c1], pj, self.flag[:], ALU.mult, 1.0 / SCALE, ALU.mult)
            it = 0
            for h in range(8):
                w2 = h % 2
                if h % 2 == 0:
                    slabq = self.wload(w_in[:, cD + h * 128:cD + h * 128 + 256], 256)
                    slabz = self.wload(w_in[:, cD + 3072 + h * 128:cD + 3072 + h * 128 + 256], 256)
                self.proj_fm(slabq, w2 * 128, lambda ps, tt, w2=w2: fw.cp(qT[w2][:, tt * 512:(tt + 1) * 512], ps, 'act'))
                self.proj_fm(slabz, w2 * 128, lambda ps, tt, w2=w2: fw.act(zs[w2][:, tt * 512:(tt + 1) * 512], ps, AF.Silu))
                bt = biasT[w2]
                if h == 0:
                    bias_load(0)
                bias_finish(h)
                if h + 1 < 8:
                    bias_load(h + 1)
                units = []
                for j in range(8):
                    mk = maskt[it % 3]
                    slots = []
                    for si, m in enumerate(NBLK[j]):
                        slots.append((kT[:, h, m * 128:(m + 1) * 128], vtok[:, m, h * 128:(h + 1) * 128],
                                      mk[:, si * 128:(si + 1) * 128], bt[:, m - j + 3, :]))
                    for cb in range(2):
                        slots.append((kcT[:, h, cb * 128:(cb + 1) * 128], vc[:, cb, h * 128:(h + 1) * 128],
                                      mk[:, (5 + cb) * 128:(6 + cb) * 128], None))
                    w = it % 2
                    it += 1
                    nbl = NBLK[j]
                    m0, nw = nbl[0], len(nbl)
                    runs = []
                    c = 0
                    while c < nw:
                        n_ = min(nw - c, 4 - (c % 4))
                        runs.append((c * 128, n_ * 128, kT[:, h, (m0 + c) * 128:(m0 + c + n_) * 128], mk[:, c * 128:(c + n_) * 128],
                                     bt[:, m0 + c - j + 3:m0 + c + n_ - j + 3, :].rearrange("p a b -> p (a b)")))
                        c += n_
                    for cb in range(2):
                        cc = nw + cb
                        if cb == 0 and (cc % 4) != 3:
                            runs.append((cc * 128, 256, kcT[:, h, 0:256], mk[:, 5 * 128:7 * 128], None))
                            break
                        runs.append((cc * 128, 128, kcT[:, h, cb * 128:(cb + 1) * 128], mk[:, (5 + cb) * 128:(6 + cb) * 128], None))
                    pre = (lambda mk=mk, j=j: fw.dma('pool', mk[:], di['maskn'][j]))
                    units.append(self.attention('D', qT[w2][:, j * 128:(j + 1) * 128], slots, self.negbig[:],
                                                zs[w2][:, j * 128:(j + 1) * 128], self.mixT[:, h, j * 128:(j + 1) * 128],
                                                (mx[w], nm[w], rs[w], es[w], E[w], ET[w]), pre=pre, uidx=it, runs=runs))
                self.pipeline(units, 4)
            fw.barrier()


_PROG = {}


def _get_prog(**kw):
    key = tuple(sorted(kw.items()))
    if key not in _PROG:
        _PROG[key] = Prog(**kw)
    return _PROG[key]


def _prep_inputs(inp):
    f = lambda a: np.ascontiguousarray(np.asarray(a, dtype=np.float32))
    x_prompt, x_sample = f(inp['x_prompt']), f(inp['x_sample'])
    shared = {}
    shared['norm_w'] = f(inp['norm_w']).reshape(4, 16, 128).transpose(2, 0, 1)
    shared['w_ada'] = f(inp['w_ada'])
    shared['b_ada'] = f(inp['b_ada']).reshape(4, 48, 128).transpose(2, 0, 1)
    shared['w_in_even'] = f(inp['w_in_even'])
    shared['conv_a'] = f(inp['conv_a']).reshape(2, 5, 24, 128).transpose(3, 0, 2, 1)
    shared['a_log'] = np.repeat(f(inp['a_log_a']).transpose(1, 0, 2), 64, axis=0)
    shared['dt_bias'] = np.repeat(f(inp['dt_bias_a']).transpose(1, 0, 2), 64, axis=0)
    shared['onorm_a'] = f(inp['onorm_a']).T
    shared['sink_b'] = np.broadcast_to(f(inp['sink_b'])[None], (128, 2, 8))
    shared['w_out_even'] = f(inp['w_out_even'])
    shared['w_in_odd'] = f(inp['w_in_odd'])
    shared['w_glr'] = f(inp['w_glr_c'])
    shared['b_glr'] = f(inp['b_glr_c'])
    shared['onorm_c'] = f(inp['onorm_c']).reshape(2, 2, 128).transpose(2, 0, 1)
    shared['rpb'] = f(inp['rpb_d'])
    shared['w_out_odd'] = f(inp['w_out_odd'])
    shared['final_w'] = f(inp['final_norm_w']).reshape(16, 128).T
    shared = {k: np.ascontiguousarray(v) for k, v in shared.items()}
    consts = [_consts(0), _consts(1)]
    maps = []
    for core in range(8):
        role = 0 if core < 4 else 1
        m = dict(shared)
        m.update(consts[role])
        if role == 0:
            m['x'] = x_prompt[4 * core:4 * core + 4].reshape(NT, D)
            cond = f(inp['c_ctx'])
            b = 0
        else:
            b = core - 4
            m['x'] = x_sample[b]
            cond = f(inp['c'])[b]
        m['cond'] = np.ascontiguousarray(cond.reshape(16, 128).T)
        m['st_delta'] = f(inp['state_delta'])[b]
        m['kvw'] = f(inp['cache_kv_win'])[b]
        m['st_gla'] = f(inp['state_gla'])[b]
        m['kvn'] = f(inp['cache_kv_nbr'])[b]
        maps.append({k: np.ascontiguousarray(v) for k, v in m.items()})
    return maps


def _run(inp, cores=None, xover=None, **kw):
    prog = _get_prog(**kw)
    maps = _prep_inputs(inp)
    if xover is not None:
        for c in range(8):
            maps[c]['x'] = np.ascontiguousarray(xover[c], dtype=np.float32)
    if cores is not None:
        maps = [maps[c] for c in cores]
    res = run_bass_kernel_spmd(prog.nc, maps, core_ids=list(range(len(maps))))
    return res.results


def kernel(**inp):
    r = _run(inp)
    y_prompt = np.concatenate([r[c]['y'].reshape(4, 256, D) for c in range(4)], 0)
    y_sample = np.stack([r[c]['y'] for c in range(4, 8)], 0)
    nsd = np.concatenate([r[c]['nsd'] for c in range(4)], 0)
    nkw = np.concatenate([r[c]['nkw'] for c in range(4)], 0)
    nsg = np.concatenate([r[c]['nsg'] for c in range(4)], 0)
    nkn = np.concatenate([r[c]['nkn'] for c in range(4)], 0)
    return (y_prompt.astype(np.float32), y_sample.astype(np.float32), nsd.astype(np.float32),
            nkw.astype(np.float32), nsg.astype(np.float32), nkn.astype(np.float32))
```

```python
import numpy as np
from contextlib import ExitStack
import concourse.bass as bass
import concourse.mybir as mybir
from concourse.bass_utils import run_bass_kernel_spmd

F32 = mybir.dt.float32
BF16 = mybir.dt.bfloat16
AF = mybir.ActivationFunctionType
ALU = mybir.AluOpType
AX = mybir.AxisListType

COMPUTE = ('pe', 'act', 'dve', 'pool')
NDMASEM = 32

D = 2048
NT = 1024
EPS = 1e-6
NEG = -30000.0
PA_EVEN = 4128
P_EVEN = 6688
PC_ODD = 3104
P_ODD = 7200
SCALE = 128 ** -0.5


class FW:
    def __init__(self, nc, stack):
        self.nc = nc
        self.stack = stack
        self.ops = {e: [] for e in ('pe', 'act', 'dve', 'pool', 'sp')}
        self.sem = {}
        for e in COMPUTE:
            self.sem[e] = stack.enter_context(nc.semaphore('s_' + e))
        self.dsem = [stack.enter_context(nc.semaphore('d%d' % i)) for i in range(NDMASEM)]
        self.dexp = [0] * NDMASEM
        self.dnext = 0
        self.dnext_p = 0
        self.cnt = {e: 0 for e in COMPUTE}
        self.waited = {}
        self.res = {}
        self.split = {}
        self.uniq = 0
        self.marks = []
        self.ninst = 0

    def sb(self, name, shape, dt=F32, stack=None, split=None):
        t = (stack or self.stack).enter_context(self.nc.sbuf_tensor('t_' + name, list(shape), dt))
        if split:
            self.split['t_' + name] = split * (2 if dt == BF16 else 4)
        return t

    def ps(self, name, shape, dt=F32, split=None):
        t = self.stack.enter_context(self.nc.psum_tensor('t_' + name, list(shape), dt))
        if split:
            self.split['t_' + name] = split * (2 if dt == BF16 else 4)
        return t

    def keys(self, x):
        if isinstance(x, str):
            return [x]
        name = x.tensor.name
        if name in OUT_SHAPES:
            self.uniq += 1
            return ['%s@%d' % (name, self.uniq)]
        sp = self.split.get(name)
        if sp is None:
            return [name]
        dims = x.ap
        esz = 2 if x.dtype == BF16 else 4
        pstride = dims[0][0]
        off = x.offset % pstride if pstride > 0 else x.offset
        span = 0
        for st, n in dims[1:]:
            span += abs(st) * (n - 1)
        lo = (off * esz) // sp
        hi = ((off + span) * esz) // sp
        return ['%s#%d' % (name, r) for r in range(lo, hi + 1)]

    def _keys(self, lst):
        out = []
        for x in lst:
            if x is None:
                continue
            out.extend(self.keys(x))
        return out

    def _deps(self, reads, writes):
        deps = {}

        def add(d):
            if d is None:
                return
            src, val = d
            if deps.get(src, 0) < val:
                deps[src] = val
        for r in reads:
            st = self.res.get(r)
            if st:
                add(st['w'])
        for w in writes:
            st = self.res.get(w)
            if st:
                add(st['w'])
                for d in st['r']:
                    add(d)
        return deps

    def _update(self, me, reads, writes):
        for r in reads:
            st = self.res.setdefault(r, {'w': None, 'r': []})
            st['r'] = [d for d in st['r'] if d[0] != me[0]] + [me]
        for w in writes:
            self.res[w] = {'w': me, 'r': []}

    def _semof(self, src):
        if isinstance(src, str):
            return self.sem[src]
        return self.dsem[src]

    def _emit_waits(self, eng, deps):
        for src, val in deps.items():
            if src == eng and eng == 'pe':
                continue
            key = (eng, src)
            if self.waited.get(key, 0) >= val:
                continue
            self.waited[key] = val
            self.ops[eng].append(('wait', self._semof(src), val))

    def op(self, eng, fn, ins=(), outs=()):
        reads = self._keys(ins)
        writes = self._keys(outs)
        writes = writes + [k for k in reads if k.startswith('t_ps') and k not in writes]
        deps = self._deps(reads, writes)
        self._emit_waits(eng, deps)
        self.cnt[eng] += 1
        me = (eng, self.cnt[eng])
        self.ops[eng].append(('inst', fn, self.sem[eng], 1))
        self._update(me, reads, writes)
        self.ninst += 1

    def dma(self, q, out, in_, extra_ins=(), **kw):
        reads = self._keys([in_] + list(extra_ins))
        writes = self._keys([out])
        deps = self._deps(reads, writes)
        half = NDMASEM // 2
        if q == 'pool':
            s = half + self.dnext_p
            self.dnext_p = (self.dnext_p + 1) % (NDMASEM - half)
        else:
            s = self.dnext
            self.dnext = (self.dnext + 1) % half
        if self.dexp[s] > 0:
            deps[s] = max(deps.get(s, 0), self.dexp[s])
        self._emit_waits(q, deps)
        self.dexp[s] += 16
        me = (s, self.dexp[s])
        self.ops[q].append(('inst', lambda e: e.dma_start(out=out, in_=in_, **kw), self.dsem[s], 16))
        self._update(me, reads, writes)
        self.ninst += 1

    def mark(self, label):
        self.marks.append((label, dict(self.cnt)))

    def barrier(self):
        deps = {}
        for s in range(NDMASEM):
            if self.dexp[s] > 0:
                deps[s] = self.dexp[s]
        for e in COMPUTE:
            if self.cnt[e] > 0:
                deps[e] = self.cnt[e]
        for e in ('pe', 'act', 'dve', 'pool', 'sp'):
            d = dict(deps)
            d.pop(e, None)
            self._emit_waits(e, d)

    def replay(self):
        nc = self.nc
        engmap = {'pe': 'tensor', 'act': 'scalar', 'dve': 'vector', 'pool': 'gpsimd', 'sp': 'sync'}
        with nc.Block() as block:
            for e, attr in engmap.items():
                ops = self.ops[e]

                def body(engobj, ops=ops):
                    for o in ops:
                        if o[0] == 'wait':
                            engobj.wait_ge(o[1], o[2])
                        else:
                            o[1](engobj).then_inc(o[2], o[3])
                getattr(block, attr)(body)

    def mm(self, out, lhsT, rhs, start=True, stop=True):
        self.op('pe', lambda e: e.matmul(out, lhsT=lhsT, rhs=rhs, start=start, stop=stop), [lhsT, rhs], [out])

    def tr(self, out, in_, ident):
        self.op('pe', lambda e: e.transpose(out, in_, ident), [in_, ident], [out])

    def act(self, out, in_, func, bias=None, scale=None, accum_out=None):
        kw = {}
        ins = [in_]
        if bias is not None:
            kw['bias'] = bias
            if not isinstance(bias, (int, float)):
                ins.append(bias)
        if scale is not None:
            kw['scale'] = scale
            if not isinstance(scale, (int, float)):
                ins.append(scale)
        outs = [out]
        if accum_out is not None:
            kw['accum_out'] = accum_out
            outs.append(accum_out)
        self.op('act', lambda e: e.activation(out=out, in_=in_, func=func, **kw), ins, outs)

    def tt(self, out, in0, in1, op, eng='dve'):
        self.op(eng, lambda e: e.tensor_tensor(out=out, in0=in0, in1=in1, op=op), [in0, in1], [out])

    def ts(self, out, in0, s1, op0, s2=None, op1=None, eng='dve'):
        ins = [in0]
        for s in (s1, s2):
            if s is not None and not isinstance(s, (int, float)):
                ins.append(s)
        if op1 is None:
            self.op(eng, lambda e: e.tensor_scalar(out=out, in0=in0, scalar1=s1, scalar2=None, op0=op0), ins, [out])
        else:
            self.op(eng, lambda e: e.tensor_scalar(out=out, in0=in0, scalar1=s1, scalar2=s2, op0=op0, op1=op1), ins, [out])

    def stt(self, out, in0, scalar, in1, op0, op1):
        ins = [in0, in1]
        if not isinstance(scalar, (int, float)):
            ins.append(scalar)
        self.op('dve', lambda e: e.scalar_tensor_tensor(out=out, in0=in0, scalar=scalar, in1=in1, op0=op0, op1=op1), ins, [out])

    def cp(self, out, in_, eng='dve'):
        if eng == 'act':
            self.op('act', lambda e: e.copy(out=out, in_=in_), [in_], [out])
        else:
            self.op(eng, lambda e: e.tensor_copy(out=out, in_=in_), [in_], [out])

    def recip(self, out, in_):
        self.op('dve', lambda e: e.reciprocal(out=out, in_=in_), [in_], [out])

    def memset(self, ap, val, eng='dve'):
        self.op(eng, lambda e: e.memset(ap, val), [], [ap])

    def rmax(self, out, in_):
        self.op('dve', lambda e: e.reduce_max(out=out, in_=in_, axis=AX.X), [in_], [out])


def _consts(role):
    lat = (role == 1)
    c = {}
    c['ident'] = np.eye(128, dtype=np.float32)
    p = np.arange(128)
    dr = p // 64
    t = p % 64
    same = dr[:, None] == dr[None, :]
    before_eq = np.where(dr[:, None] == 0, t[:, None] <= t[None, :], t[:, None] >= t[None, :])
    c['cum'] = (same & before_eq).astype(np.float32)
    c['blk'] = same.astype(np.float32)
    c['dirf'] = np.repeat((dr == 0).astype(np.float32)[:, None], 128, 1)
    c['dirb'] = np.repeat((dr == 1).astype(np.float32)[:, None], 128, 1)
    s_before_c = np.where(dr[:, None] == 0, t[None, :] < t[:, None], t[None, :] > t[:, None])
    negB = np.where(same & s_before_c, 0.0, NEG).astype(np.float32)
    c['negB'] = negB
    c['negA'] = negB.T.copy()
    s_beq_c = np.where(dr[:, None] == 0, t[None, :] <= t[:, None], t[None, :] >= t[:, None])
    negBi = np.where(same & s_beq_c, 0.0, NEG).astype(np.float32)
    c['negAi'] = negBi.T.copy()
    c['m01Ai'] = (negBi.T == 0.0).astype(np.float32)
    R = np.zeros((128, 128), np.float32)
    for b0 in (0, 64):
        for i in range(32):
            R[b0 + 32 + i, b0 + i] = -1.0
            R[b0 + i, b0 + 32 + i] = 1.0
    c['rperm'] = R
    J = np.zeros((128, 128), np.float32)
    for rq in range(2):
        for cc in range(64):
            J[rq * 64 + 63 - cc, rq * 64 + cc] = 1.0
    c['jrev'] = J
    tok = np.arange(NT)
    cos = np.ones((128, NT), np.float32)
    sin = np.zeros((128, NT), np.float32)
    if lat:
        for d in range(128):
            i = d % 32
            freq = np.float32(10000.0) ** np.float32(-i / 32.0)
            pos = (tok // 64) if d < 64 else (tok % 64)
            ang = pos.astype(np.float32) * freq
            cos[d] = np.cos(ang)
            sin[d] = np.sin(ang)
    c['cos'] = cos
    c['sin'] = sin
    mw = np.full((8, 128, 5, 128), NEG, np.float32)
    q = np.arange(128)
    for j in range(8):
        if lat:
            for si, m in enumerate((j - 1, j, j + 1)):
                if 0 <= m < 8:
                    qpos = j * 128 + q[:, None]
                    kpos = m * 128 + q[None, :]
                    mw[j, :, si, :] = np.where(np.abs(qpos - kpos) <= 128, 0.0, NEG)
            mw[j, :, 3:5, :] = 0.0
        else:
            seq = j // 2
            for si, m in enumerate((j - 1, j, j + 1)):
                if 0 <= m < 8 and m // 2 == seq:
                    mw[j, :, si, :] = 0.0
    c['maskw'] = (mw / SCALE).reshape(8, 128, 640).astype(np.float32)
    mn = np.full((8, 128, 7, 128), NEG, np.float32)
    for j in range(8):
        blks = NBLK[j]
        for si, m in enumerate(blks):
            if lat:
                r = 2 * j + q // 64
                cq = q % 64
                rs = np.clip(r - 4, 0, 8)
                cst = np.clip(cq - 8, 0, 48)
                kr = 2 * m + q // 64
                kc = q % 64
                ok = ((kr[None, :] >= rs[:, None]) & (kr[None, :] < rs[:, None] + 8) &
                      (kc[None, :] >= cst[:, None]) & (kc[None, :] < cst[:, None] + 16))
                mn[j, :, si, :] = np.where(ok, 0.0, NEG)
            else:
                if m // 2 == j // 2:
                    mn[j, :, si, :] = 0.0
        if lat:
            mn[j, :, 5:7, :] = 0.0
    c['maskn'] = (mn / SCALE).reshape(8, 128, 896).astype(np.float32)
    c['flag'] = np.full((128, 1), 1.0 if lat else 0.0, np.float32)
    return c


NBLK = {0: [0, 1, 2, 3], 1: [0, 1, 2, 3], 2: [0, 1, 2, 3, 4], 3: [1, 2, 3, 4, 5], 4: [2, 3, 4, 5, 6],
        5: [3, 4, 5, 6, 7], 6: [4, 5, 6, 7], 7: [4, 5, 6, 7]}

CONST_SHAPES = {'ident': (128, 128), 'cum': (128, 128), 'blk': (128, 128), 'dirf': (128, 128), 'dirb': (128, 128),
                'negB': (128, 128), 'negA': (128, 128), 'negAi': (128, 128), 'm01Ai': (128, 128),
                'rperm': (128, 128), 'jrev': (128, 128), 'cos': (128, NT), 'sin': (128, NT), 'maskw': (8, 128, 640),
                'maskn': (8, 128, 896), 'flag': (128, 1)}

IN_SHAPES = {
    'x': (NT, D), 'cond': (128, 16), 'st_delta': (2, 2, 8, 128, 128), 'kvw': (2, 2, 256, 2, 128),
    'st_gla': (2, 2, 4, 128, 256), 'kvn': (2, 2, 256, 8, 128),
    'norm_w': (128, 4, 16), 'w_ada': (4, D, 3 * D), 'b_ada': (128, 4, 48),
    'w_in_even': (2, D, P_EVEN), 'conv_a': (128, 2, 24, 5), 'a_log': (128, 2, 8), 'dt_bias': (128, 2, 8),
    'onorm_a': (128, 2), 'sink_b': (128, 2, 8), 'w_out_even': (2, D, D),
    'w_in_odd': (2, D, P_ODD), 'w_glr': (2, 2, 16, 512), 'b_glr': (2, 2, 512), 'onorm_c': (128, 2, 2),
    'rpb': (2, 8, 15, 31), 'w_out_odd': (2, D, D), 'final_w': (128, 16),
}
OUT_SHAPES = {
    'y': (NT, D), 'nsd': (4, 2, 2, 8, 128, 128), 'nkw': (4, 2, 2, 256, 2, 128),
    'nsg': (4, 2, 2, 4, 128, 256), 'nkn': (4, 2, 2, 256, 8, 128),
}


class Prog:
    def __init__(self, layers=(0, 1, 2, 3), final_norm=True, mixers='ABCD'):
        self.layers = tuple(layers)
        self.final_norm = final_norm
        self.mixers = mixers
        self.nc = bass.Bass("TRN2", target_bir_lowering=False)
        nc = self.nc
        self.din = {}
        for k, shp in list(IN_SHAPES.items()) + list(CONST_SHAPES.items()):
            self.din[k] = nc.dram_tensor(k, list(shp), F32, kind="ExternalInput").ap()
        self.dout = {}
        for k, shp in OUT_SHAPES.items():
            self.dout[k] = nc.dram_tensor(k, list(shp), F32, kind="ExternalOutput").ap()
        self.rpbp = nc.dram_tensor("rpbp", [8, 15, 128], F32, kind="Internal").ap()
        self.wq = 0
        with ExitStack() as st:
            self.st = st
            self.fw = FW(nc, st)
            self.build()
            self.fw.barrier()
            self.fw.replay()

    def psum(self, ncols=512):
        skip = self.psum_skip
        if ncols <= 512:
            while True:
                i = self.pi
                self.pi = (self.pi + 1) % 8
                if i not in skip:
                    break
            return self.pst[i // 2][:, (i % 2) * 512:(i % 2) * 512 + ncols]
        while True:
            if self.pi % 2:
                self.pi = (self.pi + 1) % 8
            i = self.pi
            self.pi = (self.pi + 2) % 8
            if i not in skip and (i + 1) not in skip:
                break
        return self.pst[i // 2][:, 0:ncols]

    def bg_step(self, n=1):
        for _ in range(n):
            if self.bg is None:
                return
            try:
                next(self.bg)
            except StopIteration:
                self.bg = None

    def wload(self, src, ncols):
        slab = self.wslab[self.wq]
        self.wq = (self.wq + 1) % len(self.wslab)
        self.fw.dma('pool', slab[:, :, 0:ncols], src.rearrange("(kc p) c -> p kc c", p=128))
        return slab

    def wload_rows(self, src, k0, nk, ncols):
        slab = self.wslab[self.wq]
        self.wq = (self.wq + 1) % len(self.wslab)
        self.fw.dma('pool', slab[:, 0:nk, 0:ncols], src.rearrange("(kc p) c -> p kc c", p=128)[:, k0:k0 + nk, :])
        return slab

    def proj_fm(self, slab, c0, evac):
        fw = self.fw
        for tt in range(2):
            ps = self.psum(512)
            for kc in range(16):
                fw.mm(ps, slab[:, kc, c0:c0 + 128], self.hT[:, kc, tt * 512:(tt + 1) * 512], start=(kc == 0), stop=(kc == 15))
            evac(ps, tt)

    def proj_tm(self, slab, c0, ncols, tb, evac):
        fw = self.fw
        ps = self.psum(ncols)
        for kc in range(16):
            fw.mm(ps, self.hT[:, kc, tb * 128:(tb + 1) * 128], slab[:, kc, c0:c0 + ncols], start=(kc == 0), stop=(kc == 15))
        evac(ps)

    def outproj(self, w_out, k0):
        fw = self.fw
        for oc2 in range(8):
            slab = self.wload_rows(w_out[:, oc2 * 256:(oc2 + 1) * 256], k0, 8, 256)
            for j in range(2):
                oc = oc2 * 2 + j
                for tt in range(2):
                    ps = self.psum(512)
                    for kc in range(8):
                        fw.mm(ps, slab[:, kc, j * 128:(j + 1) * 128], self.mixT[:, kc, tt * 512:(tt + 1) * 512],
                              start=(kc == 0), stop=(kc == 7))
                    xs = self.xT[:, oc, tt * 512:(tt + 1) * 512]
                    fw.stt(xs, ps, self.gate[:, oc:oc + 1], xs, ALU.mult, ALU.add)

    def rms_bcast(self, srcs, n, inv_n, dst_rstd, sqpool):
        fw = self.fw
        ps = self.psum(n)
        for i, s in enumerate(srcs):
            sq = sqpool[i % len(sqpool)][:, 0:n]
            fw.act(sq, s, AF.Square)
            fw.mm(ps, self.ones_b[:], sq, start=(i == 0), stop=(i == len(srcs) - 1))
        fw.act(dst_rstd, ps, AF.Ln, bias=self.epscol[:], scale=inv_n)
        fw.act(dst_rstd, dst_rstd, AF.Exp, scale=-0.5)

    def build(self):
        fw = self.fw
        nc = self.nc
        di = self.din
        self.pi = 0
        self.psum_skip = set()
        self.pst = [fw.ps('ps%d' % i, [128, 1024], F32, split=512) for i in range(4)]
        self.xT = fw.sb('xT', [128, 16, NT], F32, split=512)
        self.hT = fw.sb('hT', [128, 16, NT], BF16, split=512)
        self.mixT = fw.sb('mixT', [128, 8, NT], BF16, split=512)
        self.wslab = [fw.sb('wslab%d' % i, [128, 16, 256], BF16) for i in range(2)]
        cf = {}
        for k in ('ident', 'cum', 'blk', 'dirf', 'dirb', 'negB', 'negA', 'negAi'):
            cf[k] = fw.sb('c_' + k, [128, 128], F32)
            fw.dma('sp', cf[k][:], di[k])
        self.cf = cf
        self.ident_b = fw.sb('ident_b', [128, 128], BF16)
        fw.dma('pool', self.ident_b[:], di['ident'])
        self.m01 = fw.sb('m01', [128, 128], BF16)
        fw.dma('pool', self.m01[:], di['m01Ai'])
        self.rperm = fw.sb('rperm', [128, 128], BF16)
        fw.dma('pool', self.rperm[:], di['rperm'])
        self.jrev = fw.sb('jrev', [128, 128], BF16)
        fw.dma('pool', self.jrev[:], di['jrev'])
        self.ones_b = fw.sb('ones_b', [128, 128], BF16)
        fw.memset(self.ones_b[:], 1.0)
        self.ones_f = fw.sb('ones_f', [128, 128], F32)
        fw.memset(self.ones_f[:], 1.0)
        self.epscol = fw.sb('epscol', [128, 1], F32)
        fw.memset(self.epscol[:], EPS)
        self.flag = fw.sb('flag', [128, 1], F32)
        fw.dma('sp', self.flag[:], di['flag'])
        sm = {}
        for k in ('norm_w', 'b_ada', 'conv_a', 'a_log', 'dt_bias', 'onorm_a', 'sink_b', 'onorm_c', 'final_w', 'cond'):
            shp = IN_SHAPES[k]
            sm[k] = fw.sb('p_' + k, list(shp), F32)
            fw.dma('sp', sm[k][:], di[k])
        self.sm = sm
        self.scond = fw.sb('scond', [128, 16], BF16)
        fw.act(self.scond[:], sm['cond'][:], AF.Silu)
        self.modall = fw.sb('modall', [128, 4, 48], F32)
        self.mod_ready = set()
        self.bg = None
        self.Aw = fw.sb('Aw', [128, 16], F32)
        self.negsk = fw.sb('negsk', [128, 2, 8], F32)
        fw.ts(self.negsk[:], sm['sink_b'][:], -1.0, ALU.mult)
        self.onescol = fw.sb('onescol', [128, 1], F32)
        fw.memset(self.onescol[:], 1.0)
        self.dirsel = fw.sb('dirsel', [128, 2], F32)
        fw.cp(self.dirsel[:, 0:1], cf['dirf'][:, 0:1])
        fw.cp(self.dirsel[:, 1:2], cf['dirb'][:, 0:1])
        self.negbig = fw.sb('negbig', [128, 1], F32)
        fw.memset(self.negbig[:], 30000.0)

        with ExitStack() as ph:
            stage = [fw.sb('stage%d' % i, [128, D], F32, stack=ph) for i in range(2)]
            for tb in range(8):
                sg = stage[tb % 2]
                fw.dma('sp', sg[:], di['x'][tb * 128:(tb + 1) * 128, :])
                for c4 in range(4):
                    ps = self.psum(512)
                    for j in range(4):
                        c = c4 * 4 + j
                        fw.tr(ps[:, j * 128:(j + 1) * 128], sg[:, c * 128:(c + 1) * 128], cf['ident'][:])
                    dst = self.xT[:, c4 * 4:(c4 + 1) * 4, tb * 128:(tb + 1) * 128]
                    src = ps.rearrange("p (a b) -> p a b", a=4)
                    if c4 % 2 == 0:
                        fw.cp(dst, src, 'dve')
                    else:
                        fw.cp(dst, src, 'act')
            fw.barrier()

        for li in self.layers:
            self.fw.mark('layer%d' % li)
            self.layer(li)
        self.fw.mark('final')

        with ExitStack() as ph:
            stage = [fw.sb('ostage%d' % i, [128, D], F32, stack=ph) for i in range(2)]
            sqp = [fw.sb('fsq%d' % i, [128, 512], BF16, stack=ph) for i in range(2)]
            rstd = fw.sb('frstd', [128, 512], F32, stack=ph)
            for tt in range(2):
                sl = slice(tt * 512, (tt + 1) * 512)
                if self.final_norm:
                    self.rms_bcast([self.xT[:, c, sl] for c in range(16)], 512, 1.0 / D, rstd[:], sqp)
                    for c in range(16):
                        fw.stt(self.xT[:, c, sl], self.xT[:, c, sl], sm['final_w'][:, c:c + 1], rstd[:], ALU.mult, ALU.mult)
                for t4 in range(4):
                    tb = tt * 4 + t4
                    sg = stage[tb % 2]
                    for c4 in range(4):
                        ps = self.psum(512)
                        for j in range(4):
                            c = c4 * 4 + j
                            fw.tr(ps[:, j * 128:(j + 1) * 128], self.xT[:, c, tb * 128:(tb + 1) * 128], cf['ident'][:])
                        if c4 % 2 == 0:
                            fw.cp(sg[:, c4 * 512:(c4 + 1) * 512], ps, 'dve')
                        else:
                            fw.cp(sg[:, c4 * 512:(c4 + 1) * 512], ps, 'act')
                    fw.dma('sp', self.dout['y'][tb * 128:(tb + 1) * 128, :], sg[:])
            fw.barrier()

    def mod_gen(self, layers, ring, psm):
        fw = self.fw
        di = self.din
        for li in layers:
            slabs = {}

            def issue(sidx):
                slab = ring[sidx % len(ring)]
                fw.dma('pool', slab[:, :, 0:256], di['w_ada'][li][:, sidx * 256:(sidx + 1) * 256].rearrange("(kc p) c -> p kc c", p=128))
                slabs[sidx] = slab
            issue(0)
            for sidx in range(24):
                if sidx + 1 < 24:
                    issue(sidx + 1)
                yield
                slab = slabs.pop(sidx)
                for j in range(2):
                    col = sidx * 2 + j
                    pc = psm[:, (col % 2):(col % 2) + 1] if psm.shape[1] < 48 else psm[:, col:col + 1]
                    for kc in range(16):
                        fw.mm(pc, slab[:, kc, j * 128:(j + 1) * 128], self.scond[:, kc:kc + 1], start=(kc == 0), stop=(kc == 15))
                    fw.act(self.modall[:, li, col:col + 1], pc, AF.Identity, bias=self.sm['b_ada'][:, li, col:col + 1])
                yield
            self.mod_ready.add(li)

    def layer(self, li):
        fw = self.fw
        di = self.din
        sm = self.sm
        self.mod = self.modall[:, li, :]
        self.gate = self.mod[:, 32:48]
        if li not in self.mod_ready:
            for _ in self.mod_gen([li], self.wslab, self.psum(48)):
                pass
        fw.stt(self.Aw[:], self.mod[:, 16:32], 1.0, sm['norm_w'][:, li, :], ALU.add, ALU.mult)
        fw.mark('L%d.norm' % li)
        with ExitStack() as ph:
            sqp = [fw.sb('nsq%d_%d' % (li, i), [128, 512], BF16, stack=ph) for i in range(3)]
            rstd = fw.sb('nrstd%d' % li, [128, 512], F32, stack=ph)
            tmp = [fw.sb('ntmp%d_%d' % (li, i), [128, 512], F32, stack=ph) for i in range(3)]
            for tt in range(2):
                sl = slice(tt * 512, (tt + 1) * 512)
                self.rms_bcast([self.xT[:, c, sl] for c in range(16)], 512, 1.0 / D, rstd[:], sqp)
                for c in range(16):
                    t = tmp[c % 3]
                    fw.tt(t[:], self.xT[:, c, sl], rstd[:], ALU.mult)
                    fw.act(self.hT[:, c, sl], t[:], AF.Identity, bias=self.mod[:, c:c + 1], scale=self.Aw[:, c:c + 1])
            fw.barrier()
        if li % 2 == 0:
            e = li // 2
            w_in = di['w_in_even'][e]
            w_out = di['w_out_even'][e]
            fw.mark('L%d.mixA' % li)
            if 'A' in self.mixers:
                self.mixer_delta(e, w_in)
                fw.mark('L%d.outA' % li)
                self.outproj(w_out, 0)
            fw.mark('L%d.mixB' % li)
            if 'B' in self.mixers:
                self.mixer_win(e, w_in)
                fw.mark('L%d.outB' % li)
                self.outproj(w_out, 8)
        else:
            o = li // 2
            w_in = di['w_in_odd'][o]
            w_out = di['w_out_odd'][o]
            fw.mark('L%d.mixC' % li)
            if 'C' in self.mixers:
                self.mixer_gla(o, w_in)
                fw.mark('L%d.outC' % li)
                self.outproj(w_out, 0)
            fw.mark('L%d.mixD' % li)
            if 'D' in self.mixers:
                self.mixer_nbr(o, w_in)
                fw.mark('L%d.outD' % li)
                self.outproj(w_out, 8)
        fw.barrier()

    def pipeline(self, gens, depth=3):
        it = iter(gens)
        active = []
        done = False
        while True:
            if not done and len(active) < depth:
                try:
                    active.append(next(it))
                except StopIteration:
                    done = True
            if self.bg is not None:
                try:
                    next(self.bg)
                except StopIteration:
                    self.bg = None
            for g in reversed(list(active)):
                try:
                    next(g)
                except StopIteration:
                    active.remove(g)
            if done and not active:
                break

    def attention(self, pfx, qT, slots, sink_neg, zs, mix_dst, work, pre=None, uidx=0, runs=None):
        fw = self.fw
        if pre is not None:
            pre()
        ns = len(slots)
        W = ns * 128
        S = self.pst[uidx % 3][:, 0:W]
        if runs is None:
            for i, (kT, v, mask, bias) in enumerate(slots):
                cs = S[:, i * 128:(i + 1) * 128]
                fw.mm(cs, qT, kT, start=True, stop=False)
                fw.mm(cs, self.ident_b[:], mask, start=False, stop=(bias is None))
                if bias is not None:
                    fw.mm(cs, self.ident_b[:], bias, start=False, stop=True)
        else:
            for (c0, nc_, kTr_, mk_, bs_) in runs:
                cs = S[:, c0:c0 + nc_]
                fw.mm(cs, qT, kTr_, start=True, stop=False)
                fw.mm(cs, self.ident_b[:], mk_, start=False, stop=(bs_ is None))
                if bs_ is not None:
                    fw.mm(cs, self.ident_b[:], bs_, start=False, stop=True)
        yield
        mx, nm, rs, es, E, ET = work
        fw.rmax(mx[:], S)
        fw.ts(nm[:], mx[:], -SCALE, ALU.mult, sink_neg, ALU.min)
        fw.act(E[:, 0:W], S, AF.Exp, bias=nm[:], scale=SCALE, accum_out=rs[:])
        fw.act(es[:], sink_neg, AF.Exp, bias=nm[:], scale=-1.0)
        fw.tt(rs[:], rs[:], es[:], ALU.add)
        fw.recip(rs[:], rs[:])
        fw.ts(E[:, 0:W], E[:, 0:W], rs[:], ALU.mult)
        yield
        PT = self.pst[3][:, (uidx % 2) * 512:(uidx % 2) * 512 + 512]
        PTb = PT.bitcast(BF16)
        for i in range(ns):
            fw.tr(PTb[:, i * 128:(i + 1) * 128], E[:, i * 128:(i + 1) * 128], self.ident_b[:])
        fw.cp(ET[:, 0:W], PTb[:, 0:W], 'act')
        yield
        O = self.pst[uidx % 3][:, 896:1024]
        for i, (kT, v, mask, bias) in enumerate(slots):
            fw.mm(O, v, ET[:, i * 128:(i + 1) * 128], start=(i == 0), stop=(i == ns - 1))
        fw.tt(mix_dst, O, zs, ALU.mult)

    def mixer_win(self, e, w_in):
        fw = self.fw
        di = self.din
        with ExitStack() as ph:
            cos = fw.sb('cos%d' % e, [128, NT], F32, stack=ph)
            sin = fw.sb('sin%d' % e, [128, NT], F32, stack=ph)
            fw.dma('sp', cos[:], di['cos'])
            fw.dma('sp', sin[:], di['sin'])
            kTr = fw.sb('kTr%d' % e, [128, 2, NT], BF16, stack=ph)
            vtok = fw.sb('vtok%d' % e, [128, 8, 256], BF16, stack=ph)
            kcT = fw.sb('kcT%d' % e, [128, 2, 256], BF16, stack=ph)
            vc = fw.sb('vc%d' % e, [128, 2, 256], BF16, stack=ph)
            kctok = fw.sb('kctok%d' % e, [128, 2, 256], BF16, stack=ph)
            kvst = [fw.sb('kvst%d_%d' % (e, i), [128, 512], F32, stack=ph) for i in range(2)]
            q0 = [fw.sb('q0_%d_%d' % (e, i), [128, 512], BF16, stack=ph) for i in range(2)]
            t1 = [fw.sb('rt1_%d_%d' % (e, i), [128, 512], F32, stack=ph) for i in range(1)] * 2
            t2 = [fw.sb('rt2_%d_%d' % (e, i), [128, 512], F32, stack=ph) for i in range(1)] * 2
            qTr = fw.sb('qTr%d' % e, [128, 4, NT], BF16, stack=ph, split=128)
            zs = fw.sb('zsB%d' % e, [128, 4, NT], BF16, stack=ph, split=128)
            maskt = [fw.sb('maskw%d_%d' % (e, i), [128, 640], BF16, stack=ph) for i in range(2)]
            mx = [fw.sb('amx%d_%d' % (e, i), [128, 1], F32, stack=ph) for i in range(2)]
            nm = [fw.sb('anm%d_%d' % (e, i), [128, 1], F32, stack=ph) for i in range(2)]
            rs = [fw.sb('ars%d_%d' % (e, i), [128, 1], F32, stack=ph) for i in range(2)]
            es = [fw.sb('aes%d_%d' % (e, i), [128, 1], F32, stack=ph) for i in range(2)]
            E = [fw.sb('aE%d_%d' % (e, i), [128, 640], BF16, stack=ph) for i in range(2)]
            ET = [fw.sb('aET%d_%d' % (e, i), [128, 640], BF16, stack=ph) for i in range(2)]
            cB = PA_EVEN
            rc = [0]
            nxt_layers = [l for l in ((1, 2) if e == 0 else (3,)) if l in self.layers and l not in self.mod_ready]
            if nxt_layers:
                mring = [fw.sb('mring%d_%d' % (e, i), [128, 16, 256], BF16, stack=ph) for i in range(2)]
                self.bg = self.mod_gen(nxt_layers, mring, self.pst[0][:, 768:770])
                self.psum_skip = {1}

            def rope_evac(dst_fn):
                def ev(ps, tt):
                    i = rc[0] % 2
                    rc[0] += 1
                    sl = slice(tt * 512, (tt + 1) * 512)
                    fw.cp(q0[i][:], ps, 'act')
                    rot = self.psum(512)
                    fw.mm(rot, self.rperm[:], q0[i][:])
                    fw.tt(t1[i][:], q0[i][:], cos[:, sl], ALU.mult)
                    fw.tt(t2[i][:], rot, sin[:, sl], ALU.mult)
                    fw.tt(dst_fn(sl), t1[i][:], t2[i][:], ALU.add)
                return ev
            slab = self.wload(w_in[:, cB + 1024:cB + 1280], 256)
            for g in range(2):
                self.proj_fm(slab, g * 128, rope_evac(lambda sl, g=g: kTr[:, g, sl]))
                self.bg_step()
            slabv = self.wload(w_in[:, cB + 1280:cB + 1536], 256)
            for tb in range(8):
                sg = kvst[tb % 2]
                self.proj_tm(slab, 0, 256, tb, lambda ps, sg=sg: fw.cp(sg[:, 0:256], ps, 'act'))
                self.proj_tm(slabv, 0, 256, tb, lambda ps, sg=sg: fw.cp(sg[:, 256:512], ps, 'dve'))
                self.bg_step()
                fw.cp(vtok[:, tb, :], sg[:, 256:512], 'act')
                seq, half = tb // 2, tb % 2
                for kv in range(2):
                    fw.dma('sp', self.dout['nkw'][seq, e, kv, half * 128:(half + 1) * 128].rearrange("t g d -> t (g d)"),
                           sg[:, kv * 256:(kv + 1) * 256])
            for blk in range(2):
                fw.dma('pool', kctok[:, blk, :], di['kvw'][e, 0, blk * 128:(blk + 1) * 128].rearrange("t g d -> t (g d)"))
                fw.dma('pool', vc[:, blk, :], di['kvw'][e, 1, blk * 128:(blk + 1) * 128].rearrange("t g d -> t (g d)"))
            pt = self.psum(512).bitcast(BF16)
            for blk in range(2):
                for g in range(2):
                    fw.tr(pt[:, (g * 2 + blk) * 128:(g * 2 + blk + 1) * 128], kctok[:, blk, g * 128:(g + 1) * 128], self.ident_b[:])
            fw.cp(kcT[:].rearrange("p g k -> p (g k)"), pt[:, 0:512], 'dve')
            for hg in range(2):
                for hh2 in range(2):
                    slabq = self.wload(w_in[:, cB + hg * 512 + hh2 * 256:cB + hg * 512 + (hh2 + 1) * 256], 256)
                    slabz = self.wload(w_in[:, cB + 1536 + hg * 512 + hh2 * 256:cB + 1536 + hg * 512 + (hh2 + 1) * 256], 256)
                    for j2 in range(2):
                        hh = hh2 * 2 + j2
                        self.proj_fm(slabq, j2 * 128, rope_evac(lambda sl, hh=hh: qTr[:, hh, sl]))
                        self.proj_fm(slabz, j2 * 128, lambda ps, tt, hh=hh: fw.act(zs[:, hh, tt * 512:(tt + 1) * 512], ps, AF.Silu))
                        self.bg_step(2)
                it = 0
                units = []
                for j in range(8):
                    mk = maskt[j % 2]
                    for hh in range(4):
                        h = hg * 4 + hh
                        slots = []
                        for si, m in enumerate((j - 1, j, j + 1)):
                            if 0 <= m < 8:
                                slots.append((kTr[:, hg, m * 128:(m + 1) * 128], vtok[:, m, hg * 128:(hg + 1) * 128],
                                              mk[:, si * 128:(si + 1) * 128], None))
                        for cb in range(2):
                            slots.append((kcT[:, hg, cb * 128:(cb + 1) * 128], vc[:, cb, hg * 128:(hg + 1) * 128],
                                          mk[:, (3 + cb) * 128:(4 + cb) * 128], None))
                        w = it % 2
                        it += 1
                        pre = (lambda mk=mk, j=j: fw.dma('pool', mk[:], di['maskw'][j])) if hh == 0 else None
                        units.append(self.attention('B', qTr[:, hh, j * 128:(j + 1) * 128], slots, self.negsk[:, e, h:h + 1],
                                                    zs[:, hh, j * 128:(j + 1) * 128], self.mixT[:, h, j * 128:(j + 1) * 128],
                                                    (mx[w], nm[w], rs[w], es[w], E[w], ET[w]), pre=pre, uidx=it))
                self.pipeline(units, 4)
            if self.bg is not None:
                for _ in self.bg:
                    pass
                self.bg = None
            self.psum_skip = set()
            fw.barrier()


    def mixer_delta(self, e, w_in):
        fw = self.fw
        di = self.din
        cf = self.cf
        sm = self.sm
        with ExitStack() as ph:
            xb = fw.sb('dxb%d' % e, [128, 8, 16], F32, stack=ph)
            xg = fw.sb('dxg%d' % e, [128, 8, 16], F32, stack=ph)
            beta = fw.sb('dbeta%d' % e, [128, 8, 16], F32, stack=ph)
            lnb = fw.sb('dlnb%d' % e, [128, 8, 16], F32, stack=ph)
            g = fw.sb('dg%d' % e, [128, 8, 16], F32, stack=ph)
            nal = fw.sb('dnal%d' % e, [128, 8], F32, stack=ph)
            slab = self.wload(w_in[:, 4096:4128], 32)
            ps = self.psum(512)
            for n in range(16):
                for d in range(2):
                    for kc in range(16):
                        fw.mm(ps[d * 64:(d + 1) * 64, n * 32:(n + 1) * 32], self.hT[:, kc, n * 64:(n + 1) * 64], slab[:, kc, 0:32],
                              start=(kc == 0), stop=(kc == 15))
            p3 = ps.rearrange("p (n c) -> p n c", n=16)
            for d in range(2):
                R = slice(d * 64, (d + 1) * 64)
                fw.cp(xb[R].rearrange("p h n -> p n h"), p3[R, :, d * 8:(d + 1) * 8], 'dve')
                fw.cp(xg[R].rearrange("p h n -> p n h"), p3[R, :, 16 + d * 8:16 + (d + 1) * 8], 'dve')
            fw.act(beta[:], xb[:], AF.Exp, scale=-1.0)
            fw.ts(beta[:], beta[:], 1.0, ALU.add)
            fw.act(lnb[:], beta[:], AF.Ln)
            fw.ts(lnb[:], lnb[:], -1.0, ALU.mult)
            fw.recip(beta[:], beta[:])
            fw.tt(xg[:], xg[:], sm['dt_bias'][:, e, :].unsqueeze(2).to_broadcast([128, 8, 16]), ALU.add)
            fw.act(xg[:], xg[:], AF.Exp)
            fw.act(xg[:], xg[:], AF.Ln, bias=self.onescol[:])
            fw.act(nal[:], sm['a_log'][:, e, :], AF.Exp)
            fw.ts(nal[:], nal[:], -1.0, ALU.mult)
            fw.tt(g[:], xg[:], nal[:].unsqueeze(2).to_broadcast([128, 8, 16]), ALU.mult)
            for h in range(8):
                self.delta_head(e, h, w_in, beta, lnb, g)
            fw.barrier()

    def delta_head(self, e, h, w_in, beta, lnb, g):
        fw = self.fw
        di = self.din
        cf = self.cf
        sm = self.sm
        tg = '%d_%d' % (e, h)
        if h <= 1: self.fw.mark('D%d.h%d.start' % (e, h))
        bc_last = lambda ap: ap.unsqueeze(2).to_broadcast([128, 4, 128])
        bc_n = lambda ap: ap.unsqueeze(1).to_broadcast([128, 4, 128])
        v3 = lambda ap: ap.rearrange("p (n c) -> p n c", n=4)
        with ExitStack() as hd:
            u = fw.sb('du' + tg, [128, 2048], BF16, stack=hd, split=128)
            wT = fw.sb('dw' + tg, [128, 2048], BF16, stack=hd, split=128)
            attnT = fw.sb('dat' + tg, [128, 2048], BF16, stack=hd, split=128)
            qgT = fw.sb('dqg' + tg, [128, 2048], BF16, stack=hd, split=128)
            kdec = fw.sb('dkd' + tg, [128, 2048], BF16, stack=hd, split=128)
            egl = fw.sb('degl' + tg, [128, 2, 16], F32, stack=hd)
            oacc = fw.sb('doacc' + tg, [128, NT], F32, stack=hd, split=64)
            S2 = [[fw.sb('dS%s_%d_%d' % (tg, d, i), [128, 128], F32, stack=hd) for i in range(2)] for d in range(2)]
            Sb2 = [[fw.sb('dSb%s_%d_%d' % (tg, d, i), [128, 128], BF16, stack=hd) for i in range(2)] for d in range(2)]
            vnw = [[fw.sb('dvn%s_%d_%d' % (tg, d, i), [128, 128], BF16, stack=hd) for i in range(2)] for d in range(2)]
            fw.memset(oacc[:], 0.0, 'pool')
            for d in range(2):
                fw.dma('sp', S2[d][0][:], di['st_delta'][e, d, h])
                fw.ts(S2[d][0][:], S2[d][0][:], self.flag[:], ALU.mult)
                fw.cp(Sb2[d][0][:], S2[d][0][:], 'act')
            with ExitStack() as it:
                qkv = [fw.sb('dqkv%s_%d' % (tg, i), [128, NT], BF16, stack=it) for i in range(3)]
                gcc = fw.sb('dgcc' + tg, [128, 16], F32, stack=it)
                gbc = fw.sb('dgbc' + tg, [128, 16], F32, stack=it)
                bge = fw.sb('dbge' + tg, [128, 16], F32, stack=it)
                edk = fw.sb('dedk' + tg, [128, 16], F32, stack=it)
                with ExitStack() as cv:
                    cvs = []
                    for ci in range(2):
                        xpad = fw.sb('dxp%s_%d' % (tg, ci), [128, 4, 260], BF16, stack=cv)
                        acc = fw.sb('dacc%s_%d' % (tg, ci), [128, NT], F32, stack=cv)
                        rn = fw.sb('drn%s_%d' % (tg, ci), [128, 512], F32, stack=cv)
                        sqp = [fw.sb('dsq%s_%d' % (tg, ci), [128, 512], BF16, stack=cv)]
                        dk = fw.sb('ddk%s_%d' % (tg, ci), [128, 5, 128], BF16, stack=cv)
                        fw.memset(xpad[:], 0.0, 'pool' if ci else 'dve')
                        cvs.append((xpad, acc, rn, sqp, dk))

                    def front_q(qi, cvset):
                        xpad, acc, rn, sqp, dk = cvset
                        col = qi * 1024 + h * 128
                        slab = self.wload(w_in[:, col:col + 128], 128)
                        cw = sm['conv_a'][:, e, qi * 8 + h, :]
                        for k in range(5):
                            fw.act(dk[:, k, :], self.ident_b[:], AF.Copy, scale=cw[:, k:k + 1])
                        self.proj_fm(slab, 0, lambda ps, tt: fw.cp(xpad[:, tt * 2:tt * 2 + 2, 2:258], ps.rearrange("p (s t) -> p s t", s=2), 'act'))
                        yield
                        fw.ts(xpad[:, 1:4, 0:2], xpad[:, 0:3, 256:258], self.flag[:], ALU.mult)
                        fw.ts(xpad[:, 0:3, 258:260], xpad[:, 1:4, 2:4], self.flag[:], ALU.mult)
                        pcv = self.psum(1024)
                        for sg in range(4):
                            for k in range(5):
                                fw.mm(pcv[:, sg * 256:(sg + 1) * 256], dk[:, k, :], xpad[:, sg, k:k + 256], start=(k == 0), stop=(k == 4))
                        if qi < 2:
                            fw.act(acc[:], pcv, AF.Silu)
                        else:
                            fw.act(qkv[2][:], pcv, AF.Silu)
                        yield
                        if qi < 2:
                            for tt in range(2):
                                sl = slice(tt * 512, (tt + 1) * 512)
                                self.rms_bcast([acc[:, sl]], 512, 1.0, rn[:], sqp)
                                fw.stt(qkv[qi][:, sl], acc[:, sl], SCALE if qi == 0 else 1.0, rn[:], ALU.mult, ALU.mult)
                    self.pipeline([front_q(0, cvs[0]), front_q(1, cvs[1]), front_q(2, cvs[0])], 2)
                    fw.barrier()
                if h == 0: fw.mark('D%d.h0.batches' % e)
                TP = []
                for pp in range(2):
                    P = [fw.sb('dT%s_%d_%d' % (tg, pp, i), [128, 512], F32, stack=it) for i in range(6)]
                    P += [fw.sb('dB%s_%d_%d' % (tg, pp, i), [128, 512], BF16, stack=it) for i in range(3)]
                    TP.append(P)
                qn, kn, vn = qkv
                gh = g[:, h, :]
                pc = self.psum(64)
                fw.mm(pc[:, 0:16], cf['cum'][:], gh)
                fw.mm(pc[:, 16:32], cf['blk'][:], gh)
                fw.mm(pc[:, 32:48], cf['dirf'][:], gh)
                fw.mm(pc[:, 48:64], cf['dirb'][:], gh)
                fw.cp(gcc[:], pc[:, 0:16], 'dve')
                fw.act(egl[:].rearrange("p a b -> p (a b)"), pc[:, 32:64], AF.Exp)
                fw.tt(gbc[:], gcc[:], lnb[:, h, :], ALU.add)
                fw.act(bge[:], gcc[:], AF.Exp)
                fw.tt(bge[:], bge[:], beta[:, h, :], ALU.mult)
                fw.tt(edk[:], pc[:, 16:32], gcc[:], ALU.subtract)
                fw.act(edk[:], edk[:], AF.Exp)
                def dbatch(nb, P):
                    t0, t1, tE, a0, b0, b1, kbg, vb, Xf = P
                    pp = nb % 2
                    bk = [self.pst[2 * pp][:, 0:512], self.pst[2 * pp][:, 512:1024],
                          self.pst[2 * pp + 1][:, 0:512], self.pst[2 * pp + 1][:, 512:1024]]
                    bs = slice(nb * 4, (nb + 1) * 4)
                    pk = slice(nb * 512, (nb + 1) * 512)
                    tok = slice(nb * 256, (nb + 1) * 256)
                    fw.tt(v3(t0[:]), bc_n(cf['cum'][:]), bc_last(gh[:, bs]), ALU.mult)
                    pg = bk[0]
                    fw.mm(pg, self.ones_f[:], t0[:])
                    fw.tt(v3(t1[:]), bc_n(cf['ident'][:]), bc_last(lnb[:, h, bs]), ALU.mult)
                    fw.tt(t1[:], t1[:], t0[:], ALU.add)
                    pgb = bk[1]
                    fw.mm(pgb, self.ones_f[:], t1[:])
                    pkk = bk[2]
                    pkq = bk[3]
                    for j in range(4):
                        n = nb * 4 + j
                        cs = slice(n * 64, (n + 1) * 64)
                        for d in range(2):
                            for d2 in range(2):
                                fw.mm(pkk[d * 64:(d + 1) * 64, j * 128 + d2 * 64:j * 128 + (d2 + 1) * 64], kn[:, cs], kn[:, cs])
                                fw.mm(pkq[d * 64:(d + 1) * 64, j * 128 + d2 * 64:j * 128 + (d2 + 1) * 64], kn[:, cs], qn[:, cs])
                    yield
                    fw.stt(v3(tE[:]), v3(pg), -1.0, bc_n(cf['negB'][:]), ALU.mult, ALU.add)
                    fw.tt(v3(b1[:]), v3(pgb), bc_n(cf['negA'][:]), ALU.add)
                    fw.tt(v3(t1[:]), v3(pg), bc_n(cf['negAi'][:]), ALU.add)
                    fw.tt(v3(tE[:]), v3(tE[:]), bc_last(gbc[:, bs]), ALU.add)
                    fw.act(tE[:], tE[:], AF.Exp)
                    fw.tt(v3(b1[:]), v3(b1[:]), bc_last(gcc[:, bs]), ALU.subtract)
                    fw.act(b1[:], b1[:], AF.Exp)
                    fw.tt(v3(t1[:]), v3(t1[:]), bc_last(gcc[:, bs]), ALU.subtract)
                    fw.act(t1[:], t1[:], AF.Exp)
                    yield
                    fw.stt(b0[:], pkk, -1.0, tE[:], ALU.mult, ALU.mult)
                    fw.stt(a0[:], pkk, -1.0, b1[:], ALU.mult, ALU.mult)
                    fw.tt(v3(t0[:]), v3(a0[:]), bc_n(cf['ident'][:]), ALU.add)
                    fw.tt(attnT[:, pk], pkq, t1[:], ALU.mult)
                    yield
                    fw.act(tE[:], pg, AF.Exp)
                    q4 = qn[:, tok].rearrange("p (n t) -> p n t", n=4).unsqueeze(2).to_broadcast([128, 4, 2, 64])
                    fw.tt(qgT[:, pk].rearrange("p (n d t) -> p n d t", n=4, d=2), q4,
                          tE[:].rearrange("p (n d t) -> p n d t", n=4, d=2), ALU.mult)
                    yield
                    cur = (a0, b0)
                    nxt = (tE, b1)
                    xc, xn = t0, t1

                    def mm4(pd, lt, rt):
                        for j in range(4):
                            c = slice(j * 128, (j + 1) * 128)
                            fw.mm(pd[:, c], lt[:, c], rt[:, c])
                    for lvl in range(5):
                        A, B = cur
                        if lvl < 4:
                            pa = bk[0]
                            mm4(pa, B, A)
                            pb = bk[1]
                            mm4(pb, A, B)
                            fw.cp(nxt[0][:], pa, 'act')
                            fw.cp(nxt[1][:], pb, 'act')
                        else:
                            pb = bk[1]
                            mm4(pb, A, B)
                            fw.cp(nxt[1][:], pb, 'dve')
                        yield
                        px = bk[2]
                        mm4(px, nxt[1], xc)
                        if lvl < 4:
                            fw.tt(xn[:], px, xc[:], ALU.add)
                        else:
                            fw.tt(Xf[:], px, xc[:], ALU.add)
                        cur, nxt = nxt, cur
                        xc, xn = xn, xc
                        yield
                    ptk = bk[3].bitcast(BF16)
                    ptv = bk[0].bitcast(BF16)
                    for j in range(4):
                        n = nb * 4 + j
                        for d in range(2):
                            fw.tr(ptk[d * 64:(d + 1) * 64, j * 128:(j + 1) * 128], kn[:, n * 64:(n + 1) * 64], self.ident_b[:])
                            fw.tr(ptv[d * 64:(d + 1) * 64, j * 128:(j + 1) * 128], vn[:, n * 64:(n + 1) * 64], self.ident_b[:])
                    fw.tt(v3(kbg[:]), v3(ptk[:, 0:512]), bc_last(bge[:, bs]), ALU.mult)
                    fw.tt(v3(kdec[:, pk]), v3(ptk[:, 0:512]), bc_last(edk[:, bs]), ALU.mult)
                    fw.tt(v3(vb[:]), v3(ptv[:, 0:512]), bc_last(beta[:, h, bs]), ALU.mult)
                    yield
                    pu = bk[1]
                    pw = bk[2]
                    for j in range(4):
                        c = slice(j * 128, (j + 1) * 128)
                        fw.mm(pu[:, c], Xf[:, c], vb[:, c])
                        fw.mm(pw[:, c], kbg[:, c], Xf[:, c])
                    fw.cp(u[:, pk], pu, 'act')
                    fw.cp(wT[:, pk], pw, 'dve')
                self.pipeline([dbatch(nb, TP[nb % 2]) for nb in range(4)], 2)
                fw.barrier()
            if h == 0: fw.mark('D%d.h0.scan' % e)
            cur = 0
            for i in range(16):
                nn = [i, 15 - i]
                Rr = [slice(0, 64), slice(64, 128)]
                Sc = [S2[d][cur] for d in range(2)]
                Sn = [S2[d][1 - cur] for d in range(2)]
                Sbc = [Sb2[d][cur] for d in range(2)]
                Sbn = [Sb2[d][1 - cur] for d in range(2)]
                if i > 0 and i % 4 == 0:
                    for d in range(2):
                        fw.ts(Sc[d][:], Sc[d][:], self.flag[:], ALU.mult)
                        fw.cp(Sbc[d][:], Sc[d][:], 'act')
                ccs = [slice(nn[d] * 128 + d * 64, nn[d] * 128 + d * 64 + 64) for d in range(2)]
                PSs = [self.pst[d][:, (i % 2) * 512:(i % 2) * 512 + 512] for d in range(2)]
                vws = [vnw[d][i % 2] for d in range(2)]
                for d in range(2):
                    fw.mm(PSs[d][Rr[d], 0:128], wT[:, ccs[d]], Sbc[d][:])
                for d in range(2):
                    n = nn[d]
                    fw.tt(vws[d][Rr[d], :], u[Rr[d], n * 128:(n + 1) * 128], PSs[d][Rr[d], 0:128], ALU.subtract)
                for d in range(2):
                    n = nn[d]
                    fw.mm(PSs[d][:, 128:256], kdec[Rr[d], n * 128:(n + 1) * 128], vws[d][Rr[d], :])
                for d in range(2):
                    po = PSs[d][:, 256:320]
                    fw.mm(po, Sbc[d][:], qgT[:, ccs[d]], start=True, stop=False)
                    fw.mm(po, vws[d][Rr[d], :], attnT[Rr[d], ccs[d]], start=False, stop=True)
                for d in range(2):
                    fw.stt(Sbn[d][:], Sc[d][:], egl[:, d, nn[d]:nn[d] + 1], PSs[d][:, 128:256], ALU.mult, ALU.add)
                for d in range(2):
                    fw.stt(Sn[d][:], Sc[d][:], egl[:, d, nn[d]:nn[d] + 1], PSs[d][:, 128:256], ALU.mult, ALU.add)
                for d in range(2):
                    n = nn[d]
                    oc = oacc[:, n * 64:(n + 1) * 64]
                    fw.tt(oc, oc, PSs[d][:, 256:320], ALU.add)
                    if i % 4 == 3:
                        fw.dma('sp', self.dout['nsd'][n // 4, e, d, h], Sn[d][:])
                cur = 1 - cur
            if h == 0: fw.mark('D%d.h0.post' % e)
            with ExitStack() as post:
                zs = fw.sb('dzs' + tg, [128, NT], BF16, stack=post)
                sqp = [fw.sb('dpsq%s_%d' % (tg, i), [128, 512], BF16, stack=post) for i in range(2)]
                rstd = fw.sb('dprs' + tg, [128, 512], F32, stack=post)
                tmp = fw.sb('dptmp' + tg, [128, 512], F32, stack=post)
                slab = self.wload(w_in[:, 3072 + h * 128:3072 + h * 128 + 128], 128)
                self.proj_fm(slab, 0, lambda ps, tt: fw.act(zs[:, tt * 512:(tt + 1) * 512], ps, AF.Silu))
                for tt in range(2):
                    sl = slice(tt * 512, (tt + 1) * 512)
                    self.rms_bcast([oacc[:, sl]], 512, 1.0 / 128.0, rstd[:], sqp)
                    fw.stt(tmp[:], oacc[:, sl], sm['onorm_a'][:, e:e + 1], rstd[:], ALU.mult, ALU.mult)
                    fw.tt(self.mixT[:, h, sl], tmp[:], zs[:, sl], ALU.mult)
                fw.barrier()

    def mixer_gla(self, o, w_in):
        fw = self.fw
        di = self.din
        cf = self.cf
        with ExitStack() as ph:
            lrT = [fw.sb('lrT%d_%d' % (o, d), [32, NT], BF16, stack=ph) for d in range(2)]
            waug = [fw.sb('waug%d_%d' % (o, d), [32, 512], BF16, stack=ph) for d in range(2)]
            for d in range(2):
                fw.memset(lrT[d][:], 1.0)
                fw.dma('pool', waug[d][0:16, :], di['w_glr'][o, d])
                fw.dma('pool', waug[d][16:17, :], di['b_glr'][o, d:d + 1, :])
            slab = self.wload(w_in[:, 3072:3104], 32)
            for d in range(2):
                for tt in range(2):
                    ps = self.psum(512)
                    for kc in range(16):
                        fw.mm(ps[0:16, :], slab[:, kc, d * 16:(d + 1) * 16], self.hT[:, kc, tt * 512:(tt + 1) * 512],
                              start=(kc == 0), stop=(kc == 15))
                    fw.cp(lrT[d][0:16, tt * 512:(tt + 1) * 512], ps[0:16, :], 'act')
            for h in range(4):
                self.gla_head(o, h, w_in, lrT, waug)
            fw.barrier()

    def gla_head(self, o, h, w_in, lrT, waug):
        fw = self.fw
        di = self.din
        cf = self.cf
        tg = '%d_%d' % (o, h)
        if h <= 1: self.fw.mark('G%d.h%d.start' % (o, h))
        with ExitStack() as hd:
            qg = fw.sb('gqg' + tg, [128, 2048], BF16, stack=hd, split=128)
            kg = fw.sb('gkg' + tg, [128, 2048], BF16, stack=hd, split=128)
            attnT = fw.sb('gat' + tg, [128, 2048], BF16, stack=hd, split=128)
            kdec = fw.sb('gkd' + tg, [128, 2048], BF16, stack=hd, split=128)
            v2tok = fw.sb('gv2' + tg, [128, 16, 256], BF16, stack=hd, split=256)
            ebl = fw.sb('gebl' + tg, [128, 16, 2], F32, stack=hd)
            oacc = fw.sb('goacc' + tg, [128, 2, NT], F32, stack=hd, split=64)
            S2 = [[fw.sb('gS%s_%d_%d' % (tg, d, i), [128, 256], F32, stack=hd) for i in range(2)] for d in range(2)]
            Sb2 = [[fw.sb('gSb%s_%d_%d' % (tg, d, i), [128, 256], BF16, stack=hd) for i in range(2)] for d in range(2)]
            fw.memset(oacc[:], 0.0, 'pool')
            for d in range(2):
                fw.dma('sp', S2[d][0][:], di['st_gla'][o, d, h])
                fw.ts(S2[d][0][:], S2[d][0][:], self.flag[:], ALU.mult)
                fw.cp(Sb2[d][0][:], S2[d][0][:], 'act')
            with ExitStack() as it:
                qT = fw.sb('gq' + tg, [128, NT], BF16, stack=it)
                kT = fw.sb('gk' + tg, [128, NT], BF16, stack=it)
                vT = fw.sb('gv' + tg, [128, 2, NT], BF16, stack=it)
                gk2 = fw.sb('ggk' + tg, [128, 16, 128], F32, stack=it, split=512)
                tA = fw.sb('gtA' + tg, [128, 512], F32, stack=it)
                tB = fw.sb('gtB' + tg, [128, 512], F32, stack=it)
                tC = fw.sb('gtC' + tg, [128, 512], F32, stack=it)
                for nb in range(4):
                    ps = self.psum(512)
                    for j in range(4):
                        n = nb * 4 + j
                        for d in range(2):
                            fw.mm(ps[d * 64:(d + 1) * 64, j * 128:(j + 1) * 128], lrT[d][0:17, n * 64:(n + 1) * 64],
                                  waug[d][0:17, h * 128:(h + 1) * 128])
                    fw.act(tA[:], ps, AF.Exp, scale=-1.0)
                    fw.act(tA[:], tA[:], AF.Ln, bias=self.onescol[:])
                    fw.ts(gk2[:, nb * 4:(nb + 1) * 4, :].rearrange("p a b -> p (a b)"), tA[:], -1.0 / 16.0, ALU.mult)
                pe = self.psum(32)
                for n in range(16):
                    fw.mm(pe[:, n * 2:n * 2 + 2], gk2[:, n, :], self.dirsel[:])
                fw.act(ebl[:].rearrange("p a b -> p (a b)"), pe, AF.Exp)
                slab = self.wload(w_in[:, h * 128:h * 128 + 128], 128)
                self.proj_fm(slab, 0, lambda ps, tt: fw.ts(qT[:, tt * 512:(tt + 1) * 512], ps, SCALE, ALU.mult))
                slab = self.wload(w_in[:, 512 + h * 128:512 + h * 128 + 128], 128)
                self.proj_fm(slab, 0, lambda ps, tt: fw.cp(kT[:, tt * 512:(tt + 1) * 512], ps, 'act'))
                slab = self.wload(w_in[:, 1024 + h * 256:1024 + h * 256 + 256], 256)
                for hf in range(2):
                    self.proj_fm(slab, hf * 128, lambda ps, tt, hf=hf: fw.cp(vT[:, hf, tt * 512:(tt + 1) * 512], ps, 'act'))
                for nb in range(4):
                    tok = slice(nb * 256, (nb + 1) * 256)
                    pk = slice(nb * 512, (nb + 1) * 512)
                    ps = self.psum(512)
                    for j in range(4):
                        n = nb * 4 + j
                        fw.mm(ps[:, j * 128:(j + 1) * 128], gk2[:, n, :], cf['cum'][:])
                    fw.act(tA[:], ps, AF.Exp)
                    fw.act(tB[:], ps, AF.Exp, scale=-1.0)
                    v4 = lambda ap: ap.rearrange("p (n d t) -> p n d t", n=4, d=2)
                    b4 = lambda ap: ap.rearrange("p (n t) -> p n t", n=4).unsqueeze(2).to_broadcast([128, 4, 2, 64])
                    fw.tt(v4(qg[:, pk]), b4(qT[:, tok]), v4(tA[:]), ALU.mult)
                    fw.tt(v4(kg[:, pk]), b4(kT[:, tok]), v4(tB[:]), ALU.mult)
                    pa = self.psum(512)
                    for j in range(4):
                        n = nb * 4 + j
                        cs = slice(n * 128, (n + 1) * 128)
                        fw.mm(pa[:, j * 128:(j + 1) * 128], kg[:, cs], qg[:, cs])
                    fw.tt(attnT[:, pk].rearrange("p (n c) -> p n c", n=4), pa.rearrange("p (n c) -> p n c", n=4),
                          self.m01[:].unsqueeze(1).to_broadcast([128, 4, 128]), ALU.mult)
                    pb = self.psum(512)
                    fw.mm(pb, cf['cum'][:], gk2[:, nb * 4:(nb + 1) * 4, :].rearrange("p a b -> p (a b)"))
                    pl = self.psum(512)
                    fw.mm(pl, cf['blk'][:], gk2[:, nb * 4:(nb + 1) * 4, :].rearrange("p a b -> p (a b)"))
                    fw.cp(tC[:], pb, 'act')
                    fw.tt(tC[:], pl, tC[:], ALU.subtract)
                    fw.act(tC[:], tC[:], AF.Exp)
                    pt = self.psum(512).bitcast(BF16)
                    for j in range(4):
                        n = nb * 4 + j
                        for d in range(2):
                            fw.tr(pt[d * 64:(d + 1) * 64, j * 128:(j + 1) * 128], kT[:, n * 64:(n + 1) * 64], self.ident_b[:])
                    fw.tt(kdec[:, pk], pt[:, 0:512], tC[:], ALU.mult)
                    pv = self.psum(512).bitcast(BF16)
                    for j in range(4):
                        n = nb * 4 + j
                        for hf in range(2):
                            for d in range(2):
                                fw.tr(pv[d * 64:(d + 1) * 64, j * 256 + hf * 128:j * 256 + (hf + 1) * 128],
                                      vT[:, hf, n * 64:(n + 1) * 64], self.ident_b[:])
                    fw.cp(v2tok[:, nb * 4:(nb + 1) * 4, :].rearrange("p a b -> p (a b)"), pv[:, 0:1024], 'dve')
                fw.barrier()
            if h == 0: fw.mark('G%d.h0.scan' % o)
            Rr = [slice(0, 64), slice(64, 128)]

            def issue_pss(i, d):
                n = i if d == 0 else 15 - i
                p = self.pst[d][:, (i % 2) * 256:(i % 2) * 256 + 256]
                fw.mm(p, kdec[Rr[d], n * 128:(n + 1) * 128], v2tok[Rr[d], n, :])
                return p
            pq = [issue_pss(0, 0), issue_pss(0, 1)]
            cur = 0
            for i in range(16):
                nn = [i, 15 - i]
                Sc = [S2[d][cur] for d in range(2)]
                Sn = [S2[d][1 - cur] for d in range(2)]
                Sbc = [Sb2[d][cur] for d in range(2)]
                Sbn = [Sb2[d][1 - cur] for d in range(2)]
                if i > 0 and i % 4 == 0:
                    for d in range(2):
                        fw.ts(Sc[d][:], Sc[d][:], self.flag[:], ALU.mult)
                        fw.cp(Sbc[d][:], Sc[d][:], 'act')
                for d in range(2):
                    fw.stt(Sn[d][:], Sc[d][:], ebl[:, nn[d], d:d + 1], pq[d], ALU.mult, ALU.add)
                for d in range(2):
                    fw.cp(Sbn[d][:], Sn[d][:], 'act')
                pos = []
                for d in range(2):
                    n = nn[d]
                    cc = slice(n * 128 + d * 64, n * 128 + d * 64 + 64)
                    for hf in range(2):
                        pc0 = 512 + ((2 * i + hf) % 4) * 64
                        po = self.pst[d][:, pc0:pc0 + 64]
                        fw.mm(po, Sbc[d][:, hf * 128:(hf + 1) * 128], qg[:, cc], start=True, stop=False)
                        fw.mm(po, v2tok[Rr[d], n, hf * 128:(hf + 1) * 128], attnT[Rr[d], cc], start=False, stop=True)
                        pos.append((po, oacc[:, hf, n * 64:(n + 1) * 64]))
                if i + 1 < 16:
                    pq = [issue_pss(i + 1, 0), issue_pss(i + 1, 1)]
                for po, oc in pos:
                    fw.tt(oc, oc, po, ALU.add)
                if i % 4 == 3:
                    for d in range(2):
                        fw.dma('sp', self.dout['nsg'][nn[d] // 4, o, d, h], Sn[d][:])
                cur = 1 - cur
            if h == 0: fw.mark('G%d.h0.post' % o)
            with ExitStack() as post:
                zs = fw.sb('gzs' + tg, [128, 2, NT], BF16, stack=post)
                sqp = [fw.sb('gsq%s_%d' % (tg, i), [128, 512], BF16, stack=post) for i in range(2)]
                rstd = fw.sb('grs' + tg, [128, 512], F32, stack=post)
                tmp = fw.sb('gtmp' + tg, [128, 512], F32, stack=post)
                slab = self.wload(w_in[:, 2048 + h * 256:2048 + h * 256 + 256], 256)
                for hf in range(2):
                    self.proj_fm(slab, hf * 128, lambda ps, tt, hf=hf: fw.act(zs[:, hf, tt * 512:(tt + 1) * 512], ps, AF.Silu))
                for tt in range(2):
                    sl = slice(tt * 512, (tt + 1) * 512)
                    self.rms_bcast([oacc[:, 0, sl], oacc[:, 1, sl]], 512, 1.0 / 256.0, rstd[:], sqp)
                    for hf in range(2):
                        fw.stt(tmp[:], oacc[:, hf, sl], self.sm['onorm_c'][:, o, hf:hf + 1], rstd[:], ALU.mult, ALU.mult)
                        fw.tt(self.mixT[:, h * 2 + hf, sl], tmp[:], zs[:, hf, sl], ALU.mult)
            fw.barrier()

    def mixer_nbr(self, o, w_in):
        fw = self.fw
        di = self.din
        with ExitStack() as ph:
            kT = fw.sb('nkT%d' % o, [128, 8, NT], BF16, stack=ph, split=NT)
            vtok = fw.sb('nvtok%d' % o, [128, 8, 1024], BF16, stack=ph, split=256)
            kctok = fw.sb('nkctok%d' % o, [128, 2, 1024], BF16, stack=ph)
            vc = fw.sb('nvc%d' % o, [128, 2, 1024], BF16, stack=ph)
            kcT = fw.sb('nkcT%d' % o, [128, 8, 256], BF16, stack=ph)
            kvst = [fw.sb('nkvst%d_%d' % (o, i), [128, 256], F32, stack=ph) for i in range(2)]
            qT = [fw.sb('nqT%d_%d' % (o, i), [128, NT], BF16, stack=ph) for i in range(1)] * 2
            zs = [fw.sb('nzs%d_%d' % (o, i), [128, NT], BF16, stack=ph) for i in range(1)] * 2
            biasT = [fw.sb('nbias%d_%d' % (o, i), [128, 7, 128], BF16, stack=ph) for i in range(2)]
            maskt = [fw.sb('nmask%d_%d' % (o, i), [128, 896], BF16, stack=ph) for i in range(3)]
            brevs = [fw.sb('nbrev%d_%d' % (o, i), [128, 7, 128], BF16, stack=ph) for i in range(2)]
            zero = fw.sb('nzero%d' % o, [120, 128], F32, stack=ph)
            mx = [fw.sb('nmx%d_%d' % (o, i), [128, 1], F32, stack=ph) for i in range(2)]
            nm = [fw.sb('nnm%d_%d' % (o, i), [128, 1], F32, stack=ph) for i in range(2)]
            rs = [fw.sb('nrs%d_%d' % (o, i), [128, 1], F32, stack=ph) for i in range(2)]
            es = [fw.sb('nes%d_%d' % (o, i), [128, 1], F32, stack=ph) for i in range(2)]
            E = [fw.sb('nE%d_%d' % (o, i), [128, 896], BF16, stack=ph) for i in range(2)]
            ET = [fw.sb('nET%d_%d' % (o, i), [128, 896], BF16, stack=ph) for i in range(2)]
            cD = PC_ODD
            fw.memset(zero[:], 0.0)
            fw.dma('sp', self.rpbp.rearrange("h r c -> (h r) c"), zero[:])
            fw.dma('sp', self.rpbp[:, :, 48:79], di['rpb'][o])
            for blk in range(2):
                fw.dma('pool', kctok[:, blk, :], di['kvn'][o, 0, blk * 128:(blk + 1) * 128].rearrange("t g d -> t (g d)"))
                fw.dma('pool', vc[:, blk, :], di['kvn'][o, 1, blk * 128:(blk + 1) * 128].rearrange("t g d -> t (g d)"))
            for g4 in range(2):
                pt = self.psum(512).bitcast(BF16)
                for gg in range(4):
                    g = g4 * 4 + gg
                    for blk in range(2):
                        fw.tr(pt[:, (gg * 2 + blk) * 128:(gg * 2 + blk + 1) * 128], kctok[:, blk, g * 128:(g + 1) * 128], self.ident_b[:])
                fw.cp(kcT[:, g4 * 4:(g4 + 1) * 4, :].rearrange("p g k -> p (g k)"), pt[:, 0:1024], 'dve')
            ci = [0]
            for s2 in range(4):
                slab = self.wload(w_in[:, cD + 1024 + s2 * 256:cD + 1024 + (s2 + 1) * 256], 256)
                for j2 in range(2):
                    g = s2 * 2 + j2
                    self.proj_fm(slab, j2 * 128, lambda ps, tt, g=g: fw.cp(kT[:, g, tt * 512:(tt + 1) * 512], ps, 'act'))
                for tb in range(8):
                    sg = kvst[ci[0] % 2]
                    ci[0] += 1
                    self.proj_tm(slab, 0, 256, tb, lambda ps, sg=sg: fw.cp(sg[:], ps, 'dve'))
                    fw.dma('sp', self.dout['nkn'][tb // 2, o, 0, (tb % 2) * 128:(tb % 2 + 1) * 128, s2 * 2:s2 * 2 + 2, :].rearrange("t g d -> t (g d)"), sg[:])
            for s2 in range(4):
                slab = self.wload(w_in[:, cD + 2048 + s2 * 256:cD + 2048 + (s2 + 1) * 256], 256)
                for tb in range(8):
                    sg = kvst[ci[0] % 2]
                    ci[0] += 1
                    self.proj_tm(slab, 0, 256, tb, lambda ps, sg=sg: fw.cp(sg[:], ps, 'dve'))
                    fw.cp(vtok[:, tb, s2 * 256:(s2 + 1) * 256], sg[:], 'act')
                    fw.dma('sp', self.dout['nkn'][tb // 2, o, 1, (tb % 2) * 128:(tb % 2 + 1) * 128, s2 * 2:s2 * 2 + 2, :].rearrange("t g d -> t (g d)"), sg[:])
            def bias_load(hh):
                brev = brevs[hh % 2]
                for dd in range(7):
                    dl = dd - 3
                    for rq in range(2):
                        off = hh * 15 * 128 + (2 * dl - rq + 7) * 128
                        src = bass.AP(self.rpbp.tensor, off, [[1, 64], [128, 2], [1, 64]])
                        fw.dma('pool', brev[rq * 64:(rq + 1) * 64, dd, :].rearrange("p (a b) -> p a b", a=2), src,
                               extra_ins=[self.rpbp])

            def bias_finish(hh):
                brev = brevs[hh % 2]
                bt_ = biasT[hh % 2]
                for (c0, c1) in ((0, 512), (512, 896)):
                    pj = self.psum(c1 - c0)
                    fw.mm(pj, self.jrev[:], brev[:].rearrange("p a b -> p (a b)")[:, c0:c1])
                    fw.ts(bt_[:].rearrange("p a b -> p (a b)")[:, c0:c1], pj, self.flag[:], ALU.mult, 1.0 / SCALE, ALU.mult)
            it = 0
            for h in range(8):
                w2 = h % 2
                if h % 2 == 0:
                    slabq = self.wload(w_in[:, cD + h * 128:cD + h * 128 + 256], 256)
                    slabz = self.wload(w_in[:, cD + 3072 + h * 128:cD + 3072 + h * 128 + 256], 256)
                self.proj_fm(slabq, w2 * 128, lambda ps, tt, w2=w2: fw.cp(qT[w2][:, tt * 512:(tt + 1) * 512], ps, 'act'))
                self.proj_fm(slabz, w2 * 128, lambda ps, tt, w2=w2: fw.act(zs[w2][:, tt * 512:(tt + 1) * 512], ps, AF.Silu))
                bt = biasT[w2]
                if h == 0:
                    bias_load(0)
                bias_finish(h)
                if h + 1 < 8:
                    bias_load(h + 1)
                units = []
                for j in range(8):
                    mk = maskt[it % 3]
                    slots = []
                    for si, m in enumerate(NBLK[j]):
                        slots.append((kT[:, h, m * 128:(m + 1) * 128], vtok[:, m, h * 128:(h + 1) * 128],
                                      mk[:, si * 128:(si + 1) * 128], bt[:, m - j + 3, :]))
                    for cb in range(2):
                        slots.append((kcT[:, h, cb * 128:(cb + 1) * 128], vc[:, cb, h * 128:(h + 1) * 128],
                                      mk[:, (5 + cb) * 128:(6 + cb) * 128], None))
                    w = it % 2
                    it += 1
                    nbl = NBLK[j]
                    m0, nw = nbl[0], len(nbl)
                    runs = []
                    c = 0
                    while c < nw:
                        n_ = min(nw - c, 4 - (c % 4))
                        runs.append((c * 128, n_ * 128, kT[:, h, (m0 + c) * 128:(m0 + c + n_) * 128], mk[:, c * 128:(c + n_) * 128],
                                     bt[:, m0 + c - j + 3:m0 + c + n_ - j + 3, :].rearrange("p a b -> p (a b)")))
                        c += n_
                    for cb in range(2):
                        cc = nw + cb
                        if cb == 0 and (cc % 4) != 3:
                            runs.append((cc * 128, 256, kcT[:, h, 0:256], mk[:, 5 * 128:7 * 128], None))
                            break
                        runs.append((cc * 128, 128, kcT[:, h, cb * 128:(cb + 1) * 128], mk[:, (5 + cb) * 128:(6 + cb) * 128], None))
                    pre = (lambda mk=mk, j=j: fw.dma('pool', mk[:], di['maskn'][j]))
                    units.append(self.attention('D', qT[w2][:, j * 128:(j + 1) * 128], slots, self.negbig[:],
                                                zs[w2][:, j * 128:(j + 1) * 128], self.mixT[:, h, j * 128:(j + 1) * 128],
                                                (mx[w], nm[w], rs[w], es[w], E[w], ET[w]), pre=pre, uidx=it, runs=runs))
                self.pipeline(units, 4)
            fw.barrier()


_PROG = {}


def _get_prog(**kw):
    key = tuple(sorted(kw.items()))
    if key not in _PROG:
        _PROG[key] = Prog(**kw)
    return _PROG[key]


def _prep_inputs(inp):
    f = lambda a: np.ascontiguousarray(np.asarray(a, dtype=np.float32))
    x_prompt, x_sample = f(inp['x_prompt']), f(inp['x_sample'])
    shared = {}
    shared['norm_w'] = f(inp['norm_w']).reshape(4, 16, 128).transpose(2, 0, 1)
    shared['w_ada'] = f(inp['w_ada'])
    shared['b_ada'] = f(inp['b_ada']).reshape(4, 48, 128).transpose(2, 0, 1)
    shared['w_in_even'] = f(inp['w_in_even'])
    shared['conv_a'] = f(inp['conv_a']).reshape(2, 5, 24, 128).transpose(3, 0, 2, 1)
    shared['a_log'] = np.repeat(f(inp['a_log_a']).transpose(1, 0, 2), 64, axis=0)
    shared['dt_bias'] = np.repeat(f(inp['dt_bias_a']).transpose(1, 0, 2), 64, axis=0)
    shared['onorm_a'] = f(inp['onorm_a']).T
    shared['sink_b'] = np.broadcast_to(f(inp['sink_b'])[None], (128, 2, 8))
    shared['w_out_even'] = f(inp['w_out_even'])
    shared['w_in_odd'] = f(inp['w_in_odd'])
    shared['w_glr'] = f(inp['w_glr_c'])
    shared['b_glr'] = f(inp['b_glr_c'])
    shared['onorm_c'] = f(inp['onorm_c']).reshape(2, 2, 128).transpose(2, 0, 1)
    shared['rpb'] = f(inp['rpb_d'])
    shared['w_out_odd'] = f(inp['w_out_odd'])
    shared['final_w'] = f(inp['final_norm_w']).reshape(16, 128).T
    shared = {k: np.ascontiguousarray(v) for k, v in shared.items()}
    consts = [_consts(0), _consts(1)]
    maps = []
    for core in range(8):
        role = 0 if core < 4 else 1
        m = dict(shared)
        m.update(consts[role])
        if role == 0:
            m['x'] = x_prompt[4 * core:4 * core + 4].reshape(NT, D)
            cond = f(inp['c_ctx'])
            b = 0
        else:
            b = core - 4
            m['x'] = x_sample[b]
            cond = f(inp['c'])[b]
        m['cond'] = np.ascontiguousarray(cond.reshape(16, 128).T)
        m['st_delta'] = f(inp['state_delta'])[b]
        m['kvw'] = f(inp['cache_kv_win'])[b]
        m['st_gla'] = f(inp['state_gla'])[b]
        m['kvn'] = f(inp['cache_kv_nbr'])[b]
        maps.append({k: np.ascontiguousarray(v) for k, v in m.items()})
    return maps


def _run(inp, cores=None, xover=None, **kw):
    prog = _get_prog(**kw)
    maps = _prep_inputs(inp)
    if xover is not None:
        for c in range(8):
            maps[c]['x'] = np.ascontiguousarray(xover[c], dtype=np.float32)
    if cores is not None:
        maps = [maps[c] for c in cores]
    res = run_bass_kernel_spmd(prog.nc, maps, core_ids=list(range(len(maps))))
    return res.results


def kernel(**inp):
    r = _run(inp)
    y_prompt = np.concatenate([r[c]['y'].reshape(4, 256, D) for c in range(4)], 0)
    y_sample = np.stack([r[c]['y'] for c in range(4, 8)], 0)
    nsd = np.concatenate([r[c]['nsd'] for c in range(4)], 0)
    nkw = np.concatenate([r[c]['nkw'] for c in range(4)], 0)
    nsg = np.concatenate([r[c]['nsg'] for c in range(4)], 0)
    nkn = np.concatenate([r[c]['nkn'] for c in range(4)], 0)
    return (y_prompt.astype(np.float32), y_sample.astype(np.float32), nsd.astype(np.float32),
            nkw.astype(np.float32), nsg.astype(np.float32), nkn.astype(np.float32))
```
